# Optimizing a Trainium2 kernel written in Bass

```python
import jax, jax.numpy as jnp
from jax import lax
import numpy as np

D_MODEL = 1024
BATCH = 4
SEQ = 4096
DEPTH = 1
DEC_BATCH = 32
DEC_SEQ = 16
PAST_LEN = 1024

CHUNK = 64
Q_BLOCK = 2 * CHUNK
H_A = 16
D_HEAD_A = 64
D_A = H_A * D_HEAD_A
H_B = 16
D_HEAD_B = 64
D_B = H_B * D_HEAD_B
W_LORA = 64
A_LORA = 64
C_A = 4 * D_A
C_GATE = 2 * D_MODEL
C_SHIFT = 3 * D_B + W_LORA + A_LORA + D_B
N_IN = C_A + C_GATE + C_SHIFT
B_SPLITS = (D_B, 2 * D_B, 3 * D_B, 3 * D_B + W_LORA, 3 * D_B + W_LORA + A_LORA)
EPS = 1e-6
LNX_EPS = 64e-5

kernel_name = "hybrid_stickbreak_rwkv7_stream_step"


def _rmsnorm(x, g):
    xf = x.astype(jnp.float32)
    y = xf * lax.rsqrt(jnp.mean(xf * xf, axis=-1, keepdims=True) + EPS)
    return (y * g.astype(jnp.float32)).astype(x.dtype)


def _sb_attend(q, k, v, q_pos, k_pos):
    z = jnp.einsum('bhqd,bhkd->bhqk', q.astype(jnp.float32), k.astype(jnp.float32)) * (D_HEAD_A ** -0.5)
    valid = k_pos[None, :] < q_pos[:, None]
    log_keep = jnp.where(valid, jax.nn.log_sigmoid(-z), 0.0)
    later = lax.cumsum(log_keep, axis=3, reverse=True) - log_keep
    weight = jnp.where(valid, jnp.exp(jax.nn.log_sigmoid(z) + later), 0.0)
    o = jnp.einsum('bhqk,bhkd->bhqd', weight, v.astype(jnp.float32))
    return o.astype(q.dtype)


def _sb_prompt(q, k, v):
    T = q.shape[2]
    outs = []
    for i in range(T // Q_BLOCK):
        start, end = i * Q_BLOCK, (i + 1) * Q_BLOCK
        outs.append(_sb_attend(q[:, :, start:end], k[:, :, :end], v[:, :, :end],
                               jnp.arange(start, end), jnp.arange(end)))
    return jnp.concatenate(outs, axis=2)


def _wkv_scan(r, w, k, v, kk, a, S0):
    def step(S, inp):
        r_t, w_t, k_t, v_t, kk_t, a_t = inp
        sa = jnp.einsum('bhvk,bhk->bhv', S, -kk_t)
        S = (S * w_t[:, :, None, :] + sa[..., None] * (kk_t * a_t)[:, :, None, :]
             + v_t[..., None] * k_t[:, :, None, :])
        return S, jnp.einsum('bhvk,bhk->bhv', S, r_t)
    xs = (r.swapaxes(0, 1), w.swapaxes(0, 1), k.swapaxes(0, 1), v.swapaxes(0, 1),
          kk.swapaxes(0, 1), a.swapaxes(0, 1))
    S, o = lax.scan(step, S0, xs)
    return S, o.swapaxes(0, 1)


def _layer(x, shift_prev, k_past, v_past, wkv_prev, norm_g, w_in, mu_shift, w0, w2, a0, a2,
           k_k, k_a, r_k, lnx_g, lnx_b, w_o_a, w_o_b, w_out):
    B, T, _ = x.shape
    f32 = jnp.float32
    h = _rmsnorm(x, norm_g)
    proj = jnp.einsum('btd,dc->btc', h, w_in)
    p_a = proj[..., :C_A]
    p_gate = proj[..., C_A:C_A + C_GATE]
    p_b = proj[..., C_A + C_GATE:]

    q, k, v, z_a = jnp.split(p_a, 4, axis=-1)
    q = q.reshape(B, T, H_A, D_HEAD_A).transpose(0, 2, 1, 3)
    k = k.reshape(B, T, H_A, D_HEAD_A).transpose(0, 2, 1, 3)
    v = v.reshape(B, T, H_A, D_HEAD_A).transpose(0, 2, 1, 3)
    if k_past is None:
        o_a = _sb_prompt(q, k, v)
    else:
        past = k_past.shape[2]
        k_all = jnp.concatenate([k_past.astype(k.dtype), k], axis=2)
        v_all = jnp.concatenate([v_past.astype(v.dtype), v], axis=2)
        o_a = _sb_attend(q, k_all, v_all, past + jnp.arange(T), jnp.arange(past + T))
    o_a = o_a.transpose(0, 2, 1, 3).reshape(B, T, D_A)
    y_a = jnp.einsum('btc,cd->btd', o_a * jax.nn.silu(z_a), w_o_a)

    p_prev = jnp.concatenate([shift_prev.astype(p_b.dtype), p_b[:, :-1]], axis=1)
    p_mix = p_b + mu_shift * (p_prev - p_b)
    shift_new = p_b[:, -1:]
    r, kb, vb, lat_w, lat_a, z_b = jnp.split(p_mix, B_SPLITS, axis=-1)
    w_raw = -jax.nn.softplus(-(w0 + jnp.einsum('btr,rc->btc', jnp.tanh(lat_w), w2))) - 0.5
    decay = jnp.exp(-jnp.exp(w_raw.astype(f32)))
    a = jax.nn.sigmoid(a0 + jnp.einsum('btr,rc->btc', lat_a, a2))
    kk = (kb * k_k).reshape(B, T, H_B, D_HEAD_B).astype(f32)
    kk = kk / jnp.maximum(jnp.sqrt(jnp.sum(kk * kk, axis=-1, keepdims=True)), 1e-12)
    kb = kb * (1.0 + (a - 1.0) * k_a)
    r_h = r.reshape(B, T, H_B, D_HEAD_B).astype(f32)
    k_h = kb.reshape(B, T, H_B, D_HEAD_B).astype(f32)
    v_h = vb.reshape(B, T, H_B, D_HEAD_B).astype(f32)
    a_h = a.reshape(B, T, H_B, D_HEAD_B).astype(f32)
    w_h = decay.reshape(B, T, H_B, D_HEAD_B)
    wkv_new, o_b = _wkv_scan(r_h, w_h, k_h, v_h, kk, a_h, wkv_prev.astype(f32))
    mean = jnp.mean(o_b, axis=-1, keepdims=True)
    var = jnp.mean(jnp.square(o_b - mean), axis=-1, keepdims=True)
    o_b = ((o_b - mean) * lax.rsqrt(var + LNX_EPS)).reshape(B, T, D_B)
    o_b = o_b * lnx_g.astype(f32) + lnx_b.astype(f32)
    bonus = jnp.sum(r_h * k_h * r_k.astype(f32), axis=-1, keepdims=True) * v_h
    o_b = (o_b + bonus.reshape(B, T, D_B)).astype(x.dtype)
    y_b = jnp.einsum('btc,cd->btd', o_b * jax.nn.silu(z_b), w_o_b)

    g_a, g_b = jnp.split(jax.nn.sigmoid(p_gate), 2, axis=-1)
    y = jnp.einsum('btd,de->bte', g_a * y_a + g_b * y_b, w_out)
    return x + y, k, v, shift_new, wkv_new


def setup_inputs(seed: int = 0) -> dict:
    key = jax.random.key(seed)
    ks = jax.random.split(key, 24)
    nrm = lambda k, shape, s: jax.random.normal(k, shape, jnp.float32) * s
    L = DEPTH
    return {
        "x_prompt": nrm(ks[0], (BATCH, SEQ, D_MODEL), 1.0),
        "x_sample": nrm(ks[1], (DEC_BATCH, DEC_SEQ, D_MODEL), 1.0),
        "cache_sb_k": nrm(ks[2], (L, DEC_BATCH, H_A, PAST_LEN, D_HEAD_A), 1.0),
        "cache_sb_v": nrm(ks[3], (L, DEC_BATCH, H_A, PAST_LEN, D_HEAD_A), 1.0),
        "state_shift": nrm(ks[4], (L, DEC_BATCH, 1, C_SHIFT), 1.0),
        "state_wkv": nrm(ks[5], (L, DEC_BATCH, H_B, D_HEAD_B, D_HEAD_B), 0.5),
        "norm_g": 1.0 + nrm(ks[6], (L, D_MODEL), 0.02),
        "w_in": nrm(ks[7], (L, D_MODEL, N_IN), D_MODEL ** -0.5),
        "mu_shift": jax.random.uniform(ks[8], (L, C_SHIFT), jnp.float32),
        "w0": jax.random.uniform(ks[9], (L, D_B), jnp.float32, -4.0, 0.0),
        "w2": nrm(ks[10], (L, W_LORA, D_B), 0.1 * W_LORA ** -0.5),
        "a0": nrm(ks[11], (L, D_B), 0.1),
        "a2": nrm(ks[12], (L, A_LORA, D_B), 0.1 * A_LORA ** -0.5),
        "k_k": 0.85 + nrm(ks[13], (L, D_B), 0.02),
        "k_a": 1.0 + nrm(ks[14], (L, D_B), 0.02),
        "r_k": nrm(ks[15], (L, H_B, D_HEAD_B), 0.1),
        "lnx_g": 1.0 + nrm(ks[16], (L, D_B), 0.02),
        "lnx_b": nrm(ks[17], (L, D_B), 0.02),
        "w_o_a": nrm(ks[18], (L, D_A, D_MODEL), D_A ** -0.5),
        "w_o_b": nrm(ks[19], (L, D_B, D_MODEL), D_B ** -0.5),
        "w_out": nrm(ks[20], (L, D_MODEL, D_MODEL), D_MODEL ** -0.5),
        "final_norm_g": 1.0 + nrm(ks[21], (D_MODEL,), 0.02),
    }


def reference(x_prompt, x_sample, cache_sb_k, cache_sb_v, state_shift, state_wkv,
              norm_g, w_in, mu_shift, w0, w2, a0, a2, k_k, k_a, r_k, lnx_g, lnx_b,
              w_o_a, w_o_b, w_out, final_norm_g):
    xp, xs = x_prompt, x_sample
    B = xp.shape[0]
    kp_l, vp_l, sp_l, wp_l = [], [], [], []
    ks_l, vs_l, ss_l, ws_l = [], [], [], []
    for l in range(DEPTH):
        params = (norm_g[l], w_in[l], mu_shift[l], w0[l], w2[l], a0[l], a2[l], k_k[l], k_a[l],
                  r_k[l], lnx_g[l], lnx_b[l], w_o_a[l], w_o_b[l], w_out[l])
        zero_shift = jnp.zeros((B, 1, C_SHIFT), xp.dtype)
        zero_wkv = jnp.zeros((B, H_B, D_HEAD_B, D_HEAD_B), jnp.float32)
        xp, kp, vp, sp, wp = _layer(xp, zero_shift, None, None, zero_wkv, *params)
        xs, kss, vss, sss, wss = _layer(xs, state_shift[l], cache_sb_k[l], cache_sb_v[l],
                                        state_wkv[l], *params)
        kp_l.append(kp); vp_l.append(vp); sp_l.append(sp); wp_l.append(wp)
        ks_l.append(kss); vs_l.append(vss); ss_l.append(sss); ws_l.append(wss)
    y_prompt = _rmsnorm(xp, final_norm_g)
    y_sample = _rmsnorm(xs, final_norm_g)
    return (y_prompt, y_sample,
            jnp.stack(kp_l), jnp.stack(vp_l), jnp.stack(sp_l), jnp.stack(wp_l),
            jnp.stack(ks_l), jnp.stack(vs_l), jnp.stack(ss_l), jnp.stack(ws_l))
```

```python
from contextlib import ExitStack
import numpy as np
import ml_dtypes
import concourse.bass as bass
import concourse.mybir as mybir
from concourse.bass_utils import run_bass_kernel_spmd

F32 = mybir.dt.float32
BF16 = mybir.dt.bfloat16
AF = mybir.ActivationFunctionType
ALU = mybir.AluOpType
AX = mybir.AxisListType
EPS = 1e-6
import os
NDUMMY = int(os.environ.get("NDUMMY", "8"))
LNX_EPS = 64e-5


class KB:
    def __init__(self, nc, ndma=16):
        self.nc = nc
        self.E = {}
        self.dummy = [nc.alloc_semaphore(name=f"dummy{i}") for i in range(NDUMMY)]
        for name, eng in (("pe", nc.tensor), ("act", nc.scalar), ("dve", nc.vector), ("pool", nc.gpsimd), ("sp", nc.sync)):
            self.E[name] = dict(eng=eng, sem=nc.alloc_semaphore(name="s_" + name), cnt=0, seen={})
        self.Q = {}
        for q in ("sp", "pool"):
            self.Q[q] = dict(pool=[dict(sem=nc.alloc_semaphore(name=f"d_{q}{i}"), cnt=0) for i in range(ndma)], pi=0)
        self.lw = {}
        self.rd = {}
        self.semobj = {}

    def _wait(self, en, sem, val):
        if val <= 0:
            return
        E = self.E[en]
        k = id(sem)
        self.semobj[k] = sem
        if E["seen"].get(k, 0) >= val:
            return
        E["eng"].wait_ge(sem, val)
        E["seen"][k] = val

    def _deps(self, en, r, w, skip_self=False):
        E = self.E[en]
        deps = {}

        def add(ev):
            if ev is None:
                return
            sem, val = ev
            if skip_self and sem is E["sem"]:
                return
            k = id(sem)
            self.semobj[k] = sem
            if deps.get(k, 0) < val:
                deps[k] = val

        for key in r:
            add(self.lw.get(key))
        for key in w:
            add(self.lw.get(key))
            for k, v in self.rd.get(key, {}).items():
                add((self.semobj[k], v))
        for k, v in deps.items():
            self._wait(en, self.semobj[k], v)

    def _record(self, ev, r, w):
        sem, val = ev
        k = id(sem)
        self.semobj[k] = sem
        for key in r:
            d = self.rd.setdefault(key, {})
            if d.get(k, 0) < val:
                d[k] = val
        for key in w:
            self.lw[key] = ev
            self.rd[key] = {}

    def op(self, en, fn, r=(), w=(), inc=True):
        E = self.E[en]
        self._deps(en, r, w, skip_self=(en == "pe"))
        ins = fn(E["eng"])
        if inc:
            ins.then_inc(E["sem"], 1)
            E["cnt"] += 1
            ev = (E["sem"], E["cnt"])
        else:
            ev = (E["sem"], E["cnt"] + 1)
        self._record(ev, r, w)
        return ins

    def dma(self, q, out, in_, r=(), w=(), **kw):
        Q = self.Q[q]
        E = self.E[q]
        self._deps(q, r, w)
        s = Q["pool"][Q["pi"]]
        Q["pi"] = (Q["pi"] + 1) % len(Q["pool"])
        self._wait(q, s["sem"], 16 * s["cnt"])
        E["eng"].dma_start(out=out, in_=in_, **kw).then_inc(s["sem"], 16)
        s["cnt"] += 1
        self._record((s["sem"], 16 * s["cnt"]), r, w)

    def finish(self):
        for q in self.Q:
            for s in self.Q[q]["pool"]:
                self._wait("sp", s["sem"], 16 * s["cnt"])


def _common(nc, kb, es, D, hh):
    C = {}

    def sb(name, shape, dt):
        return es.enter_context(nc.sbuf_tensor(name, shape, dt))

    for nm, dt in (("identb", BF16), ("triI", BF16), ("onesb", BF16), ("mskS", F32), ("mskSblk", F32)):
        C[nm] = sb(f"a{hh}c_" + nm, [128, 128], dt)
        kb.dma("sp", C[nm][:], D[nm], w=[nm])
    return C


def _load_w_bf16(nc, kb, wdram, wsb, ncols, key, pfx=""):
    with nc.sbuf_tensor(pfx + "wst0_" + key, [128, ncols], F32) as s0, nc.sbuf_tensor(pfx + "wst1_" + key, [128, ncols], F32) as s1:
        for c in range(8):
            st = (s0, s1)[c % 2]
            sk = f"wst{c % 2}_{key}"
            kb.dma("sp", st[:], wdram[128 * c:128 * c + 128, :], w=[sk])
            if False:
                pass
            else:
                kb.op("dve", lambda e: e.tensor_copy(out=wsb[:, c, :], in_=st[:]), r=[sk], w=[f"{key}{c}"])
        _drain(kb)


def _norm_hT(nc, kb, T, x, xk, dst=None, dstk="hT"):
    ss, junk, hb, hT, psT, gnb, identb = T["ss"], T["junk"], T["hb"], T["hT"], T["psT"], T["gnb"], T["identb"]
    kb.op("dve", lambda e: e.memset(ss[:, 0:1], 0.0), w=["ss"])
    kb.op("act", lambda e: e.activation(out=junk[:], in_=x[:], func=AF.Square, accum_out=ss[:, 0:1]), r=[xk], w=["junk", "ss"])
    kb.op("dve", lambda e: e.tensor_scalar(out=ss[:, 0:1], in0=ss[:, 0:1], scalar1=1.0 / 1024, scalar2=EPS, op0=ALU.mult, op1=ALU.add), r=["ss"], w=["ss"])
    kb.op("act", lambda e: e.activation(out=ss[:, 0:1], in_=ss[:, 0:1], func=AF.Ln), r=["ss"], w=["ss"])
    kb.op("act", lambda e: e.activation(out=ss[:, 0:1], in_=ss[:, 0:1], func=AF.Exp, scale=-0.5), r=["ss"], w=["ss"])
    kb.op("dve", lambda e: e.scalar_tensor_tensor(out=hb[:], in0=x[:], scalar=ss[:, 0:1], in1=gnb[:], op0=ALU.mult, op1=ALU.mult),
          r=[xk, "ss", "gnb"], w=["hb"])
    for c in range(8):
        kb.op("pe", lambda e: e.transpose(out=psT[:, c, :], in_=hb[:, 128 * c:128 * c + 128], identity=identb[:]),
              r=["hb", "identb"], w=["psT"], inc=(c == 7))
    if dst is None:
        kb.op("dve", lambda e: e.tensor_copy(out=hT[:], in_=psT[:]), r=["psT"], w=["hT"])
    else:
        kb.op("dve", lambda e: e.tensor_copy(out=dst, in_=psT[:]), r=["psT"], w=[dstk])


def _silu_from_psum(kb, pp, pk, et, zs, zk):
    kb.op("act", lambda e: e.activation(out=et[:], in_=pp[:], func=AF.Exp, scale=-1.0), r=[pk], w=["et"])
    kb.op("dve", lambda e: e.tensor_scalar_add(out=et[:], in0=et[:], scalar1=1.0), r=["et"], w=["et"])
    kb.op("dve", lambda e: e.reciprocal(out=et[:], in_=et[:]), r=["et"], w=["et"])
    kb.op("dve", lambda e: e.tensor_tensor(out=zs[:], in0=pp[:], in1=et[:], op=ALU.mult), r=[pk, "et"], w=[zk])


def pass_a(nc, kb, D, cfg, hh):
    NTP = cfg["NTP"]
    with ExitStack() as es:
        def sb(name, shape, dt):
            return es.enter_context(nc.sbuf_tensor(f"a{hh}_" + name, shape, dt))

        def ps(name, shape, dt):
            return es.enter_context(nc.psum_tensor(f"a{hh}_" + name, shape, dt))

        T = _common(nc, kb, es, D, hh)
        wAb = sb("wAb", [128, 8, 2048], BF16)
        if not cfg.get("skipw"):
            _load_w_bf16(nc, kb, D[f"wA{hh}"], wAb, 2048, "wA", f"a{hh}")
        T["gnb"] = sb("gnb", [128, 1024], F32)
        kb.dma("sp", T["gnb"][:], D["norm_g"].partition_broadcast(128), w=["gnb"])
        xt = [sb("xt0", [128, 1024], F32), sb("xt1", [128, 1024], F32)]
        T["ss"] = sb("ss", [128, 2], F32)
        T["junk"] = sb("junk", [128, 1024], BF16)
        T["hb"] = sb("hb", [128, 1024], BF16)
        T["hT"] = sb("hT", [128, 8, 128], BF16)
        qb = sb("qb", [128, 512], BF16)
        kf = sb("kf", [128, 512], F32)
        ksb = sb("ksb", [128, 512], BF16)
        vf = sb("vf", [128, 512], F32)
        zs = sb("zs", [128, 512], F32)
        et = sb("et", [128, 512], F32)
        qT = sb("qT", [128, 2, 4, 128], BF16)
        kb.op("dve", lambda e: e.memset(qT[:], 0.0), w=["qT"])
        ua = sb("ua", [128, 512], BF16)
        uaT = sb("uaT", [128, 4, 128], BF16)
        NBUF = 6
        eb = [sb(f"e{q}", [128, 512], F32) for q in range(NBUF)]
        spb = [sb(f"sp{q}", [128, 512], BF16) for q in range(NBUF)]
        eRb = [sb(f"eR{q}", [128, 512], F32) for q in range(3)]
        wTb = [sb(f"wT{q}", [128, 512], BF16) for q in range(3)]
        Acc = [sb(f"Acc{g}", [128, 512], BF16) for g in range(2)]
        T["psT"] = ps("psT", [128, 8, 128], BF16)
        psP = [ps("psP0", [128, 512], F32), ps("psP1", [128, 512], F32)]
        psS = [ps("psS0", [128, 512], F32), ps("psS1", [128, 512], F32)]
        psR = [ps("psR0", [128, 512], F32), ps("psR1", [128, 512], F32)]
        psO = ps("psO", [128, 512], F32)
        psT, hT, identb = T["psT"], T["hT"], T["identb"]
        par = [0, 0]

        def proj(g, pp, pk):
            for c in range(8):
                kb.op("pe", lambda e: e.matmul(pp[:], lhsT=hT[:, c, :], rhs=wAb[:, c, 512 * g:512 * g + 512], start=(c == 0), stop=(c == 7)),
                      r=["hT", f"wA{c}"], w=[pk], inc=(c == 7))

        units = []
        uctr = [0]

        def sb_block(i, j, g, S_fill, pv, diag_mask, first, last, has_acc):
            units.append(dict(g=g, S_fill=S_fill, pv=pv, diag_mask=diag_mask, first=first, last=last, has_acc=has_acc, idx=uctr[0]))
            uctr[0] += 1

        def emit_stage(U, st):
            g, q, s2 = U["g"], U["idx"] % NBUF, U["idx"] % 2
            e_, sp_, eR_, wT_ = eb[q], spb[q], eRb[q % 3], wTb[q % 3]
            ek, spk, eRk, wTk = f"e{q}", f"sp{q}", f"eR{q % 3}", f"wT{q % 3}"
            S, R, Sk, Rk = psS[s2], psR[s2], f"psS{s2}", f"psR{s2}"
            if st == 0:
                U["S_fill"](S, Sk)
            elif st == 1:
                kb.op("act", lambda e: e.activation(out=e_[:], in_=S[:], func=AF.Exp), r=[Sk], w=[ek])
                if U["diag_mask"] is not None:
                    mk, mkk = U["diag_mask"]
                    kb.op("dve", lambda e: e.tensor_tensor(out=e_[:].rearrange("p (a b) -> p a b", b=128),
                                                            in0=e_[:].rearrange("p (a b) -> p a b", b=128),
                                                            in1=mk[:].unsqueeze(1).broadcast_to([128, 4, 128]), op=ALU.mult),
                          r=[ek, mkk], w=[ek])
            elif st == 2:
                kb.op("act", lambda e: e.activation(out=sp_[:], in_=e_[:], func=AF.Ln, bias=1.0), r=[ek], w=[spk])
            elif st == 3:
                has_acc = U["has_acc"]
                kb.op("pe", lambda e: e.matmul(R[:], lhsT=T["triI"][:], rhs=sp_[:], start=True, stop=(not has_acc)),
                      r=["triI", spk], w=[Rk], inc=(not has_acc))
                if has_acc:
                    kb.op("pe", lambda e: e.matmul(R[:], lhsT=T["onesb"][:], rhs=Acc[g][:], start=False, stop=True),
                          r=["onesb", f"Acc{g}"], w=[Rk])
                if not U["last"]:
                    if U["first"]:
                        kb.op("dve", lambda e: e.tensor_copy(out=Acc[g][:], in_=sp_[:]), r=[spk], w=[f"Acc{g}"])
                    else:
                        kb.op("dve", lambda e: e.tensor_tensor(out=Acc[g][:], in0=Acc[g][:], in1=sp_[:], op=ALU.add), r=[spk, f"Acc{g}"], w=[f"Acc{g}"])
            elif st == 4:
                kb.op("act", lambda e: e.activation(out=eR_[:], in_=R[:], func=AF.Exp, scale=-1.0), r=[Rk], w=[eRk])
            elif st == 5:
                U["pv"](e_, ek, eR_, eRk, wT_, wTk, 0)
            elif st == 6:
                U["pv"](e_, ek, eR_, eRk, wT_, wTk, 1)

        def run_units(chunks=(), hooks=None):
            n = len(units)
            nsteps = n + 6
            chunks = list(chunks)
            at = dict(hooks or {})
            for m in range(len(chunks)):
                at.setdefault(min(nsteps - 1, ((m + 1) * nsteps) // (len(chunks) + 1)), []).append(chunks[m])
            for step in range(nsteps):
                for st in (3, 6, 5, 2, 1, 4, 0):
                    u = step - st
                    if 0 <= u < n:
                        emit_stage(units[u], st)
                for c in at.get(step, ()):
                    c()
            del units[:]

        if NTP > 0:
            es2 = ExitStack()
            kT = es2.enter_context(nc.sbuf_tensor(f"a{hh}_kT", [128, 4, 128 * NTP], BF16))
            vsb = es2.enter_context(nc.sbuf_tensor(f"a{hh}_vsb", [128, NTP, 512], BF16))
            qT2 = es2.enter_context(nc.sbuf_tensor(f"a{hh}_qT2", [128, 2, 4, 128], BF16))
            kb.op("dve", lambda e: e.memset(qT2[:], 0.0), w=["qT2"])
            zs2 = es2.enter_context(nc.sbuf_tensor(f"a{hh}_zs2", [128, 512], F32))
            zs3 = es2.enter_context(nc.sbuf_tensor(f"a{hh}_zs3", [128, 512], F32))
            kb.dma("sp", xt[0][:], D["xc"][0:128, :], w=["xt0"])
            qTs, zss = [qT, qT2], [zs, zs2, zs3]

            def front_chunks(i):
                x, xk = xt[i % 2], f"xt{i % 2}"
                qTi, qTk, zsi, zsk = qTs[i % 2], ("qT", "qT2")[i % 2], zss[i % 3], ("zs", "zs2", "zs3")[i % 3]

                def c0():
                    if i + 1 < NTP:
                        kb.dma("sp", xt[(i + 1) % 2][:], D["xc"][128 * (i + 1):128 * (i + 2), :], w=[f"xt{(i + 1) % 2}"])
                    _norm_hT(nc, kb, T, x, xk)

                def c1():
                    proj(0, psP[0], "psP0")
                    kb.op("dve", lambda e: e.tensor_copy(out=qb[:], in_=psP[0][:]), r=["psP0"], w=["qb"])

                def c2():
                    proj(1, psP[1], "psP1")
                    kb.op("act", lambda e: e.activation(out=kf[:], in_=psP[1][:], func=AF.Identity), r=["psP1"], w=["kf", "psP1"])
                    kb.op("dve", lambda e: e.tensor_scalar_mul(out=ksb[:], in0=psP[1][:], scalar1=0.125), r=["psP1"], w=["ksb"])
                    kb.dma("sp", D[f"ko{hh}"][:, 128 * i:128 * i + 128, :].rearrange("h t d -> t h d"),
                           kf[:].rearrange("t (h d) -> t h d", d=64), r=["kf"])

                def c3():
                    proj(2, psP[0], "psP0")
                    kb.op("act", lambda e: e.activation(out=vf[:], in_=psP[0][:], func=AF.Identity), r=["psP0"], w=["vf", "psP0"])
                    kb.op("dve", lambda e: e.tensor_copy(out=vsb[:, i, :], in_=psP[0][:]), r=["psP0"], w=[f"vsb{i}"])
                    kb.dma("sp", D[f"vo{hh}"][:, 128 * i:128 * i + 128, :].rearrange("h t d -> t h d"),
                           vf[:].rearrange("t (h d) -> t h d", d=64), r=["vf"])

                def c4():
                    proj(3, psP[1], "psP1")
                    _silu_from_psum(kb, psP[1], "psP1", et, zsi, zsk)

                def c5():
                    for p in range(4):
                        kb.op("pe", lambda e: e.transpose(out=psT[:, p, :], in_=qb[:, 128 * p:128 * p + 128], identity=identb[:]),
                              r=["qb", "identb"], w=["psT"], inc=False)
                    for p in range(4):
                        kb.op("pe", lambda e: e.transpose(out=psT[:, 4 + p, :], in_=ksb[:, 128 * p:128 * p + 128], identity=identb[:]),
                              r=["ksb", "identb"], w=["psT"], inc=(p == 3))
                    kb.op("dve", lambda e: e.tensor_copy(out=qTi[0:64, 0, :, :], in_=psT[0:64, 0:4, :]), r=["psT"], w=[qTk])
                    kb.op("dve", lambda e: e.tensor_copy(out=qTi[64:128, 1, :, :], in_=psT[64:128, 0:4, :]), r=["psT"], w=[qTk])
                    kb.op("dve", lambda e: e.tensor_copy(out=kT[:, :, 128 * i:128 * i + 128], in_=psT[:, 4:8, :]), r=["psT"], w=[f"kT{i}"])

                return [c0, c1, c2, c3, c4, c5]

            for c in front_chunks(0):
                c()
            hooks = {}

            def epilogue(i):
                zsi, zsk = zss[i % 3], ("zs", "zs2", "zs3")[i % 3]
                kb.op("dve", lambda e: e.tensor_tensor(out=ua[:], in0=psO[:], in1=zsi[:], op=ALU.mult), r=["psO", zsk], w=["ua"])
                for p in range(4):
                    kb.op("pe", lambda e: e.transpose(out=psT[:, p, :], in_=ua[:, 128 * p:128 * p + 128], identity=identb[:]),
                          r=["ua", "identb"], w=["psT"], inc=(p == 3))
                kb.op("dve", lambda e: e.tensor_copy(out=uaT[:], in_=psT[:, 0:4, :]), r=["psT"], w=["uaT"])
                kb.dma("sp", D["xab"][i // 6][0:512, :].rearrange("(p c) t -> c p t", c=128)[:, :, 128 * (i % 6):128 * (i % 6) + 128], uaT[:], r=["uaT"], w=[f"xab{i // 6}"])

            for i in range(NTP):
                qTi, qTk = qTs[i % 2], ("qT", "qT2")[i % 2]
                u0 = len(units)
                for j in range(i, -1, -1):
                    for g in range(2):
                        def S_fill(S, Sk, j=j, g=g, qTi=qTi, qTk=qTk):
                            for hh in range(4):
                                head = 4 * g + hh
                                p, h2 = head // 2, head % 2
                                kb.op("pe", lambda e: e.matmul(S[:, 128 * hh:128 * hh + 128], lhsT=kT[:, p, 128 * j:128 * j + 128],
                                                               rhs=qTi[:, h2, p, :], start=True, stop=True),
                                      r=[f"kT{j}", qTk], w=[Sk], inc=(hh == 3))

                        def pv(e_, ek, eR_, eRk, wT_, wTk, part, j=j, g=g, i=i):
                            if part == 0:
                                kb.op("dve", lambda e: e.tensor_tensor(out=wT_[:], in0=e_[:], in1=eR_[:], op=ALU.mult), r=[ek, eRk], w=[wTk])
                                return
                            for hh in range(4):
                                head = 4 * g + hh
                                kb.op("pe", lambda e: e.matmul(psO[:, 64 * head:64 * head + 64], lhsT=wT_[:, 128 * hh:128 * hh + 128],
                                                               rhs=vsb[:, j, 64 * head:64 * head + 64], start=(j == i and head == 0),
                                                               stop=(j == 0 and head == 7), skip_group_check=True),
                                      r=[wTk, f"vsb{j}"], w=["psO"], inc=(hh == 3))

                        sb_block(i, j, g, S_fill, pv, (T["mskS"], "mskS") if j == i else None,
                                 first=(j == i), last=(j == 0), has_acc=(j < i))
                u1 = len(units)
                if i + 1 < NTP:
                    ch = front_chunks(i + 1)
                    for m, c in enumerate(ch):
                        hooks.setdefault(u0 + ((m + 1) * (u1 - u0)) // (len(ch) + 1), []).append(c)
                hooks.setdefault(u1 - 1 + 6, []).append(lambda i=i: epilogue(i))
            run_units(hooks=hooks)
            kb.op("dve", lambda e: e.memset(kT[:, 0, 0:1], 0.0), w=[f"kT{j}" for j in range(NTP)])
            kb.op("dve", lambda e: e.memset(vsb[:, 0, 0:1], 0.0), w=[f"vsb{j}" for j in range(NTP)])
            es2.close()
        _drain(kb)
        if cfg.get("sample", True):
            es3 = ExitStack()

            def sb3(name, shape, dt):
                return es3.enter_context(nc.sbuf_tensor(f"a{hh}s_" + name, shape, dt))

            kTn = sb3("kTn", [128, 4, 128], BF16)
            vbn = sb3("vbn", [128, 512], BF16)
            kpT = sb3("kpT", [128, 2, 8, 1024], BF16)
            vp = sb3("vp", [128, 8, 8, 256], BF16)
            kstb = [sb3(f"kst{q}", [128, 8, 256], F32) for q in range(2)]
            vstb = [sb3("vst0", [128, 8, 256], F32)] * 2
            wTd = [sb3(f"wTd{q}", [128, 4, 1152], BF16) for q in range(2)]
            identf = sb3("identf", [128, 128], F32)
            kb.dma("sp", identf[:], D["identf"], w=["identf"])
            for q in range(2):
                kb.op("dve", lambda e: e.memset(wTd[q][:], 0.0), w=[f"wTd{q}"])
            x, xk = xt[0], "xt0"
            kb.dma("sp", x[:], D["xc"][4096:4224, :], w=[xk])
            _norm_hT(nc, kb, T, x, xk)
            proj(0, psP[0], "psP0")
            kb.op("dve", lambda e: e.tensor_copy(out=qb[:], in_=psP[0][:]), r=["psP0"], w=["qb"])
            proj(1, psP[1], "psP1")
            kb.op("act", lambda e: e.activation(out=kf[:], in_=psP[1][:], func=AF.Identity), r=["psP1"], w=["kf", "psP1"])
            kb.op("dve", lambda e: e.tensor_scalar_mul(out=ksb[:], in0=psP[1][:], scalar1=0.125), r=["psP1"], w=["ksb"])
            for b in range(8):
                kb.dma("sp", D[f"kso{hh}"][b].rearrange("h t d -> t h d"), kf[16 * b:16 * b + 16, :].rearrange("t (h d) -> t h d", d=64), r=["kf"])
            proj(2, psP[0], "psP0")
            kb.op("act", lambda e: e.activation(out=vf[:], in_=psP[0][:], func=AF.Identity), r=["psP0"], w=["vf", "psP0"])
            kb.op("dve", lambda e: e.tensor_copy(out=vbn[:], in_=psP[0][:]), r=["psP0"], w=["vbn"])
            for b in range(8):
                kb.dma("sp", D[f"vso{hh}"][b].rearrange("h t d -> t h d"), vf[16 * b:16 * b + 16, :].rearrange("t (h d) -> t h d", d=64), r=["vf"])
            proj(3, psP[1], "psP1")
            _silu_from_psum(kb, psP[1], "psP1", et, zs, "zs")
            for p in range(4):
                kb.op("pe", lambda e: e.transpose(out=psT[:, p, :], in_=qb[:, 128 * p:128 * p + 128], identity=identb[:]),
                      r=["qb", "identb"], w=["psT"], inc=False)
            for p in range(4):
                kb.op("pe", lambda e: e.transpose(out=psT[:, 4 + p, :], in_=ksb[:, 128 * p:128 * p + 128], identity=identb[:]),
                      r=["ksb", "identb"], w=["psT"], inc=(p == 3))
            kb.op("dve", lambda e: e.tensor_copy(out=qT[0:64, 0, :, :], in_=psT[0:64, 0:4, :]), r=["psT"], w=["qT"])
            kb.op("dve", lambda e: e.tensor_copy(out=qT[64:128, 1, :, :], in_=psT[64:128, 0:4, :]), r=["psT"], w=["qT"])
            kb.op("dve", lambda e: e.tensor_copy(out=kTn[:], in_=psT[:, 4:8, :]), r=["psT"], w=["kTn"])
            for g in range(2):
                def load_cache(b, g=g):
                    kst, vst, q = kstb[b % 2], vstb[b % 2], b % 2
                    for j in range(8):
                        kb.dma("sp", kst[:, j, :].rearrange("s (h d) -> s h d", d=64),
                               D[f"kc{hh}"][b, 4 * g:4 * g + 4, 128 * j:128 * j + 128, :].rearrange("h s d -> s h d"), w=[f"kst{q}_{j}"])
                        kb.dma("sp", vst[:, j, :].rearrange("s (h d) -> s h d", d=64),
                               D[f"vc{hh}"][b, 4 * g:4 * g + 4, 128 * j:128 * j + 128, :].rearrange("h s d -> s h d"), w=[f"vst_{j}"])

                load_cache(0)
                for b in range(8):
                    kst, vst, q = kstb[b % 2], vstb[b % 2], b % 2
                    kb.op("dve", lambda e: e.tensor_copy(out=vp[:, :, b, :], in_=vst[:]), r=[f"vst_{j}" for j in range(8)], w=["vp"])
                    if b + 1 < 8:
                        load_cache(b + 1)
                    for pl in range(2):
                        for half in range(2):
                            Pq, Pqk = psP[half], f"psP{half}"
                            for jj in range(4):
                                j = 4 * half + jj
                                kb.op("pe", lambda e: e.matmul(Pq[:, 128 * jj:128 * jj + 128], lhsT=kst[:, j, 128 * pl:128 * pl + 128], rhs=identf[:], start=True, stop=True),
                                      r=[f"kst{q}_{j}", "identf"], w=[Pqk], inc=(jj == 3))
                            kb.op("dve", lambda e: e.tensor_scalar_mul(out=kpT[:, pl, b, 512 * half:512 * half + 512], in0=Pq[:], scalar1=0.125), r=[Pqk], w=["kpT"])
                def S_new(S, Sk, g=g):
                    for h4 in range(4):
                        head = 4 * g + h4
                        p, h2 = head // 2, head % 2
                        kb.op("pe", lambda e: e.matmul(S[:, 128 * h4:128 * h4 + 128], lhsT=kTn[:, p, :], rhs=qT[:, h2, p, :], start=True, stop=True),
                              r=["kTn", "qT"], w=[Sk], inc=(h4 == 3))

                def pv_new(e_, ek, eR_, eRk, wT_, wTk, part, g=g):
                    if part == 0:
                        kb.op("dve", lambda e: e.tensor_tensor(out=wT_[:], in0=e_[:], in1=eR_[:], op=ALU.mult), r=[ek, eRk], w=[wTk])
                        return
                    for h4 in range(4):
                        head = 4 * g + h4
                        kb.op("pe", lambda e: e.matmul(psO[:, 64 * head:64 * head + 64], lhsT=wT_[:, 128 * h4:128 * h4 + 128], rhs=vbn[:, 64 * head:64 * head + 64],
                                                       start=(head == 0), stop=False, skip_group_check=True),
                              r=[wTk, "vbn"], w=["psO"], inc=(h4 == 3))

                sb_block(32, 8, g, S_new, pv_new, (T["mskSblk"], "mskSblk"), first=True, last=False, has_acc=False)
                for j in range(7, -1, -1):
                    def S_past(S, Sk, g=g, j=j):
                        for h4 in range(4):
                            head = 4 * g + h4
                            p, h2, pl = head // 2, head % 2, h4 // 2
                            for b in range(8):
                                kb.op("pe", lambda e: e.matmul(S[:, 128 * h4 + 16 * b:128 * h4 + 16 * b + 16], lhsT=kpT[:, pl, b, 128 * j:128 * j + 128],
                                                               rhs=qT[:, h2, p, 16 * b:16 * b + 16], start=True, stop=True),
                                      r=["kpT", "qT"], w=[Sk], inc=(h4 == 3 and b == 7))

                    def pv_past(e_, ek, eR_, eRk, wT_, wTk, part, g=g, j=j):
                        q = j % 2
                        Wd = wTd[q]
                        dview = Wd[:].rearrange("p h (b x) -> p h b x", x=144)[:, :, :, 0:16]
                        if part == 0:
                            kb.op("dve", lambda e: e.tensor_tensor(out=dview, in0=e_[:].rearrange("p (h b i) -> p h b i", h=4, b=8),
                                                                    in1=eR_[:].rearrange("p (h b i) -> p h b i", h=4, b=8), op=ALU.mult), r=[ek, eRk], w=[f"wTd{q}"])
                            return
                        for h4 in range(4):
                            head = 4 * g + h4
                            for b in range(8):
                                kb.op("pe", lambda e: e.matmul(psO[:, 64 * head:64 * head + 64], lhsT=Wd[:, h4, 128 * b:128 * b + 128], rhs=vp[:, j, b, 64 * h4:64 * h4 + 64],
                                                               start=False, stop=(j == 0 and head == 7 and b == 7), skip_group_check=True),
                                      r=[f"wTd{q}", "vp"], w=["psO"], inc=(h4 == 3 and b == 7))

                    sb_block(32, j, g, S_past, pv_past, None, first=False, last=(j == 0), has_acc=True)
                run_units()
            kb.op("dve", lambda e: e.tensor_tensor(out=ua[:], in0=psO[:], in1=zs[:], op=ALU.mult), r=["psO", "zs"], w=["ua"])
            for p in range(4):
                kb.op("pe", lambda e: e.transpose(out=psT[:, p, :], in_=ua[:, 128 * p:128 * p + 128], identity=identb[:]),
                      r=["ua", "identb"], w=["psT"], inc=(p == 3))
            kb.op("dve", lambda e: e.tensor_copy(out=uaT[:], in_=psT[:, 0:4, :]), r=["psT"], w=["uaT"])
            kb.dma("sp", D["xab"][5][0:512, :].rearrange("(p c) t -> c p t", c=128)[:, :, 256:384], uaT[:], r=["uaT"], w=["xab5"])
            _drain(kb)
            es3.close()
        _drain(kb)


def pass_b(nc, kb, D, cfg, hh):
    NTP = cfg["NTP"]
    with ExitStack() as es:
        def sb(name, shape, dt):
            return es.enter_context(nc.sbuf_tensor(f"b{hh}_" + name, shape, dt))

        def ps(name, shape, dt):
            return es.enter_context(nc.psum_tensor(f"b{hh}_" + name, shape, dt))

        T = {}
        for nm, dt, shp in (("identb", BF16, [128, 128]), ("identf", F32, [128, 128]), ("triF", F32, [128, 128]),
                            ("mS2", F32, [128, 256]), ("mN", F32, [128, 128]), ("onesF", F32, [128, 8])):
            T[nm] = sb("c_" + nm, shp, dt)
            kb.dma("sp", T[nm][:], D[nm], w=[nm])
        wBb = sb("wBb", [128, 8, 2176], BF16)
        _load_w_bf16(nc, kb, D[f"wB{hh}"], wBb, 2176, "wB", f"b{hh}")
        T["gnb"] = sb("gnb", [128, 1024], F32)
        kb.dma("sp", T["gnb"][:], D["norm_g"].partition_broadcast(128), w=["gnb"])
        mub = sb("mub", [128, 2176], F32)
        kb.dma("sp", mub[:], D[f"muB{hh}"].partition_broadcast(128), w=["mub"])
        PB = {}
        for nm in ("w0", "a0", "k_k", "k_a", "r_k", "lnx_g", "lnx_b"):
            PB[nm] = sb("p_" + nm, [128, 512], F32)
            kb.dma("sp", PB[nm][:], D[f"{nm}{hh}"].partition_broadcast(128), w=["p_" + nm])
        w2z = sb("w2z", [128, 2, 512], BF16)
        with nc.sbuf_tensor(f"b{hh}_w2st", [128, 2, 512], F32) as w2st:
            kb.dma("sp", w2st[:], D[f"w2a2z{hh}"], w=["w2st"])
            kb.op("dve", lambda e: e.tensor_copy(out=w2z[:], in_=w2st[:]), r=["w2st"], w=["w2z"])
            _drain(kb)
        xt = [sb("xt0", [128, 1024], F32), sb("xt1", [128, 1024], F32)]
        T["ss"] = sb("ss", [128, 2], F32)
        T["junk"] = sb("junk", [128, 1024], BF16)
        T["hb"] = sb("hb", [128, 1024], BF16)
        T["hT"] = sb("hT", [128, 8, 128], BF16)
        P = [sb("P0", [128, 2176], F32), sb("P1", [128, 2176], F32)]
        Pprev = sb("Pprev", [128, 2176], F32)
        pm = sb("pm_", [128, 2176], F32)
        F = {nm: sb(nm, [128, 512], F32) for nm in ("ew", "Wt", "Winv", "Wprev", "a", "kk", "kmod", "t1", "t2", "zsb", "ob", "cen")}
        Bt = {nm: sb(nm, [128, 512], BF16) for nm in ("Ab", "Rb", "Bb", "Kb", "Vb", "ub")}
        ltb = sb("ltb", [128, 128], BF16)
        ltf = sb("ltf", [128, 128], F32)
        ltT = sb("ltT", [128, 128], BF16)
        n8 = sb("n8", [128, 8], F32)
        bn8 = sb("bn8", [128, 8], F32)
        m8 = sb("m8", [128, 8], F32)
        Wend = sb("Wend", [128, 4], F32)
        ARz = sb("ARz", [128, 2, 4, 2, 128], BF16)
        BKT = sb("BKT", [128, 4, 2, 128], BF16)
        ST = sb("ST", [128, 4, 64], F32)
        STb = sb("STb", [128, 4, 64], BF16)
        sc = sb("sc", [128, 2, 512], BF16)
        Pn = [sb("Pn0", [128, 2, 2, 128], BF16), sb("Pn1", [128, 2, 2, 128], BF16)]
        YT = sb("YT", [128, 2, 128], BF16)
        scp = [sb(f"scp{p}", [128, 2, 512], BF16) for p in range(4)]
        Pnp = [[sb(f"Pnp{p}_{q}", [128, 2, 2, 128], BF16) for q in range(3)] for p in range(4)]
        YTp = [sb(f"YTp{p}", [128, 2, 128], BF16) for p in range(4)]
        rhs0a = sb("rhs0a", [128, 4, 2, 64], BF16)
        Uba = sb("Uba", [128, 4, 2, 64], BF16)
        rhs0 = sb("rhs0", [128, 2, 64], BF16)
        Ub = sb("Ub", [128, 2, 64], BF16)
        ubT = sb("ubT", [128, 4, 128], BF16)
        psT = ps("psT", [128, 8, 128], BF16)
        T["psT"] = psT
        psP = [ps("psP0", [128, 512], F32), ps("psP1", [128, 512], F32)]
        psW = [ps("psW0", [128, 512], F32), ps("psW1", [128, 512], F32)]
        psC = [ps("psC0", [128, 512], F32), ps("psC1", [128, 512], F32)]
        psO = ps("psO", [128, 512], F32)
        psX = psP[0]
        hT, identb = T["hT"], T["identb"]
        kb.op("dve", lambda e: e.memset(ARz[:], 0.0), w=["ARz"])
        kb.op("dve", lambda e: e.memset(ST[:], 0.0), w=["ST"])
        kb.op("dve", lambda e: e.memset(STb[:], 0.0), w=["STb"])

        def v3(ap):
            return ap.rearrange("p (a b) -> p a b", b=64)

        def bc8(ap8):
            return ap8.unsqueeze(2).broadcast_to([128, 8, 64])

        def sigmoid_inplace(buf, key, src, srck, scale=-1.0):
            kb.op("act", lambda e: e.activation(out=buf, in_=src, func=AF.Exp, scale=scale), r=[srck], w=[key])
            kb.op("act", lambda e: e.activation(out=buf, in_=buf, func=AF.Ln, bias=1.0), r=[key], w=[key])
            kb.op("act", lambda e: e.activation(out=buf, in_=buf, func=AF.Exp, scale=-1.0), r=[key], w=[key])

        def tile_b(ti, x, xk, Pc, Pk, sample):
            _norm_hT(nc, kb, T, x, xk)
            for g in range(5):
                ncol = 512 if g < 4 else 128
                pp, pk = psP[g % 2], f"psP{g % 2}"
                for c in range(8):
                    kb.op("pe", lambda e: e.matmul(pp[:, 0:ncol], lhsT=hT[:, c, :], rhs=wBb[:, c, 512 * g:512 * g + ncol], start=(c == 0), stop=(c == 7)),
                          r=["hT", f"wB{c}"], w=[pk], inc=(c == 7))
                kb.op("act", lambda e: e.activation(out=Pc[:, 512 * g:512 * g + ncol], in_=pp[:, 0:ncol], func=AF.Identity), r=[pk], w=[Pk, pk])
            return


        def lockstep_pairs(mS2, mS2k, mN, mNk):
            identf = T["identf"]
            for p in range(4):
                for h2 in range(2):
                    W = psW[h2]
                    kb.op("pe", lambda e: e.matmul(W[:, 0:256], lhsT=BKT[:, p, 0, :], rhs=ARz[:, h2, p].rearrange("p a c -> p (a c)"), start=True, stop=True),
                          r=["BKT", "ARz"], w=[f"psW{h2}"], inc=False)
                    kb.op("pe", lambda e: e.matmul(W[:, 256:512], lhsT=BKT[:, p, 1, :], rhs=ARz[:, h2, p].rearrange("p a c -> p (a c)"), start=True, stop=True),
                          r=["BKT", "ARz"], w=[f"psW{h2}"])
                    kb.op("pe", lambda e: e.matmul(psC[1][:, 256 + 128 * h2:256 + 128 * h2 + 128], lhsT=ARz[:, h2, p, 0, :], rhs=BKT[:, p, 0, :], start=True, stop=True),
                          r=["BKT", "ARz"], w=["psC1"])
                    kb.op("dve", lambda e: e.tensor_tensor(out=scp[p][:, h2, :].rearrange("p (a b) -> p a b", b=256), in0=W[:].rearrange("p (a b) -> p a b", b=256),
                                                            in1=mS2[:].unsqueeze(1).broadcast_to([128, 2, 256]), op=ALU.mult),
                          r=[f"psW{h2}", mS2k], w=[f"scp{p}"])
                cur = Pnp[p][0]
                kb.op("dve", lambda e: e.tensor_tensor(out=cur[:, :, 0, :], in0=psC[1][:, 256:512].rearrange("p (a b) -> p a b", b=128),
                                                        in1=mN[:].unsqueeze(1).broadcast_to([128, 2, 128]), op=ALU.mult), r=["psC1", mNk], w=[f"Pnp{p}_0"])
                kb.op("dve", lambda e: e.tensor_copy(out=cur[:, :, 1, :], in_=scp[p][:, :, 0:128]), r=[f"scp{p}"], w=[f"Pnp{p}_0"])
                kb.op("dve", lambda e: e.tensor_tensor(out=YTp[p][:], in0=scp[p][:, :, 0:128], in1=identf[:].unsqueeze(1).broadcast_to([128, 2, 128]), op=ALU.add),
                      r=[f"scp{p}", "identf"], w=[f"YTp{p}"])
            sqb = [(psW[0], "psW0"), (psW[1], "psW1"), (psC[0], "psC0"), (psP[1], "psP1")]
            ytb = [(psC[1], 0, "psC1"), (psC[1], 256, "psC1"), (psP[0], 0, "psP0"), (psP[0], 256, "psP0")]
            def yt_update(lv):
                for p in range(4):
                    cur, curk = Pnp[p][lv % 3], f"Pnp{p}_{lv % 3}"
                    YB, y0, YBk = ytb[p]
                    for h2 in range(2):
                        kb.op("pe", lambda e: e.matmul(YB[:, y0 + 128 * h2:y0 + 128 * h2 + 128], lhsT=cur[:, h2, 0, :], rhs=YTp[p][:, h2, :], start=True, stop=True),
                              r=[curk, f"YTp{p}"], w=[YBk], inc=(h2 == 1))
                for p in range(4):
                    YB, y0, YBk = ytb[p]
                    kb.op("dve", lambda e: e.tensor_tensor(out=YTp[p][:], in0=YTp[p][:], in1=YB[:, y0:y0 + 256].rearrange("p (a b) -> p a b", b=128), op=ALU.add),
                          r=[f"YTp{p}", YBk], w=[f"YTp{p}", YBk])

            for lv in range(1, 7):
                for p in range(4):
                    prv, prvk = Pnp[p][(lv - 1) % 3], f"Pnp{p}_{(lv - 1) % 3}"
                    SQ, SQk = sqb[p]
                    for h2 in range(2):
                        kb.op("pe", lambda e: e.matmul(SQ[:, 256 * h2:256 * h2 + 128], lhsT=prv[:, h2, 1, :], rhs=prv[:, h2, 0, :], start=True, stop=True),
                              r=[prvk], w=[SQk], inc=False)
                        kb.op("pe", lambda e: e.matmul(SQ[:, 256 * h2 + 128:256 * h2 + 256], lhsT=prv[:, h2, 0, :], rhs=prv[:, h2, 1, :], start=True, stop=True),
                              r=[prvk], w=[SQk], inc=(h2 == 1))
                for p in range(4):
                    cur, curk = Pnp[p][lv % 3], f"Pnp{p}_{lv % 3}"
                    SQ, SQk = sqb[p]
                    if p < 1:
                        kb.op("dve", lambda e: e.tensor_copy(out=cur[:].rearrange("p a b c -> p (a b c)"), in_=SQ[:]), r=[SQk], w=[curk])
                    else:
                        kb.op("act", lambda e: e.activation(out=cur[:].rearrange("p a b c -> p (a b c)"), in_=SQ[:], func=AF.Identity), r=[SQk], w=[curk])
                if lv >= 2:
                    yt_update(lv - 1)
            yt_update(6)
            for p in range(4):
                for h2 in range(2):
                    hd = 2 * p + h2
                    oc = psP[0][:, 128 * p + 64 * h2:128 * p + 64 * h2 + 64]
                    kb.op("pe", lambda e: e.matmul(oc, lhsT=ARz[:, h2, p, 0, :], rhs=STb[:, p, :], start=(hd == 0), stop=False, skip_group_check=True),
                          r=["ARz", "STb"], w=["psP0"], inc=False)
                    kb.op("pe", lambda e: e.matmul(oc, lhsT=scp[p][:, h2, 256:384], rhs=Bt["Vb"][:, 64 * hd:64 * hd + 64], start=False, stop=(hd == 7), skip_group_check=True),
                          r=[f"scp{p}", "Vb"], w=["psP0"], inc=(hd == 7))
            kb.op("dve", lambda e: e.tensor_copy(out=rhs0a[:].rearrange("p a b c -> p (a b c)"), in_=psP[0][:]), r=["psP0"], w=["rhs0a"])
            for p in range(4):
                for h2 in range(2):
                    hd = 2 * p + h2
                    kb.op("pe", lambda e: e.matmul(psP[1][:, 64 * hd:64 * hd + 64], lhsT=YTp[p][:, h2, :], rhs=rhs0a[:, p, h2, :], start=True, stop=True),
                          r=[f"YTp{p}", "rhs0a"], w=["psP1"], inc=(hd == 7))
            kb.op("dve", lambda e: e.tensor_copy(out=Uba[:].rearrange("p a b c -> p (a b c)"), in_=psP[1][:]), r=["psP1"], w=["Uba"])
            for p in range(4):
                for h2 in range(2):
                    hd = 2 * p + h2
                    oc = psO[:, 64 * hd:64 * hd + 64]
                    kb.op("pe", lambda e: e.matmul(oc, lhsT=ARz[:, h2, p, 1, :], rhs=STb[:, p, :], start=(hd == 0), stop=False, skip_group_check=True),
                          r=["ARz", "STb"], w=["psO"], inc=False)
                    kb.op("pe", lambda e: e.matmul(oc, lhsT=scp[p][:, h2, 128:256], rhs=Uba[:, p, h2, :], start=False, stop=False, skip_group_check=True),
                          r=[f"scp{p}", "Uba"], w=["psO"], inc=False)
                    kb.op("pe", lambda e: e.matmul(oc, lhsT=scp[p][:, h2, 384:512], rhs=Bt["Vb"][:, 64 * hd:64 * hd + 64], start=False, stop=(hd == 7), skip_group_check=True),
                          r=[f"scp{p}", "Vb"], w=["psO"], inc=(hd == 7))
            for p in range(4):
                sc_ = psC[0][:, 128 * p:128 * p + 128]
                kb.op("pe", lambda e: e.matmul(sc_, lhsT=Bt["Bb"][:, 128 * p:128 * p + 128], rhs=Uba[:, p].rearrange("p a b -> p (a b)"), start=(p == 0), stop=False, skip_group_check=True),
                      r=["Bb", "Uba"], w=["psC0"], inc=False)
                kb.op("pe", lambda e: e.matmul(sc_, lhsT=Bt["Kb"][:, 128 * p:128 * p + 128], rhs=Bt["Vb"][:, 128 * p:128 * p + 128], start=False, stop=(p == 3), skip_group_check=True),
                      r=["Kb", "Vb"], w=["psC0"], inc=(p == 3))
            for h2 in range(2):
                rows = slice(64 * h2, 64 * h2 + 64)
                kb.op("dve", lambda e: e.tensor_tensor(out=ST[rows, :, :], in0=ST[rows, :, :],
                                                        in1=psC[0][rows, :].rearrange("p (a b) -> p a b", b=128)[:, :, 64 * h2:64 * h2 + 64], op=ALU.add),
                      r=["ST", "psC0"], w=["ST", "psC0"])
                kb.op("dve", lambda e: e.tensor_tensor(out=ST[rows, :, :], in0=ST[rows, :, :], in1=Wend[rows, 0:4].unsqueeze(2).broadcast_to([64, 4, 64]), op=ALU.mult),
                      r=["ST", "Wend"], w=["ST"])
            kb.op("dve", lambda e: e.tensor_copy(out=STb[:], in_=ST[:]), r=["ST"], w=["STb"])

        def mix_and_scan(ti, Pc, Pk, sample, mid_hook=None):
            if cfg.get("bcut", 99) < 1:
                return
            for pme, c0, c1, pk_ in (("pool", 1280, 2176, "pmB"), ("dve", 0, 1280, "pmA")):
                kb.op(pme, lambda e: e.tensor_tensor(out=pm[:, c0:c1], in0=Pprev[:, c0:c1], in1=Pc[:, c0:c1], op=ALU.subtract), r=["Pprev", Pk], w=[pk_])
                kb.op(pme, lambda e: e.tensor_tensor(out=pm[:, c0:c1], in0=pm[:, c0:c1], in1=mub[:, c0:c1], op=ALU.mult), r=[pk_, "mub"], w=[pk_])
                kb.op(pme, lambda e: e.tensor_tensor(out=pm[:, c0:c1], in0=pm[:, c0:c1], in1=Pc[:, c0:c1], op=ALU.add), r=[pk_, Pk], w=[pk_])
            r_, k_, v_, z_ = pm[:, 0:512], pm[:, 512:1024], pm[:, 1024:1536], pm[:, 1536:2048]
            kb.op("act", lambda e: e.activation(out=ltf[:, 0:64], in_=pm[:, 2048:2112], func=AF.Exp, scale=-2.0), r=["pmA", "pmB"], w=["ltf"])
            kb.op("dve", lambda e: e.tensor_scalar_add(out=ltf[:, 0:64], in0=ltf[:, 0:64], scalar1=1.0), r=["ltf"], w=["ltf"])
            kb.op("dve", lambda e: e.reciprocal(out=ltf[:, 0:64], in_=ltf[:, 0:64]), r=["ltf"], w=["ltf"])
            kb.op("dve", lambda e: e.tensor_scalar(out=ltb[:, 0:64], in0=ltf[:, 0:64], scalar1=2.0, scalar2=-1.0, op0=ALU.mult, op1=ALU.add), r=["ltf"], w=["ltb"])
            kb.op("dve", lambda e: e.tensor_copy(out=ltb[:, 64:128], in_=pm[:, 2112:2176]), r=["pmA", "pmB"], w=["ltb"])
            kb.op("pe", lambda e: e.transpose(out=psT[:, 0, :], in_=ltb[:], identity=identb[:]), r=["ltb", "identb"], w=["psT"])
            kb.op("dve", lambda e: e.tensor_copy(out=ltT[:], in_=psT[:, 0, :]), r=["psT"], w=["ltT"])
            kb.op("pe", lambda e: e.matmul(psP[0][:], lhsT=ltT[:], rhs=w2z[:, 0, :], start=True, stop=True), r=["ltT", "w2z"], w=["psP0"])
            kb.op("pe", lambda e: e.matmul(psP[1][:], lhsT=ltT[:], rhs=w2z[:, 1, :], start=True, stop=True), r=["ltT", "w2z"], w=["psP1"])
            if cfg.get("bcut", 99) < 2:
                return
            ew, Wt, Winv, Wprev, a_, kk, kmod, t1, t2, zsb, ob, cen = (F[n][:] for n in ("ew", "Wt", "Winv", "Wprev", "a", "kk", "kmod", "t1", "t2", "zsb", "ob", "cen"))
            kb.op("dve", lambda e: e.tensor_tensor(out=t1, in0=psP[0][:], in1=PB["w0"][:], op=ALU.add), r=["psP0", "p_w0"], w=["t1"])
            kb.op("act", lambda e: e.activation(out=t1, in_=t1, func=AF.Exp, scale=-1.0), r=["t1"], w=["t1"])
            kb.op("act", lambda e: e.activation(out=t1, in_=t1, func=AF.Ln, bias=1.0), r=["t1"], w=["t1"])
            kb.op("act", lambda e: e.activation(out=ew, in_=t1, func=AF.Exp, scale=-1.0, bias=-0.5), r=["t1"], w=["ew"])
            kb.op("dve", lambda e: e.tensor_tensor(out=t2, in0=psP[1][:], in1=PB["a0"][:], op=ALU.add), r=["psP1", "p_a0"], w=["t2"])
            sigmoid_inplace(a_, "a", t2, "t2")
            if cfg.get("bcut", 99) < 3:
                return
            tri = T["triFblk"] if sample else T["triF"]
            trik = "triFblk" if sample else "triF"
            kb.op("pe", lambda e: e.matmul(psP[0][:], lhsT=tri[:], rhs=ew, start=True, stop=True), r=[trik, "ew"], w=["psP0"])
            kb.op("act", lambda e: e.activation(out=Wt, in_=psP[0][:], func=AF.Exp, scale=-1.0), r=["psP0"], w=["Wt", "psP0"])
            kb.op("act", lambda e: e.activation(out=Winv, in_=psP[0][:], func=AF.Exp), r=["psP0"], w=["Winv", "psP0"])
            kb.op("dve", lambda e: e.tensor_tensor(out=t1, in0=psP[0][:], in1=ew, op=ALU.subtract), r=["psP0", "ew"], w=["t1", "psP0"])
            kb.op("act", lambda e: e.activation(out=Wprev, in_=t1, func=AF.Exp, scale=-1.0), r=["t1"], w=["Wprev"])
            nb = 8 if sample else 1
            for p in range(4):
                kb.op("pe", lambda e: e.matmul(psP[1][:, nb * p:nb * p + nb], lhsT=F["ew"][:, 128 * p:128 * p + 128],
                                               rhs=(T["rowmF"][:, 0:8] if sample else T["onesF"][:, 0:1]), start=True, stop=True),
                      r=["ew", "onesF", "rowmF"], w=["psP1"], inc=(p == 3))
            kb.op("act", lambda e: e.activation(out=(WendS[:].rearrange("p a b -> p (a b)") if sample else Wend[:, 0:4]),
                                                in_=psP[1][:, 0:4 * nb], func=AF.Exp, scale=-1.0), r=["psP1"], w=["Wend", "psP1"])
            if cfg.get("bcut", 99) < 4:
                return
            kb.op("dve", lambda e: e.tensor_tensor(out=kk, in0=k_, in1=PB["k_k"][:], op=ALU.mult), r=["pmA", "pmB", "p_k_k"], w=["kk"])
            kb.op("dve", lambda e: e.tensor_tensor(out=t1, in0=kk, in1=kk, op=ALU.mult), r=["kk"], w=["t1"])
            kb.op("dve", lambda e: e.tensor_reduce(out=n8[:], in_=v3(t1), axis=AX.X, op=ALU.add), r=["t1"], w=["n8"])
            kb.op("dve", lambda e: e.tensor_scalar_max(out=n8[:], in0=n8[:], scalar1=1e-24), r=["n8"], w=["n8"])
            kb.op("act", lambda e: e.activation(out=n8[:], in_=n8[:], func=AF.Ln), r=["n8"], w=["n8"])
            kb.op("act", lambda e: e.activation(out=n8[:], in_=n8[:], func=AF.Exp, scale=-0.5), r=["n8"], w=["n8"])
            if mid_hook is not None:
                mid_hook()
            kb.op("dve", lambda e: e.tensor_tensor(out=v3(kk), in0=v3(kk), in1=bc8(n8[:]), op=ALU.mult), r=["kk", "n8"], w=["kk"])
            kb.op("dve", lambda e: e.scalar_tensor_tensor(out=kmod, in0=a_, scalar=-1.0, in1=PB["k_a"][:], op0=ALU.add, op1=ALU.mult), r=["a", "p_k_a"], w=["kmod"])
            kb.op("dve", lambda e: e.scalar_tensor_tensor(out=kmod, in0=kmod, scalar=1.0, in1=k_, op0=ALU.add, op1=ALU.mult), r=["kmod", "pmA", "pmB"], w=["kmod"])
            kb.op("dve", lambda e: e.scalar_tensor_tensor(out=Bt["Ab"][:], in0=kk, scalar=-1.0, in1=Wprev, op0=ALU.mult, op1=ALU.mult), r=["kk", "Wprev"], w=["Ab"])
            kb.op("dve", lambda e: e.tensor_tensor(out=Bt["Rb"][:], in0=r_, in1=Wt, op=ALU.mult), r=["pmA", "pmB", "Wt"], w=["Rb"])
            kb.op("dve", lambda e: e.tensor_tensor(out=t1, in0=kk, in1=a_, op=ALU.mult), r=["kk", "a"], w=["t1"])
            kb.op("dve", lambda e: e.tensor_tensor(out=Bt["Bb"][:], in0=t1, in1=Winv, op=ALU.mult), r=["t1", "Winv"], w=["Bb"])
            kb.op("dve", lambda e: e.tensor_tensor(out=Bt["Kb"][:], in0=kmod, in1=Winv, op=ALU.mult), r=["kmod", "Winv"], w=["Kb"])
            kb.op("dve", lambda e: e.tensor_copy(out=Bt["Vb"][:], in_=v_), r=["pmA", "pmB"], w=["Vb"])
            kb.op("pool", lambda e: e.tensor_tensor(out=t2, in0=r_, in1=kmod, op=ALU.mult), r=["pmA", "pmB", "kmod"], w=["t2"])
            kb.op("pool", lambda e: e.tensor_tensor(out=t2, in0=t2, in1=PB["r_k"][:], op=ALU.mult), r=["t2", "p_r_k"], w=["t2"])
            kb.op("dve", lambda e: e.tensor_reduce(out=bn8[:], in_=v3(t2), axis=AX.X, op=ALU.add), r=["t2"], w=["bn8"])
            sigmoid_inplace(zsb, "zsb", z_, "pmB")
            kb.op("pool", lambda e: e.tensor_tensor(out=zsb, in0=zsb, in1=z_, op=ALU.mult), r=["zsb", "pmA", "pmB"], w=["zsb"])
            if cfg.get("bcut", 99) < 5:
                return
            for p in range(4):
                for a, nm in enumerate(("Ab", "Rb")):
                    kb.op("pe", lambda e: e.transpose(out=psT[:, 2 * p + a, :], in_=Bt[nm][:, 128 * p:128 * p + 128], identity=identb[:]),
                          r=[nm, "identb"], w=["psT"], inc=(p == 3 and a == 1))
            kb.op("dve", lambda e: e.tensor_copy(out=ARz[0:64, 0].rearrange("p a b c -> p (a b) c"), in_=psT[0:64, :, :]), r=["psT"], w=["ARz"])
            kb.op("dve", lambda e: e.tensor_copy(out=ARz[64:128, 1].rearrange("p a b c -> p (a b) c"), in_=psT[64:128, :, :]), r=["psT"], w=["ARz"])
            for p in range(4):
                for a, nm in enumerate(("Bb", "Kb")):
                    kb.op("pe", lambda e: e.transpose(out=psT[:, 2 * p + a, :], in_=Bt[nm][:, 128 * p:128 * p + 128], identity=identb[:]),
                          r=[nm, "identb"], w=["psT"], inc=(p == 3 and a == 1))
            kb.op("dve", lambda e: e.tensor_copy(out=BKT[:].rearrange("p a b c -> p (a b) c"), in_=psT[:]), r=["psT"], w=["BKT"])
            if cfg.get("bcut", 99) < 6:
                return
            mS2 = T["mS2blk"] if sample else T["mS2"]
            mN = T["mNblk"] if sample else T["mN"]
            mS2k, mNk = ("mS2blk", "mNblk") if sample else ("mS2", "mN")
            first_o = [True]
            if not sample:
                lockstep_pairs(mS2, mS2k, mN, mNk)
            for p in (range(4) if sample else ()):
                for h2 in range(2):
                    W = psW[h2]
                    kb.op("pe", lambda e: e.matmul(W[:, 0:256], lhsT=BKT[:, p, 0, :], rhs=ARz[:, h2, p].rearrange("p a c -> p (a c)"), start=True, stop=True),
                          r=["BKT", "ARz"], w=[f"psW{h2}"], inc=False)
                    kb.op("pe", lambda e: e.matmul(W[:, 256:512], lhsT=BKT[:, p, 1, :], rhs=ARz[:, h2, p].rearrange("p a c -> p (a c)"), start=True, stop=True),
                          r=["BKT", "ARz"], w=[f"psW{h2}"])
                    kb.op("pe", lambda e: e.matmul(psC[1][:, 256 + 128 * h2:256 + 128 * h2 + 128], lhsT=ARz[:, h2, p, 0, :], rhs=BKT[:, p, 0, :], start=True, stop=True),
                          r=["BKT", "ARz"], w=["psC1"])
                    kb.op("dve", lambda e: e.tensor_tensor(out=sc[:, h2, :].rearrange("p (a b) -> p a b", b=256), in0=W[:].rearrange("p (a b) -> p a b", b=256),
                                                            in1=mS2[:].unsqueeze(1).broadcast_to([128, 2, 256]), op=ALU.mult),
                          r=[f"psW{h2}", mS2k], w=[f"sc{h2}"])
                cur = Pn[0]
                kb.op("dve", lambda e: e.tensor_tensor(out=cur[:, :, 0, :], in0=psC[1][:, 256:512].rearrange("p (a b) -> p a b", b=128),
                                                        in1=mN[:].unsqueeze(1).broadcast_to([128, 2, 128]), op=ALU.mult), r=["psC1", mNk], w=["Pn0"])
                kb.op("dve", lambda e: e.tensor_copy(out=cur[:, :, 1, :], in_=sc[:, :, 0:128]), r=["sc0", "sc1"], w=["Pn0"])
                kb.op("dve", lambda e: e.tensor_tensor(out=YT[:], in0=sc[:, :, 0:128], in1=T["identf"][:].unsqueeze(1).broadcast_to([128, 2, 128]), op=ALU.add),
                      r=["sc0", "sc1", "identf"], w=["YT"])
                nlev = 4 if sample else 7
                for lv in range(1, nlev):
                    prv, prvk = Pn[(lv - 1) % 2], f"Pn{(lv - 1) % 2}"
                    cur, curk = Pn[lv % 2], f"Pn{lv % 2}"
                    for h2 in range(2):
                        kb.op("pe", lambda e: e.matmul(psC[0][:, 256 * h2:256 * h2 + 128], lhsT=prv[:, h2, 1, :], rhs=prv[:, h2, 0, :], start=True, stop=True),
                              r=[prvk], w=["psC0"], inc=False)
                        kb.op("pe", lambda e: e.matmul(psC[0][:, 256 * h2 + 128:256 * h2 + 256], lhsT=prv[:, h2, 0, :], rhs=prv[:, h2, 1, :], start=True, stop=True),
                              r=[prvk], w=["psC0"], inc=(h2 == 1))
                    kb.op("dve", lambda e: e.tensor_copy(out=cur[:].rearrange("p a b c -> p (a b c)"), in_=psC[0][:]), r=["psC0"], w=[curk])
                    for h2 in range(2):
                        kb.op("pe", lambda e: e.matmul(psC[1][:, 128 * h2:128 * h2 + 128], lhsT=cur[:, h2, 0, :], rhs=YT[:, h2, :], start=True, stop=True),
                              r=[curk, "YT"], w=["psC1"], inc=(h2 == 1))
                    kb.op("dve", lambda e: e.tensor_tensor(out=YT[:], in0=YT[:], in1=psC[1][:, 0:256].rearrange("p (a b) -> p a b", b=128), op=ALU.add),
                          r=["YT", "psC1"], w=["YT", "psC1"])
                if sample:
                    for h2 in range(2):
                        for a in range(2):
                            kb.op("pe", lambda e: e.matmul(psW[a][:], lhsT=ARz[:, h2, p, a, :], rhs=S0b[:, p].rearrange("p b v -> p (b v)"), start=True, stop=True),
                                  r=["ARz", "S0b"], w=[f"psW{a}"])
                            kb.op("dve", lambda e: e.tensor_tensor(out=tmpZ[:], in0=psW[a][:].rearrange("p (b v) -> p b v", v=64),
                                                                    in1=T["rowmF"][:].unsqueeze(2).broadcast_to([128, 8, 64]), op=ALU.mult),
                                  r=[f"psW{a}", "rowmF"], w=["tmpZ", f"psW{a}"])
                            kb.op("dve", lambda e: e.tensor_reduce(out=s0t[a][:, h2, :], in_=tmpZ[:].rearrange("p b v -> p v b"), axis=AX.X, op=ALU.add),
                                  r=["tmpZ"], w=[f"s0t{a}"])
                    for h2 in range(2):
                        hd = 2 * p + h2
                        kb.op("pe", lambda e: e.matmul(psX[:, 64 * h2:64 * h2 + 64], lhsT=sc[:, h2, 256:384], rhs=Bt["Vb"][:, 64 * hd:64 * hd + 64], start=(h2 == 0), stop=(h2 == 1), skip_group_check=True),
                              r=[f"sc{h2}", "Vb"], w=["psP0"], inc=(h2 == 1))
                    kb.op("dve", lambda e: e.tensor_tensor(out=rhs0[:].rearrange("p a b -> p (a b)"), in0=psX[:, 0:128], in1=s0t[0][:].rearrange("p a b -> p (a b)"), op=ALU.add),
                          r=["psP0", "s0t0"], w=["rhs0", "psP0"])
                for h2 in range(2):
                    hd = 2 * p + h2
                    if not sample:
                        kb.op("pe", lambda e: e.matmul(psX[:, 64 * h2:64 * h2 + 64], lhsT=ARz[:, h2, p, 0, :], rhs=STb[:, p, :], start=(h2 == 0), stop=False, skip_group_check=True),
                              r=["ARz", "STb"], w=["psP0"], inc=False)
                        kb.op("pe", lambda e: e.matmul(psX[:, 64 * h2:64 * h2 + 64], lhsT=sc[:, h2, 256:384], rhs=Bt["Vb"][:, 64 * hd:64 * hd + 64], start=False, stop=(h2 == 1), skip_group_check=True),
                              r=[f"sc{h2}", "Vb"], w=["psP0"], inc=(h2 == 1))
                if not sample:
                    kb.op("dve", lambda e: e.tensor_copy(out=rhs0[:].rearrange("p a b -> p (a b)"), in_=psX[:, 0:128]), r=["psP0"], w=["rhs0"])
                for h2 in range(2):
                    kb.op("pe", lambda e: e.matmul(psX[:, 128 + 64 * h2:128 + 64 * h2 + 64], lhsT=YT[:, h2, :], rhs=rhs0[:, h2, :], start=True, stop=True),
                          r=["YT", "rhs0"], w=["psP0"], inc=(h2 == 1))
                kb.op("dve", lambda e: e.tensor_copy(out=Ub[:].rearrange("p a b -> p (a b)"), in_=psX[:, 128:256]), r=["psP0"], w=["Ub"])
                for h2 in range(2):
                    hd = 2 * p + h2
                    oc = psO[:, 64 * hd:64 * hd + 64]
                    if not sample:
                        kb.op("pe", lambda e: e.matmul(oc, lhsT=ARz[:, h2, p, 1, :], rhs=STb[:, p, :], start=first_o[0], stop=False, skip_group_check=True),
                              r=["ARz", "STb"], w=["psO"], inc=False)
                        first_o[0] = False
                    kb.op("pe", lambda e: e.matmul(oc, lhsT=sc[:, h2, 128:256], rhs=Ub[:, h2, :], start=first_o[0], stop=False, skip_group_check=True),
                          r=[f"sc{h2}", "Ub"], w=["psO"], inc=False)
                    first_o[0] = False
                    kb.op("pe", lambda e: e.matmul(oc, lhsT=sc[:, h2, 384:512], rhs=Bt["Vb"][:, 64 * hd:64 * hd + 64], start=False, stop=(hd == 7), skip_group_check=True),
                          r=[f"sc{h2}", "Vb"], w=["psO"], inc=(h2 == 1))
                if sample:
                    kb.op("dve", lambda e: e.tensor_copy(out=s0R[:, 2 * p:2 * p + 2, :], in_=s0t[1][:]), r=["s0t1"], w=["s0R"])
                    for h2 in range(2):
                        hd = 2 * p + h2
                        rows = slice(64 * h2, 64 * h2 + 64)
                        kb.op("dve", lambda e: e.tensor_tensor(out=Uexp[:], in0=Ub[:, h2, :].unsqueeze(1).broadcast_to([128, 8, 64]),
                                                                in1=T["rowmF"][:].unsqueeze(2).broadcast_to([128, 8, 64]), op=ALU.mult), r=["Ub", "rowmF"], w=["Uexp"])
                        kb.op("dve", lambda e: e.tensor_tensor(out=Vexp[:], in0=Bt["Vb"][:, 64 * hd:64 * hd + 64].unsqueeze(1).broadcast_to([128, 8, 64]),
                                                                in1=T["rowmF"][:].unsqueeze(2).broadcast_to([128, 8, 64]), op=ALU.mult), r=["Vb", "rowmF"], w=["Vexp"])
                        kb.op("pe", lambda e: e.matmul(psW[h2][:], lhsT=Bt["Bb"][:, 128 * p:128 * p + 128], rhs=Uexp[:].rearrange("p b v -> p (b v)"), start=True, stop=False),
                              r=["Bb", "Uexp"], w=[f"psW{h2}"], inc=False)
                        kb.op("pe", lambda e: e.matmul(psW[h2][:], lhsT=Bt["Kb"][:, 128 * p:128 * p + 128], rhs=Vexp[:].rearrange("p b v -> p (b v)"), start=False, stop=True),
                              r=["Kb", "Vexp"], w=[f"psW{h2}"])
                        kb.op("dve", lambda e: e.tensor_tensor(out=S0f[rows, p], in0=S0f[rows, p], in1=psW[h2][rows, :].rearrange("p (b v) -> p b v", v=64), op=ALU.add),
                              r=["S0f", f"psW{h2}"], w=["S0f", f"psW{h2}"])
                        kb.op("dve", lambda e: e.tensor_tensor(out=S0f[rows, p], in0=S0f[rows, p], in1=WendS[rows, p, :].unsqueeze(2).broadcast_to([64, 8, 64]), op=ALU.mult),
                              r=["S0f", "Wend"], w=["S0f"])
                    continue
                kb.op("pe", lambda e: e.matmul(psX[:, 256:384], lhsT=Bt["Bb"][:, 128 * p:128 * p + 128], rhs=Ub[:].rearrange("p a b -> p (a b)"), start=True, stop=False, skip_group_check=True),
                      r=["Bb", "Ub"], w=["psP0"], inc=False)
                kb.op("pe", lambda e: e.matmul(psX[:, 256:384], lhsT=Bt["Kb"][:, 128 * p:128 * p + 128], rhs=Bt["Vb"][:, 128 * p:128 * p + 128], start=False, stop=True, skip_group_check=True),
                      r=["Kb", "Vb"], w=["psP0"])
                for h2 in range(2):
                    rows = slice(64 * h2, 64 * h2 + 64)
                    kb.op("dve", lambda e: e.tensor_tensor(out=ST[rows, p, :], in0=ST[rows, p, :], in1=psX[rows, 256 + 64 * h2:256 + 64 * h2 + 64], op=ALU.add),
                          r=["ST", "psP0"], w=["ST", "psP0"])
                    kb.op("dve", lambda e: e.tensor_scalar_mul(out=ST[rows, p, :], in0=ST[rows, p, :], scalar1=Wend[rows, p:p + 1]),
                          r=["ST", "Wend"], w=["ST"])
                kb.op("dve", lambda e: e.tensor_copy(out=STb[:, p, :], in_=ST[:, p, :]), r=["ST"], w=["STb"])
            if cfg.get("bcut", 99) < 8:
                return
            if sample:
                kb.op("dve", lambda e: e.tensor_tensor(out=ob, in0=psO[:], in1=s0R[:].rearrange("p a b -> p (a b)"), op=ALU.add), r=["psO", "s0R"], w=["ob"])
            else:
                kb.op("dve", lambda e: e.tensor_copy(out=ob, in_=psO[:]), r=["psO"], w=["ob"])
            kb.op("dve", lambda e: e.tensor_reduce(out=m8[:], in_=v3(ob), axis=AX.X, op=ALU.add), r=["ob"], w=["m8"])
            kb.op("dve", lambda e: e.tensor_scalar_mul(out=m8[:], in0=m8[:], scalar1=1.0 / 64), r=["m8"], w=["m8"])
            kb.op("dve", lambda e: e.tensor_tensor(out=v3(cen), in0=v3(ob), in1=bc8(m8[:]), op=ALU.subtract), r=["ob", "m8"], w=["cen"])
            kb.op("dve", lambda e: e.tensor_tensor(out=t1, in0=cen, in1=cen, op=ALU.mult), r=["cen"], w=["t1"])
            kb.op("dve", lambda e: e.tensor_reduce(out=m8[:], in_=v3(t1), axis=AX.X, op=ALU.add), r=["t1"], w=["m8"])
            kb.op("dve", lambda e: e.tensor_scalar(out=m8[:], in0=m8[:], scalar1=1.0 / 64, scalar2=LNX_EPS, op0=ALU.mult, op1=ALU.add), r=["m8"], w=["m8"])
            kb.op("act", lambda e: e.activation(out=m8[:], in_=m8[:], func=AF.Ln), r=["m8"], w=["m8"])
            kb.op("act", lambda e: e.activation(out=m8[:], in_=m8[:], func=AF.Exp, scale=-0.5), r=["m8"], w=["m8"])
            kb.op("dve", lambda e: e.tensor_tensor(out=v3(cen), in0=v3(cen), in1=bc8(m8[:]), op=ALU.mult), r=["cen", "m8"], w=["cen"])
            kb.op("dve", lambda e: e.tensor_tensor(out=cen, in0=cen, in1=PB["lnx_g"][:], op=ALU.mult), r=["cen", "p_lnx_g"], w=["cen"])
            kb.op("dve", lambda e: e.tensor_tensor(out=cen, in0=cen, in1=PB["lnx_b"][:], op=ALU.add), r=["cen", "p_lnx_b"], w=["cen"])
            kb.op("pool", lambda e: e.tensor_tensor(out=v3(t2), in0=v3(v_), in1=bc8(bn8[:]), op=ALU.mult), r=["pmA", "pmB", "bn8"], w=["t2"])
            kb.op("dve", lambda e: e.tensor_tensor(out=cen, in0=cen, in1=t2, op=ALU.add), r=["cen", "t2"], w=["cen"])
            kb.op("dve", lambda e: e.tensor_tensor(out=Bt["ub"][:], in0=cen, in1=zsb, op=ALU.mult), r=["cen", "zsb"], w=["ub"])
            for p in range(4):
                kb.op("pe", lambda e: e.transpose(out=psT[:, p, :], in_=Bt["ub"][:, 128 * p:128 * p + 128], identity=identb[:]),
                      r=["ub", "identb"], w=["psT"], inc=(p == 3))
            kb.op("dve", lambda e: e.tensor_copy(out=ubT[:], in_=psT[:, 0:4, :]), r=["psT"], w=["ubT"])
            kb.dma("sp", D["xab"][ti // 6][512:1024, :].rearrange("(p c) t -> c p t", c=128)[:, :, 128 * (ti % 6):128 * (ti % 6) + 128], ubT[:], r=["ubT"], w=[f"xab{ti // 6}"])
            if (ti % 6 == 5 or ti == 32) and "gather" in cfg:
                cfg["gather"](ti // 6)

        if NTP > 0:
            kb.dma("sp", xt[0][:], D["xc"][0:128, :], w=["xt0"])
        def front(j):
            Pc, Pk = P[j % 2], f"P{j % 2}"
            if j + 1 < NTP:
                kb.dma("sp", xt[(j + 1) % 2][:], D["xc"][128 * (j + 1):128 * (j + 2), :], w=[f"xt{(j + 1) % 2}"])
            tile_b(j, xt[j % 2], f"xt{j % 2}", Pc, Pk, False)
            if j == 0:
                kb.op("dve", lambda e: e.memset(Pprev[0:1, :], 0.0), w=["Pprev"])
            else:
                kb.dma("sp", Pprev[0:1, :], P[(j - 1) % 2][127:128, :], r=[f"P{(j - 1) % 2}"], w=["Pprev"])
            kb.dma("sp", Pprev[1:128, :], Pc[0:127, :], r=[Pk], w=["Pprev"])
            if j == NTP - 1:
                kb.dma("sp", D[f"sho{hh}"], Pc[127:128, :], r=[Pk])

        if NTP > 0:
            front(0)
        for i in range(NTP):
            mix_and_scan(i, P[i % 2], f"P{i % 2}", False, mid_hook=((lambda i=i: front(i + 1)) if i + 1 < NTP else None))
        if NTP > 0 and cfg.get("bcut", 99) >= 9:
            psTf = psW[0]
            for p in range(4):
                kb.op("pe", lambda e: e.matmul(psTf[0:64, 128 * p:128 * p + 128], lhsT=ST[:, p, :], rhs=T["identf"][:], start=True, stop=True),
                      r=["ST", "identf"], w=["psW0"], inc=(p == 3))
            stT = F["ob"][0:64, :]
            kb.op("dve", lambda e: e.tensor_copy(out=stT, in_=psTf[0:64, :]), r=["psW0"], w=["ob"])
            kb.dma("sp", D[f"wkvo{hh}"].rearrange("(p h) v k -> v p h k", h=2), stT.rearrange("v (p h k) -> v p h k", p=4, k=64), r=["ob"])
        _drain(kb)
        if cfg.get("sample", True):
            for nm, shp in (("triFblk", [128, 128]), ("mS2blk", [128, 256]), ("mNblk", [128, 128]), ("rowmF", [128, 8])):
                T[nm] = sb("c_" + nm, shp, F32)
                kb.dma("sp", T[nm][:], D[nm], w=[nm])
            WendS = sb("WendS", [128, 4, 8], F32)
            tmpZ = sb("tmpZ", [128, 8, 64], F32)
            s0t = [sb("s0t0", [128, 2, 64], F32), sb("s0t1", [128, 2, 64], F32)]
            s0R = sb("s0R", [128, 8, 64], F32)
            Uexp = sb("Uexp", [128, 8, 64], BF16)
            Vexp = sb("Vexp", [128, 8, 64], BF16)
            S0f = sb("S0f", [128, 4, 8, 64], F32)
            S0b = sb("S0b", [128, 4, 8, 64], BF16)
            with nc.sbuf_tensor(f"b{hh}_S0v", [64, 2, 4, 128], F32) as S0v:
                for bp in range(4):
                    for bb in range(2):
                        kb.dma("sp", S0v[:, bb].rearrange("v p (h k) -> v (p h) k", k=64), D[f"s0{hh}"][2 * bp + bb].rearrange("h v k -> v h k"), w=["S0v"])
                    for p in range(4):
                        W, Wk = psW[p % 2], f"psW{p % 2}"
                        for bb in range(2):
                            kb.op("pe", lambda e: e.matmul(W[:, 64 * bb:64 * bb + 64], lhsT=S0v[:, bb, p, :], rhs=T["identf"][0:64, 0:64], start=True, stop=True),
                                  r=["S0v", "identf"], w=[Wk], inc=(bb == 1))
                        kb.op("dve", lambda e: e.tensor_copy(out=S0f[:, p, 2 * bp:2 * bp + 2, :].rearrange("p b v -> p (b v)"), in_=W[:, 0:128]),
                              r=[Wk], w=["S0f"])
                kb.op("dve", lambda e: e.tensor_copy(out=S0b[:], in_=S0f[:]), r=["S0f"], w=["S0b"])
                _drain(kb)
            x, xk = xt[0], "xt0"
            Pc, Pk = P[0], "P0"
            kb.dma("sp", x[:], D["xc"][4096:4224, :], w=[xk])
            tile_b(32, x, xk, Pc, Pk, True)
            kb.dma("sp", Pprev[1:128, :], Pc[0:127, :], r=[Pk], w=["Pprev"])
            for b in range(8):
                kb.dma("sp", Pprev[16 * b:16 * b + 1, :], D[f"shs{hh}"][b:b + 1, :], w=["Pprev"])
                kb.dma("sp", D[f"shso{hh}"][b:b + 1, :], Pc[16 * b + 15:16 * b + 16, :], r=[Pk])
            mix_and_scan(32, Pc, Pk, True)
            for p in range(4):
                for half in range(2):
                    W = psW[half]
                    for bb in range(4):
                        b = 4 * half + bb
                        kb.op("pe", lambda e: e.matmul(W[0:64, 128 * bb:128 * bb + 128], lhsT=S0f[:, p, b, :], rhs=T["identf"][:], start=True, stop=True),
                              r=["S0f", "identf"], w=[f"psW{half}"], inc=(bb == 3))
                    kb.op("dve", lambda e: e.tensor_copy(out=pm[0:64, 512 * half:512 * half + 512], in_=W[0:64, :]),
                          r=[f"psW{half}"], w=["pmA", "pmB"])
                for b in range(8):
                    kb.dma("sp", D[f"wkvso{hh}"][b, 2 * p:2 * p + 2].rearrange("h v k -> v h k"), pm[0:64, 128 * b:128 * b + 128].rearrange("v (h k) -> v h k", k=64), r=["pmA", "pmB"])
            _drain(kb)


def pass_c(nc, kb, D, cfg):
    tiles = cfg["ctiles"]
    with ExitStack() as es:
        def sb(name, shape, dt):
            return es.enter_context(nc.sbuf_tensor("c_" + name, shape, dt))

        def ps(name, shape, dt):
            return es.enter_context(nc.psum_tensor("c_" + name, shape, dt))

        T = {}
        T["identb"] = sb("identb", [128, 128], BF16)
        kb.dma("sp", T["identb"][:], D["identb"], w=["identb"])
        wGb = sb("wGb", [128, 8, 2048], BF16)
        _load_w_bf16(nc, kb, D["wG"], wGb, 2048, "wG", "c")
        WO = {}
        for nm in ("woa", "wob", "wout"):
            WO[nm] = sb(nm, [128, 8, 1024], BF16)
            _load_w_bf16(nc, kb, D[nm], WO[nm], 1024, nm, "c")
        T["gnb"] = sb("gnb", [128, 1024], F32)
        kb.dma("sp", T["gnb"][:], D["norm_g"].partition_broadcast(128), w=["gnb"])
        fnb = sb("fnb", [128, 1024], F32)
        kb.dma("sp", fnb[:], D["fnorm_g"].partition_broadcast(128), w=["fnb"])
        GS = 4
        xt = [sb(f"xt{q}", [128, 1024], F32) for q in range(GS)]
        T["ss"] = sb("ss", [128, 2], F32)
        T["junk"] = sb("junk", [128, 1024], BF16)
        T["hb"] = sb("hb", [128, 1024], BF16)
        T["hT"] = sb("hT", [128, 8, 128], BF16)
        T["psT"] = ps("psT", [128, 8, 128], BF16)
        hT4 = sb("hT4", [128, 8, GS * 128], BF16)
        gT = sb("gT", [128, 16, GS * 128], F32)
        uaT = sb("uaT", [128, 8, GS * 128], BF16)
        ubT = sb("ubT", [128, 8, GS * 128], BF16)
        tmp = sb("tmp", [128, GS * 128], F32)
        mT = sb("mT", [128, 8, GS * 128], BF16)
        xo = sb("xo", [128, 1024], F32)
        yo = sb("yo", [128, 1024], F32)
        psG = [ps("psG0", [128, 512], F32), ps("psG1", [128, 512], F32)]
        psA = ps("psA", [128, 512], F32)
        psB = ps("psB", [128, 512], F32)
        psY = [ps("psY0", [128, 512], F32), ps("psY1", [128, 512], F32)]
        groups = [tiles[a:a + GS] for a in range(0, len(tiles), GS)]
        for gi, grp in enumerate(groups):
            n = len(grp)
            W = 128 * n
            for q, (r0, o0) in enumerate(grp):
                kb.dma("sp", xt[q][:], D["xc"][r0:r0 + 128, :], w=[f"xt{q}"])
                tix = r0 // 128
                ck, c0 = tix // 6, 128 * (tix % 6)
                for rk in range(2):
                    kb.dma("sp", uaT[:, 4 * rk:4 * rk + 4, 128 * q:128 * q + 128],
                           D["gath"][ck][1024 * rk:1024 * rk + 512, :].rearrange("(k c) t -> c k t", c=128)[:, :, c0:c0 + 128], r=[f"gath{ck}"], w=["uaT"])
                    kb.dma("sp", ubT[:, 4 * rk:4 * rk + 4, 128 * q:128 * q + 128],
                           D["gath"][ck][1024 * rk + 512:1024 * rk + 1024, :].rearrange("(k c) t -> c k t", c=128)[:, :, c0:c0 + 128], r=[f"gath{ck}"], w=["ubT"])
            for q in range(n):
                _norm_hT(nc, kb, T, xt[q], f"xt{q}", dst=hT4[:, :, 128 * q:128 * q + 128], dstk="hT4")
            for dc in range(16):
                G, Gk = psG[dc % 2], f"psG{dc % 2}"
                for c in range(8):
                    kb.op("pe", lambda e: e.matmul(G[:, 0:W], lhsT=wGb[:, c, 128 * dc:128 * dc + 128], rhs=hT4[:, c, 0:W], start=(c == 0), stop=(c == 7)),
                          r=["hT4", f"wG{c}"], w=[Gk], inc=(c == 7))
                gq = gT[:, dc, 0:W]
                kb.op("act", lambda e: e.activation(out=gq, in_=G[:, 0:W], func=AF.Exp, scale=-1.0), r=[Gk], w=[f"gT{dc}"])
                kb.op("act", lambda e: e.activation(out=gq, in_=gq, func=AF.Ln, bias=1.0), r=[f"gT{dc}"], w=[f"gT{dc}"])
                kb.op("act", lambda e: e.activation(out=gq, in_=gq, func=AF.Exp, scale=-1.0), r=[f"gT{dc}"], w=[f"gT{dc}"])
            for j in range(8):
                for (PS, PSk, wnm, uT, uk) in ((psA, "psA", "woa", uaT, "uaT"), (psB, "psB", "wob", ubT, "ubT")):
                    for k in range(8):
                        kb.op("pe", lambda e: e.matmul(PS[:, 0:W], lhsT=WO[wnm][:, k, 128 * j:128 * j + 128], rhs=uT[:, k, 0:W], start=(k == 0), stop=(k == 7)),
                              r=[uk, f"{wnm}{k}"], w=[PSk], inc=(k == 7))
                ga, gb = gT[:, j, 0:W], gT[:, 8 + j, 0:W]
                kb.op("dve", lambda e: e.tensor_tensor(out=tmp[:, 0:W], in0=ga, in1=psA[:, 0:W], op=ALU.mult), r=[f"gT{j}", "psA"], w=["tmp"])
                kb.op("dve", lambda e: e.tensor_tensor(out=ga, in0=gb, in1=psB[:, 0:W], op=ALU.mult), r=[f"gT{8 + j}", "psB"], w=[f"gT{j}"])
                kb.op("dve", lambda e: e.tensor_tensor(out=mT[:, j, 0:W], in0=tmp[:, 0:W], in1=ga, op=ALU.add), r=["tmp", f"gT{j}"], w=[f"mT{j}"])
            for q, (r0, o0) in enumerate(grp):
                x, xk = xt[q], f"xt{q}"
                for eg in range(2):
                    Y, Yk = psY[eg], f"psY{eg}"
                    for j in range(8):
                        kb.op("pe", lambda e: e.matmul(Y[:], lhsT=mT[:, j, 128 * q:128 * q + 128], rhs=WO["wout"][:, j, 512 * eg:512 * eg + 512], start=(j == 0), stop=(j == 7)),
                              r=[f"mT{j}", f"wout{j}"], w=[Yk], inc=(j == 7))
                    kb.op("dve", lambda e: e.tensor_tensor(out=xo[:, 512 * eg:512 * eg + 512], in0=x[:, 512 * eg:512 * eg + 512], in1=Y[:], op=ALU.add),
                          r=[xk, Yk], w=["xo"])
                ss = T["ss"]
                kb.op("dve", lambda e: e.memset(ss[:, 1:2], 0.0), w=["ss2"])
                kb.op("act", lambda e: e.activation(out=T["junk"][:], in_=xo[:], func=AF.Square, accum_out=ss[:, 1:2]), r=["xo"], w=["junk", "ss2"])
                kb.op("dve", lambda e: e.tensor_scalar(out=ss[:, 1:2], in0=ss[:, 1:2], scalar1=1.0 / 1024, scalar2=EPS, op0=ALU.mult, op1=ALU.add), r=["ss2"], w=["ss2"])
                kb.op("act", lambda e: e.activation(out=ss[:, 1:2], in_=ss[:, 1:2], func=AF.Ln), r=["ss2"], w=["ss2"])
                kb.op("act", lambda e: e.activation(out=ss[:, 1:2], in_=ss[:, 1:2], func=AF.Exp, scale=-0.5), r=["ss2"], w=["ss2"])
                kb.op("dve", lambda e: e.scalar_tensor_tensor(out=yo[:], in0=xo[:], scalar=ss[:, 1:2], in1=fnb[:], op0=ALU.mult, op1=ALU.mult),
                      r=["xo", "ss2", "fnb"], w=["yo"])
                kb.dma("sp", D["yo"][o0:o0 + 128, :], yo[:], r=["yo"])
        _drain(kb)


def _drain(kb):
    for en in ("pe", "act", "dve"):
        for other in ("pe", "act", "dve"):
            kb._wait(en, kb.E[other]["sem"], kb.E[other]["cnt"])
    for q in kb.Q:
        for s in kb.Q[q]["pool"]:
            for en in ("pe", "act", "dve", "sp", "pool"):
                kb._wait(en, s["sem"], 16 * s["cnt"])
    for en in ("sp", "pool"):
        for other in ("pe", "act", "dve"):
            kb._wait(en, kb.E[other]["sem"], kb.E[other]["cnt"])


def build(cfg):
    nc = bass.Bass("TRN2", target_bir_lowering=False)
    D = {}
    passes = cfg.get("passes", "ABGC")

    def din(name, shape, dt=F32):
        D[name] = nc.dram_tensor(name, shape, dt, kind="ExternalInput").ap()

    def dout(name, shape, dt=F32):
        D[name] = nc.dram_tensor(name, shape, dt, kind="ExternalOutput").ap()

    din("xc", [4224, 1024])
    din("norm_g", [1024])
    din("fnorm_g", [1024])
    for nm, shp, dt in CONST_SPECS:
        din(nm, shp, dt)
    hh = 0
    din(f"wA{hh}", [1024, 2048])
    din(f"wB{hh}", [1024, 2176])
    din(f"muB{hh}", [2176])
    for nm in ("w0", "a0", "k_k", "k_a", "r_k", "lnx_g", "lnx_b"):
        din(f"{nm}{hh}", [512])
    din(f"w2a2z{hh}", [128, 2, 512])
    dout(f"ko{hh}", [8, 4096, 64])
    dout(f"vo{hh}", [8, 4096, 64])
    dout(f"kso{hh}", [8, 8, 16, 64])
    dout(f"vso{hh}", [8, 8, 16, 64])
    dout(f"sho{hh}", [1, 2176])
    dout(f"wkvo{hh}", [8, 64, 64])
    din(f"s0{hh}", [8, 8, 64, 64])
    din(f"kc{hh}", [8, 8, 1024, 64])
    din(f"vc{hh}", [8, 8, 1024, 64])
    din(f"shs{hh}", [8, 2176])
    dout(f"shso{hh}", [8, 2176])
    dout(f"wkvso{hh}", [8, 8, 64, 64])
    din("wG", [1024, 2048])
    for nm in ("woa", "wob", "wout"):
        din(nm, [1024, 1024])
    dout("yo", [128 * len(cfg["ctiles"]), 1024])
    D["xab"] = [nc.dram_tensor(f"xab{k}", [1024, 768], BF16).ap() for k in range(6)]
    D["gath"] = [nc.dram_tensor(f"gath{k}", [2048, 768], BF16).ap() for k in range(6)]
    kb = KB(nc)

    def gather(k):
        kb._deps("pool", [f"xab{k}"], [f"gath{k}"])
        ccs = nc.alloc_semaphore(name=f"ccsem{k}")
        nc.gpsimd.collective_compute("AllGather", op=ALU.bypass, replica_groups=[[0, 1], [2, 3], [4, 5], [6, 7]],
                                     ins=[D["xab"][k].opt()], outs=[D["gath"][k].opt()]).then_inc(ccs)
        kb._record((ccs, 1), [f"xab{k}"], [f"gath{k}"])

    if "G" in passes:
        cfg = dict(cfg, gather=gather)
    if "A" in passes:
        pass_a(nc, kb, D, cfg, hh)
    if "B" in passes:
        pass_b(nc, kb, D, cfg, hh)
    if "C" in passes:
        pass_c(nc, kb, D, cfg)
    kb.finish()
    return nc


CONST_SPECS = [("identb", [128, 128], BF16), ("triI", [128, 128], BF16), ("onesb", [128, 128], BF16), ("mskS", [128, 128], F32),
               ("mskSblk", [128, 128], F32), ("identf", [128, 128], F32), ("triF", [128, 128], F32), ("mS2", [128, 256], F32),
               ("mN", [128, 128], F32), ("onesF", [128, 8], F32), ("triFblk", [128, 128], F32), ("mS2blk", [128, 256], F32),
               ("mNblk", [128, 128], F32), ("rowmF", [128, 8], F32)]


def host_consts():
    s = np.arange(128)[:, None]
    t = np.arange(128)[None, :]
    bf = ml_dtypes.bfloat16
    c = {}
    c["identb"] = np.eye(128).astype(bf)
    c["triI"] = (s >= t).astype(bf)
    c["onesb"] = np.ones((128, 128), bf)
    c["mskS"] = (s < t).astype(np.float32)
    c["mskSblk"] = ((s < t) & (s // 16 == t // 16)).astype(np.float32)
    c["identf"] = np.eye(128, dtype=np.float32)
    c["triF"] = (s <= t).astype(np.float32)
    c["mS2"] = np.concatenate([(s < t), (s <= t)], axis=1).astype(np.float32)
    c["mN"] = (t < s).astype(np.float32)
    c["onesF"] = np.ones((128, 8), np.float32)
    blk = (s // 16 == t // 16)
    c["triFblk"] = ((s <= t) & blk).astype(np.float32)
    c["mS2blk"] = np.concatenate([(s < t) & blk, (s <= t) & blk], axis=1).astype(np.float32)
    c["mNblk"] = ((t < s) & blk).astype(np.float32)
    c["rowmF"] = (np.arange(128)[:, None] // 16 == np.arange(8)[None, :]).astype(np.float32)
    return c


def core_inputs(inputs, c):
    p, hh = c // 2, c % 2
    w_in = inputs["w_in"][0]
    m = {}
    m["xc"] = np.ascontiguousarray(np.concatenate([inputs["x_prompt"][p], inputs["x_sample"][8 * p:8 * p + 8].reshape(128, 1024)], axis=0))
    wb = w_in[:, 6144:]
    own = slice(512 * hh, 512 * hh + 512)
    bsel = lambda a: np.concatenate([a[..., 0:1024][..., own], a[..., 1024:2048][..., own], a[..., 2048:3072][..., own],
                                     a[..., 3200:4224][..., own], a[..., 3072:3136], a[..., 3136:3200]], axis=-1)
    m["wA0"] = np.ascontiguousarray(np.concatenate([w_in[:, 1024 * g:1024 * g + 1024][:, own] for g in range(4)], axis=1))
    m["wB0"] = np.ascontiguousarray(bsel(wb))
    m["muB0"] = np.ascontiguousarray(bsel(inputs["mu_shift"][0]))
    for nm in ("w0", "a0", "k_k", "k_a", "lnx_g", "lnx_b"):
        m[f"{nm}0"] = np.ascontiguousarray(inputs[nm][0][own])
    m["r_k0"] = np.ascontiguousarray(inputs["r_k"][0].reshape(1024)[own])
    z = np.zeros((128, 2, 512), np.float32)
    z[0:64, 0] = inputs["w2"][0][:, own]
    z[64:128, 1] = inputs["a2"][0][:, own]
    m["w2a2z0"] = z
    m["kc0"] = np.ascontiguousarray(inputs["cache_sb_k"][0, 8 * p:8 * p + 8, 8 * hh:8 * hh + 8])
    m["vc0"] = np.ascontiguousarray(inputs["cache_sb_v"][0, 8 * p:8 * p + 8, 8 * hh:8 * hh + 8])
    m["s00"] = np.ascontiguousarray(inputs["state_wkv"][0, 8 * p:8 * p + 8, 8 * hh:8 * hh + 8])
    m["shs0"] = np.ascontiguousarray(bsel(inputs["state_shift"][0, 8 * p:8 * p + 8, 0]))
    m["wG"] = np.ascontiguousarray(w_in[:, 4096:6144])
    m["woa"] = np.ascontiguousarray(inputs["w_o_a"][0])
    m["wob"] = np.ascontiguousarray(inputs["w_o_b"][0])
    m["wout"] = np.ascontiguousarray(inputs["w_out"][0])
    m["norm_g"] = np.ascontiguousarray(inputs["norm_g"][0])
    m["fnorm_g"] = np.ascontiguousarray(inputs["final_norm_g"])
    m.update(host_consts())
    return m


ALL_TILES = [(128 * n, 128 * n) for n in range(33)]

_CACHE = {}


def _unsel(vec, hh, out):
    own = slice(512 * hh, 512 * hh + 512)
    out[..., 0:1024][..., own] = vec[..., 0:512]
    out[..., 1024:2048][..., own] = vec[..., 512:1024]
    out[..., 2048:3072][..., own] = vec[..., 1024:1536]
    out[..., 3200:4224][..., own] = vec[..., 1536:2048]
    out[..., 3072:3136] = vec[..., 2048:2112]
    out[..., 3136:3200] = vec[..., 2112:2176]


def kernel(**inputs):
    inputs = {k: np.asarray(v) for k, v in inputs.items()}
    cfg = dict(NTP=32, ctiles=ALL_TILES)
    if "nc" not in _CACHE:
        _CACHE["nc"] = build(cfg)
    nc = _CACHE["nc"]
    maps = [core_inputs(inputs, c) for c in range(8)]
    res = run_bass_kernel_spmd(nc, maps, core_ids=list(range(8))).results
    f32 = np.float32
    y_prompt = np.zeros((4, 4096, 1024), f32)
    y_sample = np.zeros((32, 16, 1024), f32)
    k_prompt = np.zeros((1, 4, 16, 4096, 64), f32)
    v_prompt = np.zeros((1, 4, 16, 4096, 64), f32)
    k_sample = np.zeros((1, 32, 16, 16, 64), f32)
    v_sample = np.zeros((1, 32, 16, 16, 64), f32)
    shift_prompt = np.zeros((1, 4, 1, 4224), f32)
    wkv_prompt = np.zeros((1, 4, 16, 64, 64), f32)
    shift_sample = np.zeros((1, 32, 1, 4224), f32)
    wkv_sample = np.zeros((1, 32, 16, 64, 64), f32)
    for c in range(8):
        p, hh = c // 2, c % 2
        r = res[c]
        hs = slice(8 * hh, 8 * hh + 8)
        if hh == 0:
            y_prompt[p] = r["yo"][0:4096]
            y_sample[8 * p:8 * p + 8] = r["yo"][4096:4224].reshape(8, 16, 1024)
        k_prompt[0, p, hs] = r["ko0"]
        v_prompt[0, p, hs] = r["vo0"]
        k_sample[0, 8 * p:8 * p + 8, hs] = r["kso0"]
        v_sample[0, 8 * p:8 * p + 8, hs] = r["vso0"]
        _unsel(r["sho0"][0], hh, shift_prompt[0, p, 0])
        wkv_prompt[0, p, hs] = r["wkvo0"]
        _unsel(r["shso0"], hh, shift_sample[0, 8 * p:8 * p + 8, 0])
        wkv_sample[0, 8 * p:8 * p + 8, hs] = r["wkvso0"]
    return (y_prompt, y_sample, k_prompt, v_prompt, shift_prompt, wkv_prompt,
            k_sample, v_sample, shift_sample, wkv_sample)
```

```python
from contextlib import ExitStack
import numpy as np
import ml_dtypes
import concourse.bass as bass
import concourse.mybir as mybir
from concourse.bass_utils import run_bass_kernel_spmd

F32 = mybir.dt.float32
BF16 = mybir.dt.bfloat16
AF = mybir.ActivationFunctionType
ALU = mybir.AluOpType
AX = mybir.AxisListType
EPS = 1e-6
import os
NDUMMY = int(os.environ.get("NDUMMY", "8"))
LNX_EPS = 64e-5


class KB:
    def __init__(self, nc, ndma=16):
        self.nc = nc
        self.E = {}
        self.dummy = [nc.alloc_semaphore(name=f"dummy{i}") for i in range(NDUMMY)]
        for name, eng in (("pe", nc.tensor), ("act", nc.scalar), ("dve", nc.vector), ("pool", nc.gpsimd), ("sp", nc.sync)):
            self.E[name] = dict(eng=eng, sem=nc.alloc_semaphore(name="s_" + name), cnt=0, seen={})
        self.Q = {}
        for q in ("sp", "pool"):
            self.Q[q] = dict(pool=[dict(sem=nc.alloc_semaphore(name=f"d_{q}{i}"), cnt=0) for i in range(ndma)], pi=0)
        self.lw = {}
        self.rd = {}
        self.semobj = {}

    def _wait(self, en, sem, val):
        if val <= 0:
            return
        E = self.E[en]
        k = id(sem)
        self.semobj[k] = sem
        if E["seen"].get(k, 0) >= val:
            return
        E["eng"].wait_ge(sem, val)
        E["seen"][k] = val

    def _deps(self, en, r, w, skip_self=False):
        E = self.E[en]
        deps = {}

        def add(ev):
            if ev is None:
                return
            sem, val = ev
            if skip_self and sem is E["sem"]:
                return
            k = id(sem)
            self.semobj[k] = sem
            if deps.get(k, 0) < val:
                deps[k] = val

        for key in r:
            add(self.lw.get(key))
        for key in w:
            add(self.lw.get(key))
            for k, v in self.rd.get(key, {}).items():
                add((self.semobj[k], v))
        for k, v in deps.items():
            self._wait(en, self.semobj[k], v)

    def _record(self, ev, r, w):
        sem, val = ev
        k = id(sem)
        self.semobj[k] = sem
        for key in r:
            d = self.rd.setdefault(key, {})
            if d.get(k, 0) < val:
                d[k] = val
        for key in w:
            self.lw[key] = ev
            self.rd[key] = {}

    def op(self, en, fn, r=(), w=(), inc=True):
        E = self.E[en]
        self._deps(en, r, w, skip_self=(en == "pe"))
        ins = fn(E["eng"])
        if inc:
            ins.then_inc(E["sem"], 1)
            E["cnt"] += 1
            ev = (E["sem"], E["cnt"])
        else:
            ev = (E["sem"], E["cnt"] + 1)
        self._record(ev, r, w)
        return ins

    def dma(self, q, out, in_, r=(), w=(), **kw):
        Q = self.Q[q]
        E = self.E[q]
        self._deps(q, r, w)
        s = Q["pool"][Q["pi"]]
        Q["pi"] = (Q["pi"] + 1) % len(Q["pool"])
        self._wait(q, s["sem"], 16 * s["cnt"])
        E["eng"].dma_start(out=out, in_=in_, **kw).then_inc(s["sem"], 16)
        s["cnt"] += 1
        self._record((s["sem"], 16 * s["cnt"]), r, w)

    def finish(self):
        for q in self.Q:
            for s in self.Q[q]["pool"]:
                self._wait("sp", s["sem"], 16 * s["cnt"])


def _common(nc, kb, es, D, hh):
    C = {}

    def sb(name, shape, dt):
        return es.enter_context(nc.sbuf_tensor(name, shape, dt))

    for nm, dt in (("identb", BF16), ("triI", BF16), ("onesb", BF16), ("mskS", F32), ("mskSblk", F32)):
        C[nm] = sb(f"a{hh}c_" + nm, [128, 128], dt)
        kb.dma("sp", C[nm][:], D[nm], w=[nm])
    return C


def _load_w_bf16(nc, kb, wdram, wsb, ncols, key, pfx=""):
    NS = 4
    with ExitStack() as es:
        st = [es.enter_context(nc.sbuf_tensor(f"{pfx}wst{q}_{key}", [128, ncols], F32)) for q in range(NS)]
        for c in range(8):
            q = c % NS
            sk = f"wst{q}_{key}"
            kb.dma("sp", st[q][:], wdram[128 * c:128 * c + 128, :], w=[sk])
            kb.op("dve", lambda e: e.tensor_copy(out=wsb[:, c, :], in_=st[q][:]), r=[sk], w=[f"{key}{c}"])
        _drain(kb)


def _norm_hT(nc, kb, T, x, xk, dst=None, dstk="hT"):
    ss, junk, hb, hT, psT, gnb, identb = T["ss"], T["junk"], T["hb"], T["hT"], T["psT"], T["gnb"], T["identb"]
    kb.op("dve", lambda e: e.memset(ss[:, 0:1], 0.0), w=["ss"])
    kb.op("act", lambda e: e.activation(out=junk[:], in_=x[:], func=AF.Square, accum_out=ss[:, 0:1]), r=[xk], w=["junk", "ss"])
    kb.op("dve", lambda e: e.tensor_scalar(out=ss[:, 0:1], in0=ss[:, 0:1], scalar1=1.0 / 1024, scalar2=EPS, op0=ALU.mult, op1=ALU.add), r=["ss"], w=["ss"])
    kb.op("act", lambda e: e.activation(out=ss[:, 0:1], in_=ss[:, 0:1], func=AF.Ln), r=["ss"], w=["ss"])
    kb.op("act", lambda e: e.activation(out=ss[:, 0:1], in_=ss[:, 0:1], func=AF.Exp, scale=-0.5), r=["ss"], w=["ss"])
    kb.op("dve", lambda e: e.scalar_tensor_tensor(out=hb[:], in0=x[:], scalar=ss[:, 0:1], in1=gnb[:], op0=ALU.mult, op1=ALU.mult),
          r=[xk, "ss", "gnb"], w=["hb"])
    for c in range(8):
        kb.op("pe", lambda e: e.transpose(out=psT[:, c, :], in_=hb[:, 128 * c:128 * c + 128], identity=identb[:]),
              r=["hb", "identb"], w=["psT"], inc=(c == 7))
    if dst is None:
        kb.op("dve", lambda e: e.tensor_copy(out=hT[:], in_=psT[:]), r=["psT"], w=["hT"])
    else:
        kb.op("dve", lambda e: e.tensor_copy(out=dst, in_=psT[:]), r=["psT"], w=[dstk])


def _silu_from_psum(kb, pp, pk, et, zs, zk):
    kb.op("act", lambda e: e.activation(out=et[:], in_=pp[:], func=AF.Exp, scale=-1.0), r=[pk], w=["et"])
    kb.op("dve", lambda e: e.tensor_scalar_add(out=et[:], in0=et[:], scalar1=1.0), r=["et"], w=["et"])
    kb.op("dve", lambda e: e.reciprocal(out=et[:], in_=et[:]), r=["et"], w=["et"])
    kb.op("dve", lambda e: e.tensor_tensor(out=zs[:], in0=pp[:], in1=et[:], op=ALU.mult), r=[pk, "et"], w=[zk])


def pass_a(nc, kb, D, cfg, hh):
    NTP = cfg["NTP"]
    with ExitStack() as es:
        def sb(name, shape, dt):
            return es.enter_context(nc.sbuf_tensor(f"a{hh}_" + name, shape, dt))

        def ps(name, shape, dt):
            return es.enter_context(nc.psum_tensor(f"a{hh}_" + name, shape, dt))

        T = _common(nc, kb, es, D, hh)
        wAb = sb("wAb", [128, 8, 2048], BF16)
        if not cfg.get("skipw"):
            _load_w_bf16(nc, kb, D[f"wA{hh}"], wAb, 2048, "wA", f"a{hh}")
        T["gnb"] = sb("gnb", [128, 1024], F32)
        kb.dma("sp", T["gnb"][:], D["norm_g"].partition_broadcast(128), w=["gnb"])
        xt = [sb("xt0", [128, 1024], F32), sb("xt1", [128, 1024], F32)]
        T["ss"] = sb("ss", [128, 2], F32)
        T["junk"] = sb("junk", [128, 1024], BF16)
        T["hb"] = sb("hb", [128, 1024], BF16)
        T["hT"] = sb("hT", [128, 8, 128], BF16)
        qb = sb("qb", [128, 512], BF16)
        kf = sb("kf", [128, 512], F32)
        ksb = sb("ksb", [128, 512], BF16)
        vf = sb("vf", [128, 512], F32)
        zs = sb("zs", [128, 512], F32)
        et = sb("et", [128, 512], F32)
        qT = sb("qT", [128, 2, 4, 128], BF16)
        kb.op("dve", lambda e: e.memset(qT[:], 0.0), w=["qT"])
        ua = sb("ua", [128, 512], BF16)
        uaT = sb("uaT", [128, 4, 128], BF16)
        NBUF = 6
        eb = [sb(f"e{q}", [128, 512], F32) for q in range(NBUF)]
        spb = [sb(f"sp{q}", [128, 512], BF16) for q in range(NBUF)]
        eRb = [sb(f"eR{q}", [128, 512], F32) for q in range(3)]
        wTb = [sb(f"wT{q}", [128, 512], BF16) for q in range(3)]
        Acc = [sb(f"Acc{g}", [128, 512], BF16) for g in range(2)]
        T["psT"] = ps("psT", [128, 8, 128], BF16)
        psP = [ps("psP0", [128, 512], F32), ps("psP1", [128, 512], F32)]
        psS = [ps("psS0", [128, 512], F32), ps("psS1", [128, 512], F32)]
        psR = [ps("psR0", [128, 512], F32), ps("psR1", [128, 512], F32)]
        psO = ps("psO", [128, 512], F32)
        psT, hT, identb = T["psT"], T["hT"], T["identb"]
        par = [0, 0]

        def proj(g, pp, pk):
            for c in range(8):
                kb.op("pe", lambda e: e.matmul(pp[:], lhsT=hT[:, c, :], rhs=wAb[:, c, 512 * g:512 * g + 512], start=(c == 0), stop=(c == 7)),
                      r=["hT", f"wA{c}"], w=[pk], inc=(c == 7))

        units = []
        uctr = [0]

        def sb_block(i, j, g, S_fill, pv, diag_mask, first, last, has_acc):
            units.append(dict(g=g, S_fill=S_fill, pv=pv, diag_mask=diag_mask, first=first, last=last, has_acc=has_acc, idx=uctr[0]))
            uctr[0] += 1

        def emit_stage(U, st):
            g, q, s2 = U["g"], U["idx"] % NBUF, U["idx"] % 2
            e_, sp_, eR_, wT_ = eb[q], spb[q], eRb[q % 3], wTb[q % 3]
            ek, spk, eRk, wTk = f"e{q}", f"sp{q}", f"eR{q % 3}", f"wT{q % 3}"
            S, R, Sk, Rk = psS[s2], psR[s2], f"psS{s2}", f"psR{s2}"
            if st == 0:
                U["S_fill"](S, Sk)
            elif st == 1:
                kb.op("act", lambda e: e.activation(out=e_[:], in_=S[:], func=AF.Exp), r=[Sk], w=[ek])
                if U["diag_mask"] is not None:
                    mk, mkk = U["diag_mask"]
                    kb.op("dve", lambda e: e.tensor_tensor(out=e_[:].rearrange("p (a b) -> p a b", b=128),
                                                            in0=e_[:].rearrange("p (a b) -> p a b", b=128),
                                                            in1=mk[:].unsqueeze(1).broadcast_to([128, 4, 128]), op=ALU.mult),
                          r=[ek, mkk], w=[ek])
            elif st == 2:
                kb.op("act", lambda e: e.activation(out=sp_[:], in_=e_[:], func=AF.Ln, bias=1.0), r=[ek], w=[spk])
            elif st == 3:
                has_acc = U["has_acc"]
                kb.op("pe", lambda e: e.matmul(R[:], lhsT=T["triI"][:], rhs=sp_[:], start=True, stop=(not has_acc)),
                      r=["triI", spk], w=[Rk], inc=(not has_acc))
                if has_acc:
                    kb.op("pe", lambda e: e.matmul(R[:], lhsT=T["onesb"][:], rhs=Acc[g][:], start=False, stop=True),
                          r=["onesb", f"Acc{g}"], w=[Rk])
                if not U["last"]:
                    if U["first"]:
                        kb.op("dve", lambda e: e.tensor_copy(out=Acc[g][:], in_=sp_[:]), r=[spk], w=[f"Acc{g}"])
                    else:
                        kb.op("dve", lambda e: e.tensor_tensor(out=Acc[g][:], in0=Acc[g][:], in1=sp_[:], op=ALU.add), r=[spk, f"Acc{g}"], w=[f"Acc{g}"])
            elif st == 4:
                kb.op("act", lambda e: e.activation(out=eR_[:], in_=R[:], func=AF.Exp, scale=-1.0), r=[Rk], w=[eRk])
            elif st == 5:
                U["pv"](e_, ek, eR_, eRk, wT_, wTk, 0)
            elif st == 6:
                U["pv"](e_, ek, eR_, eRk, wT_, wTk, 1)

        def run_units(chunks=(), hooks=None):
            n = len(units)
            nsteps = n + 6
            chunks = list(chunks)
            at = dict(hooks or {})
            for m in range(len(chunks)):
                at.setdefault(min(nsteps - 1, ((m + 1) * nsteps) // (len(chunks) + 1)), []).append(chunks[m])
            for step in range(nsteps):
                for st in range(6, -1, -1):
                    u = step - st
                    if 0 <= u < n:
                        emit_stage(units[u], st)
                for c in at.get(step, ()):
                    c()
            del units[:]

        if NTP > 0:
            es2 = ExitStack()
            kT = es2.enter_context(nc.sbuf_tensor(f"a{hh}_kT", [128, 4, 128 * NTP], BF16))
            vsb = es2.enter_context(nc.sbuf_tensor(f"a{hh}_vsb", [128, NTP, 512], BF16))
            qT2 = es2.enter_context(nc.sbuf_tensor(f"a{hh}_qT2", [128, 2, 4, 128], BF16))
            kb.op("dve", lambda e: e.memset(qT2[:], 0.0), w=["qT2"])
            zs2 = es2.enter_context(nc.sbuf_tensor(f"a{hh}_zs2", [128, 512], F32))
            zs3 = es2.enter_context(nc.sbuf_tensor(f"a{hh}_zs3", [128, 512], F32))
            kb.dma("sp", xt[0][:], D["xc"][0:128, :], w=["xt0"])
            qTs, zss = [qT, qT2], [zs, zs2, zs3]

            def front_chunks(i):
                x, xk = xt[i % 2], f"xt{i % 2}"
                qTi, qTk, zsi, zsk = qTs[i % 2], ("qT", "qT2")[i % 2], zss[i % 3], ("zs", "zs2", "zs3")[i % 3]

                def c0():
                    if i + 1 < NTP:
                        kb.dma("sp", xt[(i + 1) % 2][:], D["xc"][128 * (i + 1):128 * (i + 2), :], w=[f"xt{(i + 1) % 2}"])
                    _norm_hT(nc, kb, T, x, xk)

                def c1():
                    proj(0, psP[0], "psP0")
                    kb.op("dve", lambda e: e.tensor_copy(out=qb[:], in_=psP[0][:]), r=["psP0"], w=["qb"])

                def c2():
                    proj(1, psP[1], "psP1")
                    kb.op("act", lambda e: e.activation(out=kf[:], in_=psP[1][:], func=AF.Identity), r=["psP1"], w=["kf", "psP1"])
                    kb.op("dve", lambda e: e.tensor_scalar_mul(out=ksb[:], in0=psP[1][:], scalar1=0.125), r=["psP1"], w=["ksb"])
                    kb.dma("sp", D[f"ko{hh}"][:, 128 * i:128 * i + 128, :].rearrange("h t d -> t h d"),
                           kf[:].rearrange("t (h d) -> t h d", d=64), r=["kf"])

                def c3():
                    proj(2, psP[0], "psP0")
                    kb.op("act", lambda e: e.activation(out=vf[:], in_=psP[0][:], func=AF.Identity), r=["psP0"], w=["vf", "psP0"])
                    kb.op("dve", lambda e: e.tensor_copy(out=vsb[:, i, :], in_=psP[0][:]), r=["psP0"], w=[f"vsb{i}"])
                    kb.dma("sp", D[f"vo{hh}"][:, 128 * i:128 * i + 128, :].rearrange("h t d -> t h d"),
                           vf[:].rearrange("t (h d) -> t h d", d=64), r=["vf"])

                def c4():
                    proj(3, psP[1], "psP1")
                    _silu_from_psum(kb, psP[1], "psP1", et, zsi, zsk)

                def c5():
                    for p in range(4):
                        kb.op("pe", lambda e: e.transpose(out=psT[:, p, :], in_=qb[:, 128 * p:128 * p + 128], identity=identb[:]),
                              r=["qb", "identb"], w=["psT"], inc=False)
                    for p in range(4):
                        kb.op("pe", lambda e: e.transpose(out=psT[:, 4 + p, :], in_=ksb[:, 128 * p:128 * p + 128], identity=identb[:]),
                              r=["ksb", "identb"], w=["psT"], inc=(p == 3))
                    kb.op("dve", lambda e: e.tensor_copy(out=qTi[0:64, 0, :, :], in_=psT[0:64, 0:4, :]), r=["psT"], w=[qTk])
                    kb.op("dve", lambda e: e.tensor_copy(out=qTi[64:128, 1, :, :], in_=psT[64:128, 0:4, :]), r=["psT"], w=[qTk])
                    kb.op("dve", lambda e: e.tensor_copy(out=kT[:, :, 128 * i:128 * i + 128], in_=psT[:, 4:8, :]), r=["psT"], w=[f"kT{i}"])

                return [c0, c1, c2, c3, c4, c5]

            for c in front_chunks(0):
                c()
            hooks = {}

            def epilogue(i):
                zsi, zsk = zss[i % 3], ("zs", "zs2", "zs3")[i % 3]
                kb.op("dve", lambda e: e.tensor_tensor(out=ua[:], in0=psO[:], in1=zsi[:], op=ALU.mult), r=["psO", zsk], w=["ua"])
                for p in range(4):
                    kb.op("pe", lambda e: e.transpose(out=psT[:, p, :], in_=ua[:, 128 * p:128 * p + 128], identity=identb[:]),
                          r=["ua", "identb"], w=["psT"], inc=(p == 3))
                kb.op("dve", lambda e: e.tensor_copy(out=uaT[:], in_=psT[:, 0:4, :]), r=["psT"], w=["uaT"])
                kb.dma("sp", D["xab"][i // 6][0:512, :].rearrange("(p c) t -> c p t", c=128)[:, :, 128 * (i % 6):128 * (i % 6) + 128], uaT[:], r=["uaT"], w=[f"xab{i // 6}"])

            for i in range(NTP):
                qTi, qTk = qTs[i % 2], ("qT", "qT2")[i % 2]
                u0 = len(units)
                for j in range(i, -1, -1):
                    for g in range(2):
                        def S_fill(S, Sk, j=j, g=g, qTi=qTi, qTk=qTk):
                            for hh in range(4):
                                head = 4 * g + hh
                                p, h2 = head // 2, head % 2
                                kb.op("pe", lambda e: e.matmul(S[:, 128 * hh:128 * hh + 128], lhsT=kT[:, p, 128 * j:128 * j + 128],
                                                               rhs=qTi[:, h2, p, :], start=True, stop=True),
                                      r=[f"kT{j}", qTk], w=[Sk], inc=(hh == 3))

                        def pv(e_, ek, eR_, eRk, wT_, wTk, part, j=j, g=g, i=i):
                            if part == 0:
                                kb.op("dve", lambda e: e.tensor_tensor(out=wT_[:], in0=e_[:], in1=eR_[:], op=ALU.mult), r=[ek, eRk], w=[wTk])
                                return
                            for hh in range(4):
                                head = 4 * g + hh
                                kb.op("pe", lambda e: e.matmul(psO[:, 64 * head:64 * head + 64], lhsT=wT_[:, 128 * hh:128 * hh + 128],
                                                               rhs=vsb[:, j, 64 * head:64 * head + 64], start=(j == i and head == 0),
                                                               stop=(j == 0 and head == 7), skip_group_check=True),
                                      r=[wTk, f"vsb{j}"], w=["psO"], inc=(hh == 3))

                        sb_block(i, j, g, S_fill, pv, (T["mskS"], "mskS") if j == i else None,
                                 first=(j == i), last=(j == 0), has_acc=(j < i))
                u1 = len(units)
                if i + 1 < NTP:
                    ch = front_chunks(i + 1)
                    for m, c in enumerate(ch):
                        hooks.setdefault(u0 + ((m + 1) * (u1 - u0)) // (len(ch) + 1), []).append(c)
                hooks.setdefault(u1 - 1 + 6, []).append(lambda i=i: epilogue(i))
            run_units(hooks=hooks)
            kb.op("dve", lambda e: e.memset(kT[:, 0, 0:1], 0.0), w=[f"kT{j}" for j in range(NTP)])
            kb.op("dve", lambda e: e.memset(vsb[:, 0, 0:1], 0.0), w=[f"vsb{j}" for j in range(NTP)])
            es2.close()
        _drain(kb)
        if cfg.get("sample", True):
            es3 = ExitStack()

            def sb3(name, shape, dt):
                return es3.enter_context(nc.sbuf_tensor(f"a{hh}s_" + name, shape, dt))

            kTn = sb3("kTn", [128, 4, 128], BF16)
            vbn = sb3("vbn", [128, 512], BF16)
            kpT = sb3("kpT", [128, 2, 8, 1024], BF16)
            vp = sb3("vp", [128, 8, 8, 256], BF16)
            kstb = [sb3(f"kst{q}", [128, 8, 256], F32) for q in range(2)]
            vstb = [sb3("vst0", [128, 8, 256], F32)] * 2
            wTd = [sb3(f"wTd{q}", [128, 4, 1152], BF16) for q in range(2)]
            identf = sb3("identf", [128, 128], F32)
            kb.dma("sp", identf[:], D["identf"], w=["identf"])
            for q in range(2):
                kb.op("dve", lambda e: e.memset(wTd[q][:], 0.0), w=[f"wTd{q}"])
            x, xk = xt[0], "xt0"
            kb.dma("sp", x[:], D["xc"][4096:4224, :], w=[xk])
            _norm_hT(nc, kb, T, x, xk)
            proj(0, psP[0], "psP0")
            kb.op("dve", lambda e: e.tensor_copy(out=qb[:], in_=psP[0][:]), r=["psP0"], w=["qb"])
            proj(1, psP[1], "psP1")
            kb.op("act", lambda e: e.activation(out=kf[:], in_=psP[1][:], func=AF.Identity), r=["psP1"], w=["kf", "psP1"])
            kb.op("dve", lambda e: e.tensor_scalar_mul(out=ksb[:], in0=psP[1][:], scalar1=0.125), r=["psP1"], w=["ksb"])
            for b in range(8):
                kb.dma("sp", D[f"kso{hh}"][b].rearrange("h t d -> t h d"), kf[16 * b:16 * b + 16, :].rearrange("t (h d) -> t h d", d=64), r=["kf"])
            proj(2, psP[0], "psP0")
            kb.op("act", lambda e: e.activation(out=vf[:], in_=psP[0][:], func=AF.Identity), r=["psP0"], w=["vf", "psP0"])
            kb.op("dve", lambda e: e.tensor_copy(out=vbn[:], in_=psP[0][:]), r=["psP0"], w=["vbn"])
            for b in range(8):
                kb.dma("sp", D[f"vso{hh}"][b].rearrange("h t d -> t h d"), vf[16 * b:16 * b + 16, :].rearrange("t (h d) -> t h d", d=64), r=["vf"])
            proj(3, psP[1], "psP1")
            _silu_from_psum(kb, psP[1], "psP1", et, zs, "zs")
            for p in range(4):
                kb.op("pe", lambda e: e.transpose(out=psT[:, p, :], in_=qb[:, 128 * p:128 * p + 128], identity=identb[:]),
                      r=["qb", "identb"], w=["psT"], inc=False)
            for p in range(4):
                kb.op("pe", lambda e: e.transpose(out=psT[:, 4 + p, :], in_=ksb[:, 128 * p:128 * p + 128], identity=identb[:]),
                      r=["ksb", "identb"], w=["psT"], inc=(p == 3))
            kb.op("dve", lambda e: e.tensor_copy(out=qT[0:64, 0, :, :], in_=psT[0:64, 0:4, :]), r=["psT"], w=["qT"])
            kb.op("dve", lambda e: e.tensor_copy(out=qT[64:128, 1, :, :], in_=psT[64:128, 0:4, :]), r=["psT"], w=["qT"])
            kb.op("dve", lambda e: e.tensor_copy(out=kTn[:], in_=psT[:, 4:8, :]), r=["psT"], w=["kTn"])
            for g in range(2):
                def load_cache(b, g=g):
                    kst, vst, q = kstb[b % 2], vstb[b % 2], b % 2
                    for j in range(8):
                        kb.dma("sp", kst[:, j, :].rearrange("s (h d) -> s h d", d=64),
                               D[f"kc{hh}"][b, 4 * g:4 * g + 4, 128 * j:128 * j + 128, :].rearrange("h s d -> s h d"), w=[f"kst{q}_{j}"])
                        kb.dma("sp", vst[:, j, :].rearrange("s (h d) -> s h d", d=64),
                               D[f"vc{hh}"][b, 4 * g:4 * g + 4, 128 * j:128 * j + 128, :].rearrange("h s d -> s h d"), w=[f"vst_{j}"])

                load_cache(0)
                for b in range(8):
                    kst, vst, q = kstb[b % 2], vstb[b % 2], b % 2
                    kb.op("dve", lambda e: e.tensor_copy(out=vp[:, :, b, :], in_=vst[:]), r=[f"vst_{j}" for j in range(8)], w=["vp"])
                    if b + 1 < 8:
                        load_cache(b + 1)
                    for pl in range(2):
                        for half in range(2):
                            Pq, Pqk = psP[half], f"psP{half}"
                            for jj in range(4):
                                j = 4 * half + jj
                                kb.op("pe", lambda e: e.matmul(Pq[:, 128 * jj:128 * jj + 128], lhsT=kst[:, j, 128 * pl:128 * pl + 128], rhs=identf[:], start=True, stop=True),
                                      r=[f"kst{q}_{j}", "identf"], w=[Pqk], inc=(jj == 3))
                            kb.op("dve", lambda e: e.tensor_scalar_mul(out=kpT[:, pl, b, 512 * half:512 * half + 512], in0=Pq[:], scalar1=0.125), r=[Pqk], w=["kpT"])
                def S_new(S, Sk, g=g):
                    for h4 in range(4):
                        head = 4 * g + h4
                        p, h2 = head // 2, head % 2
                        kb.op("pe", lambda e: e.matmul(S[:, 128 * h4:128 * h4 + 128], lhsT=kTn[:, p, :], rhs=qT[:, h2, p, :], start=True, stop=True),
                              r=["kTn", "qT"], w=[Sk], inc=(h4 == 3))

                def pv_new(e_, ek, eR_, eRk, wT_, wTk, part, g=g):
                    if part == 0:
                        kb.op("dve", lambda e: e.tensor_tensor(out=wT_[:], in0=e_[:], in1=eR_[:], op=ALU.mult), r=[ek, eRk], w=[wTk])
                        return
                    for h4 in range(4):
                        head = 4 * g + h4
                        kb.op("pe", lambda e: e.matmul(psO[:, 64 * head:64 * head + 64], lhsT=wT_[:, 128 * h4:128 * h4 + 128], rhs=vbn[:, 64 * head:64 * head + 64],
                                                       start=(head == 0), stop=False, skip_group_check=True),
                              r=[wTk, "vbn"], w=["psO"], inc=(h4 == 3))

                sb_block(32, 8, g, S_new, pv_new, (T["mskSblk"], "mskSblk"), first=True, last=False, has_acc=False)
                for j in range(7, -1, -1):
                    def S_past(S, Sk, g=g, j=j):
                        for h4 in range(4):
                            head = 4 * g + h4
                            p, h2, pl = head // 2, head % 2, h4 // 2
                            for b in range(8):
                                kb.op("pe", lambda e: e.matmul(S[:, 128 * h4 + 16 * b:128 * h4 + 16 * b + 16], lhsT=kpT[:, pl, b, 128 * j:128 * j + 128],
                                                               rhs=qT[:, h2, p, 16 * b:16 * b + 16], start=True, stop=True),
                                      r=["kpT", "qT"], w=[Sk], inc=(h4 == 3 and b == 7))

                    def pv_past(e_, ek, eR_, eRk, wT_, wTk, part, g=g, j=j):
                        q = j % 2
                        Wd = wTd[q]
                        dview = Wd[:].rearrange("p h (b x) -> p h b x", x=144)[:, :, :, 0:16]
                        if part == 0:
                            kb.op("dve", lambda e: e.tensor_tensor(out=dview, in0=e_[:].rearrange("p (h b i) -> p h b i", h=4, b=8),
                                                                    in1=eR_[:].rearrange("p (h b i) -> p h b i", h=4, b=8), op=ALU.mult), r=[ek, eRk], w=[f"wTd{q}"])
                            return
                        for h4 in range(4):
                            head = 4 * g + h4
                            for b in range(8):
                                kb.op("pe", lambda e: e.matmul(psO[:, 64 * head:64 * head + 64], lhsT=Wd[:, h4, 128 * b:128 * b + 128], rhs=vp[:, j, b, 64 * h4:64 * h4 + 64],
                                                               start=False, stop=(j == 0 and head == 7 and b == 7), skip_group_check=True),
                                      r=[f"wTd{q}", "vp"], w=["psO"], inc=(h4 == 3 and b == 7))

                    sb_block(32, j, g, S_past, pv_past, None, first=False, last=(j == 0), has_acc=True)
                run_units()
            kb.op("dve", lambda e: e.tensor_tensor(out=ua[:], in0=psO[:], in1=zs[:], op=ALU.mult), r=["psO", "zs"], w=["ua"])
            for p in range(4):
                kb.op("pe", lambda e: e.transpose(out=psT[:, p, :], in_=ua[:, 128 * p:128 * p + 128], identity=identb[:]),
                      r=["ua", "identb"], w=["psT"], inc=(p == 3))
            kb.op("dve", lambda e: e.tensor_copy(out=uaT[:], in_=psT[:, 0:4, :]), r=["psT"], w=["uaT"])
            kb.dma("sp", D["xab"][5][0:512, :].rearrange("(p c) t -> c p t", c=128)[:, :, 256:384], uaT[:], r=["uaT"], w=["xab5"])
            _drain(kb)
            es3.close()
        _drain(kb)


def pass_b(nc, kb, D, cfg, hh):
    NTP = cfg["NTP"]
    with ExitStack() as es:
        def sb(name, shape, dt):
            return es.enter_context(nc.sbuf_tensor(f"b{hh}_" + name, shape, dt))

        def ps(name, shape, dt):
            return es.enter_context(nc.psum_tensor(f"b{hh}_" + name, shape, dt))

        T = {}
        for nm, dt, shp in (("identb", BF16, [128, 128]), ("identf", F32, [128, 128]), ("triF", F32, [128, 128]),
                            ("mS2", F32, [128, 256]), ("mN", F32, [128, 128]), ("onesF", F32, [128, 8])):
            T[nm] = sb("c_" + nm, shp, dt)
            kb.dma("sp", T[nm][:], D[nm], w=[nm])
        wBb = sb("wBb", [128, 8, 2176], BF16)
        _load_w_bf16(nc, kb, D[f"wB{hh}"], wBb, 2176, "wB", f"b{hh}")
        T["gnb"] = sb("gnb", [128, 1024], F32)
        kb.dma("sp", T["gnb"][:], D["norm_g"].partition_broadcast(128), w=["gnb"])
        mub = sb("mub", [128, 2176], F32)
        kb.dma("sp", mub[:], D[f"muB{hh}"].partition_broadcast(128), w=["mub"])
        PB = {}
        for nm in ("w0", "a0", "k_k", "k_a", "r_k", "lnx_g", "lnx_b"):
            PB[nm] = sb("p_" + nm, [128, 512], F32)
            kb.dma("sp", PB[nm][:], D[f"{nm}{hh}"].partition_broadcast(128), w=["p_" + nm])
        w2z = sb("w2z", [128, 2, 512], BF16)
        with nc.sbuf_tensor(f"b{hh}_w2st", [128, 2, 512], F32) as w2st:
            kb.dma("sp", w2st[:], D[f"w2a2z{hh}"], w=["w2st"])
            kb.op("dve", lambda e: e.tensor_copy(out=w2z[:], in_=w2st[:]), r=["w2st"], w=["w2z"])
            _drain(kb)
        xt = [sb("xt0", [128, 1024], F32), sb("xt1", [128, 1024], F32)]
        T["ss"] = sb("ss", [128, 2], F32)
        T["junk"] = sb("junk", [128, 1024], BF16)
        T["hb"] = sb("hb", [128, 1024], BF16)
        T["hT"] = sb("hT", [128, 8, 128], BF16)
        P = [sb("P0", [128, 2176], F32), sb("P1", [128, 2176], F32)]
        Pprev = sb("Pprev", [128, 2176], F32)
        pm = sb("pm_", [128, 2176], F32)
        F = {nm: sb(nm, [128, 512], F32) for nm in ("ew", "Wt", "Winv", "Wprev", "a", "kk", "kmod", "t1", "t2", "zsb", "ob", "cen")}
        Bt = {nm: sb(nm, [128, 512], BF16) for nm in ("Ab", "Rb", "Bb", "Kb", "Vb", "ub")}
        ltb = sb("ltb", [128, 128], BF16)
        ltf = sb("ltf", [128, 128], F32)
        ltT = sb("ltT", [128, 128], BF16)
        n8 = sb("n8", [128, 8], F32)
        bn8 = sb("bn8", [128, 8], F32)
        m8 = sb("m8", [128, 8], F32)
        Wend = sb("Wend", [128, 4], F32)
        ARz = sb("ARz", [128, 2, 4, 2, 128], BF16)
        BKT = sb("BKT", [128, 4, 2, 128], BF16)
        ST = sb("ST", [128, 4, 64], F32)
        STb = sb("STb", [128, 4, 64], BF16)
        sc = sb("sc", [128, 2, 512], BF16)
        Pn = [sb("Pn0", [128, 2, 2, 128], BF16), sb("Pn1", [128, 2, 2, 128], BF16)]
        YT = sb("YT", [128, 2, 128], BF16)
        scp = [sb(f"scp{p}", [128, 2, 512], BF16) for p in range(4)]
        Pnp = [[sb(f"Pnp{p}_{q}", [128, 2, 2, 128], BF16) for q in range(3)] for p in range(4)]
        YTp = [sb(f"YTp{p}", [128, 2, 128], BF16) for p in range(4)]
        rhs0a = sb("rhs0a", [128, 4, 2, 64], BF16)
        Uba = sb("Uba", [128, 4, 2, 64], BF16)
        rhs0 = sb("rhs0", [128, 2, 64], BF16)
        Ub = sb("Ub", [128, 2, 64], BF16)
        ubT = sb("ubT", [128, 4, 128], BF16)
        psT = ps("psT", [128, 8, 128], BF16)
        T["psT"] = psT
        psP = [ps("psP0", [128, 512], F32), ps("psP1", [128, 512], F32)]
        psW = [ps("psW0", [128, 512], F32), ps("psW1", [128, 512], F32)]
        psC = [ps("psC0", [128, 512], F32), ps("psC1", [128, 512], F32)]
        psO = ps("psO", [128, 512], F32)
        psX = psP[0]
        hT, identb = T["hT"], T["identb"]
        kb.op("dve", lambda e: e.memset(ARz[:], 0.0), w=["ARz"])
        kb.op("dve", lambda e: e.memset(ST[:], 0.0), w=["ST"])
        kb.op("dve", lambda e: e.memset(STb[:], 0.0), w=["STb"])

        def v3(ap):
            return ap.rearrange("p (a b) -> p a b", b=64)

        def bc8(ap8):
            return ap8.unsqueeze(2).broadcast_to([128, 8, 64])

        def sigmoid_inplace(buf, key, src, srck, scale=-1.0):
            kb.op("act", lambda e: e.activation(out=buf, in_=src, func=AF.Exp, scale=scale), r=[srck], w=[key])
            kb.op("act", lambda e: e.activation(out=buf, in_=buf, func=AF.Ln, bias=1.0), r=[key], w=[key])
            kb.op("act", lambda e: e.activation(out=buf, in_=buf, func=AF.Exp, scale=-1.0), r=[key], w=[key])

        def tile_b(ti, x, xk, Pc, Pk, sample):
            _norm_hT(nc, kb, T, x, xk)
            for g in range(5):
                ncol = 512 if g < 4 else 128
                pp, pk = psP[g % 2], f"psP{g % 2}"
                for c in range(8):
                    kb.op("pe", lambda e: e.matmul(pp[:, 0:ncol], lhsT=hT[:, c, :], rhs=wBb[:, c, 512 * g:512 * g + ncol], start=(c == 0), stop=(c == 7)),
                          r=["hT", f"wB{c}"], w=[pk], inc=(c == 7))
                kb.op("act", lambda e: e.activation(out=Pc[:, 512 * g:512 * g + ncol], in_=pp[:, 0:ncol], func=AF.Identity), r=[pk], w=[Pk, pk])
            return


        def lockstep_pairs(mS2, mS2k, mN, mNk):
            identf = T["identf"]
            for p in range(4):
                for h2 in range(2):
                    W = psW[h2]
                    kb.op("pe", lambda e: e.matmul(W[:, 0:256], lhsT=BKT[:, p, 0, :], rhs=ARz[:, h2, p].rearrange("p a c -> p (a c)"), start=True, stop=True),
                          r=["BKT", "ARz"], w=[f"psW{h2}"], inc=False)
                    kb.op("pe", lambda e: e.matmul(W[:, 256:512], lhsT=BKT[:, p, 1, :], rhs=ARz[:, h2, p].rearrange("p a c -> p (a c)"), start=True, stop=True),
                          r=["BKT", "ARz"], w=[f"psW{h2}"])
                    kb.op("pe", lambda e: e.matmul(psC[1][:, 256 + 128 * h2:256 + 128 * h2 + 128], lhsT=ARz[:, h2, p, 0, :], rhs=BKT[:, p, 0, :], start=True, stop=True),
                          r=["BKT", "ARz"], w=["psC1"])
                    kb.op("dve", lambda e: e.tensor_tensor(out=scp[p][:, h2, :].rearrange("p (a b) -> p a b", b=256), in0=W[:].rearrange("p (a b) -> p a b", b=256),
                                                            in1=mS2[:].unsqueeze(1).broadcast_to([128, 2, 256]), op=ALU.mult),
                          r=[f"psW{h2}", mS2k], w=[f"scp{p}"])
                cur = Pnp[p][0]
                kb.op("dve", lambda e: e.tensor_tensor(out=cur[:, :, 0, :], in0=psC[1][:, 256:512].rearrange("p (a b) -> p a b", b=128),
                                                        in1=mN[:].unsqueeze(1).broadcast_to([128, 2, 128]), op=ALU.mult), r=["psC1", mNk], w=[f"Pnp{p}_0"])
                kb.op("dve", lambda e: e.tensor_copy(out=cur[:, :, 1, :], in_=scp[p][:, :, 0:128]), r=[f"scp{p}"], w=[f"Pnp{p}_0"])
                kb.op("dve", lambda e: e.tensor_tensor(out=YTp[p][:], in0=scp[p][:, :, 0:128], in1=identf[:].unsqueeze(1).broadcast_to([128, 2, 128]), op=ALU.add),
                      r=[f"scp{p}", "identf"], w=[f"YTp{p}"])
            sqb = [(psW[0], "psW0"), (psW[1], "psW1"), (psC[0], "psC0"), (psP[1], "psP1")]
            ytb = [(psC[1], 0, "psC1"), (psC[1], 256, "psC1"), (psP[0], 0, "psP0"), (psP[0], 256, "psP0")]
            def yt_update(lv):
                for p in range(4):
                    cur, curk = Pnp[p][lv % 3], f"Pnp{p}_{lv % 3}"
                    YB, y0, YBk = ytb[p]
                    for h2 in range(2):
                        kb.op("pe", lambda e: e.matmul(YB[:, y0 + 128 * h2:y0 + 128 * h2 + 128], lhsT=cur[:, h2, 0, :], rhs=YTp[p][:, h2, :], start=True, stop=True),
                              r=[curk, f"YTp{p}"], w=[YBk], inc=(h2 == 1))
                for p in range(4):
                    YB, y0, YBk = ytb[p]
                    kb.op("dve", lambda e: e.tensor_tensor(out=YTp[p][:], in0=YTp[p][:], in1=YB[:, y0:y0 + 256].rearrange("p (a b) -> p a b", b=128), op=ALU.add),
                          r=[f"YTp{p}", YBk], w=[f"YTp{p}", YBk])

            for lv in range(1, 7):
                for p in range(4):
                    prv, prvk = Pnp[p][(lv - 1) % 3], f"Pnp{p}_{(lv - 1) % 3}"
                    SQ, SQk = sqb[p]
                    for h2 in range(2):
                        kb.op("pe", lambda e: e.matmul(SQ[:, 256 * h2:256 * h2 + 128], lhsT=prv[:, h2, 1, :], rhs=prv[:, h2, 0, :], start=True, stop=True),
                              r=[prvk], w=[SQk], inc=False)
                        kb.op("pe", lambda e: e.matmul(SQ[:, 256 * h2 + 128:256 * h2 + 256], lhsT=prv[:, h2, 0, :], rhs=prv[:, h2, 1, :], start=True, stop=True),
                              r=[prvk], w=[SQk], inc=(h2 == 1))
                for p in range(4):
                    cur, curk = Pnp[p][lv % 3], f"Pnp{p}_{lv % 3}"
                    SQ, SQk = sqb[p]
                    if p < 1:
                        kb.op("dve", lambda e: e.tensor_copy(out=cur[:].rearrange("p a b c -> p (a b c)"), in_=SQ[:]), r=[SQk], w=[curk])
                    else:
                        kb.op("act", lambda e: e.activation(out=cur[:].rearrange("p a b c -> p (a b c)"), in_=SQ[:], func=AF.Identity), r=[SQk], w=[curk])
                if lv >= 2:
                    yt_update(lv - 1)
            yt_update(6)
            for p in range(4):
                for h2 in range(2):
                    hd = 2 * p + h2
                    oc = psP[0][:, 128 * p + 64 * h2:128 * p + 64 * h2 + 64]
                    kb.op("pe", lambda e: e.matmul(oc, lhsT=ARz[:, h2, p, 0, :], rhs=STb[:, p, :], start=(hd == 0), stop=False, skip_group_check=True),
                          r=["ARz", "STb"], w=["psP0"], inc=False)
                    kb.op("pe", lambda e: e.matmul(oc, lhsT=scp[p][:, h2, 256:384], rhs=Bt["Vb"][:, 64 * hd:64 * hd + 64], start=False, stop=(hd == 7), skip_group_check=True),
                          r=[f"scp{p}", "Vb"], w=["psP0"], inc=(hd == 7))
            kb.op("dve", lambda e: e.tensor_copy(out=rhs0a[:].rearrange("p a b c -> p (a b c)"), in_=psP[0][:]), r=["psP0"], w=["rhs0a"])
            for p in range(4):
                for h2 in range(2):
                    hd = 2 * p + h2
                    kb.op("pe", lambda e: e.matmul(psP[1][:, 64 * hd:64 * hd + 64], lhsT=YTp[p][:, h2, :], rhs=rhs0a[:, p, h2, :], start=True, stop=True),
                          r=[f"YTp{p}", "rhs0a"], w=["psP1"], inc=(hd == 7))
            kb.op("dve", lambda e: e.tensor_copy(out=Uba[:].rearrange("p a b c -> p (a b c)"), in_=psP[1][:]), r=["psP1"], w=["Uba"])
            for p in range(4):
                for h2 in range(2):
                    hd = 2 * p + h2
                    oc = psO[:, 64 * hd:64 * hd + 64]
                    kb.op("pe", lambda e: e.matmul(oc, lhsT=ARz[:, h2, p, 1, :], rhs=STb[:, p, :], start=(hd == 0), stop=False, skip_group_check=True),
                          r=["ARz", "STb"], w=["psO"], inc=False)
                    kb.op("pe", lambda e: e.matmul(oc, lhsT=scp[p][:, h2, 128:256], rhs=Uba[:, p, h2, :], start=False, stop=False, skip_group_check=True),
                          r=[f"scp{p}", "Uba"], w=["psO"], inc=False)
                    kb.op("pe", lambda e: e.matmul(oc, lhsT=scp[p][:, h2, 384:512], rhs=Bt["Vb"][:, 64 * hd:64 * hd + 64], start=False, stop=(hd == 7), skip_group_check=True),
                          r=[f"scp{p}", "Vb"], w=["psO"], inc=(hd == 7))
            for p in range(4):
                sc_ = psC[0][:, 128 * p:128 * p + 128]
                kb.op("pe", lambda e: e.matmul(sc_, lhsT=Bt["Bb"][:, 128 * p:128 * p + 128], rhs=Uba[:, p].rearrange("p a b -> p (a b)"), start=(p == 0), stop=False, skip_group_check=True),
                      r=["Bb", "Uba"], w=["psC0"], inc=False)
                kb.op("pe", lambda e: e.matmul(sc_, lhsT=Bt["Kb"][:, 128 * p:128 * p + 128], rhs=Bt["Vb"][:, 128 * p:128 * p + 128], start=False, stop=(p == 3), skip_group_check=True),
                      r=["Kb", "Vb"], w=["psC0"], inc=(p == 3))
            for h2 in range(2):
                rows = slice(64 * h2, 64 * h2 + 64)
                kb.op("dve", lambda e: e.tensor_tensor(out=ST[rows, :, :], in0=ST[rows, :, :],
                                                        in1=psC[0][rows, :].rearrange("p (a b) -> p a b", b=128)[:, :, 64 * h2:64 * h2 + 64], op=ALU.add),
                      r=["ST", "psC0"], w=["ST", "psC0"])
                kb.op("dve", lambda e: e.tensor_tensor(out=ST[rows, :, :], in0=ST[rows, :, :], in1=Wend[rows, 0:4].unsqueeze(2).broadcast_to([64, 4, 64]), op=ALU.mult),
                      r=["ST", "Wend"], w=["ST"])
            kb.op("dve", lambda e: e.tensor_copy(out=STb[:], in_=ST[:]), r=["ST"], w=["STb"])

        def mix_and_scan(ti, Pc, Pk, sample, mid_hook=None):
            if cfg.get("bcut", 99) < 1:
                return
            for pme, c0, c1, pk_ in (("pool", 1280, 2176, "pmB"), ("dve", 0, 1280, "pmA")):
                kb.op(pme, lambda e: e.tensor_tensor(out=pm[:, c0:c1], in0=Pprev[:, c0:c1], in1=Pc[:, c0:c1], op=ALU.subtract), r=["Pprev", Pk], w=[pk_])
                kb.op(pme, lambda e: e.tensor_tensor(out=pm[:, c0:c1], in0=pm[:, c0:c1], in1=mub[:, c0:c1], op=ALU.mult), r=[pk_, "mub"], w=[pk_])
                kb.op(pme, lambda e: e.tensor_tensor(out=pm[:, c0:c1], in0=pm[:, c0:c1], in1=Pc[:, c0:c1], op=ALU.add), r=[pk_, Pk], w=[pk_])
            r_, k_, v_, z_ = pm[:, 0:512], pm[:, 512:1024], pm[:, 1024:1536], pm[:, 1536:2048]
            kb.op("act", lambda e: e.activation(out=ltf[:, 0:64], in_=pm[:, 2048:2112], func=AF.Exp, scale=-2.0), r=["pmA", "pmB"], w=["ltf"])
            kb.op("dve", lambda e: e.tensor_scalar_add(out=ltf[:, 0:64], in0=ltf[:, 0:64], scalar1=1.0), r=["ltf"], w=["ltf"])
            kb.op("dve", lambda e: e.reciprocal(out=ltf[:, 0:64], in_=ltf[:, 0:64]), r=["ltf"], w=["ltf"])
            kb.op("dve", lambda e: e.tensor_scalar(out=ltb[:, 0:64], in0=ltf[:, 0:64], scalar1=2.0, scalar2=-1.0, op0=ALU.mult, op1=ALU.add), r=["ltf"], w=["ltb"])
            kb.op("dve", lambda e: e.tensor_copy(out=ltb[:, 64:128], in_=pm[:, 2112:2176]), r=["pmA", "pmB"], w=["ltb"])
            kb.op("pe", lambda e: e.transpose(out=psT[:, 0, :], in_=ltb[:], identity=identb[:]), r=["ltb", "identb"], w=["psT"])
            kb.op("dve", lambda e: e.tensor_copy(out=ltT[:], in_=psT[:, 0, :]), r=["psT"], w=["ltT"])
            kb.op("pe", lambda e: e.matmul(psP[0][:], lhsT=ltT[:], rhs=w2z[:, 0, :], start=True, stop=True), r=["ltT", "w2z"], w=["psP0"])
            kb.op("pe", lambda e: e.matmul(psP[1][:], lhsT=ltT[:], rhs=w2z[:, 1, :], start=True, stop=True), r=["ltT", "w2z"], w=["psP1"])
            if cfg.get("bcut", 99) < 2:
                return
            ew, Wt, Winv, Wprev, a_, kk, kmod, t1, t2, zsb, ob, cen = (F[n][:] for n in ("ew", "Wt", "Winv", "Wprev", "a", "kk", "kmod", "t1", "t2", "zsb", "ob", "cen"))
            kb.op("dve", lambda e: e.tensor_tensor(out=t1, in0=psP[0][:], in1=PB["w0"][:], op=ALU.add), r=["psP0", "p_w0"], w=["t1"])
            kb.op("act", lambda e: e.activation(out=t1, in_=t1, func=AF.Exp, scale=-1.0), r=["t1"], w=["t1"])
            kb.op("act", lambda e: e.activation(out=t1, in_=t1, func=AF.Ln, bias=1.0), r=["t1"], w=["t1"])
            kb.op("act", lambda e: e.activation(out=ew, in_=t1, func=AF.Exp, scale=-1.0, bias=-0.5), r=["t1"], w=["ew"])
            kb.op("dve", lambda e: e.tensor_tensor(out=t2, in0=psP[1][:], in1=PB["a0"][:], op=ALU.add), r=["psP1", "p_a0"], w=["t2"])
            sigmoid_inplace(a_, "a", t2, "t2")
            if cfg.get("bcut", 99) < 3:
                return
            tri = T["triFblk"] if sample else T["triF"]
            trik = "triFblk" if sample else "triF"
            kb.op("pe", lambda e: e.matmul(psP[0][:], lhsT=tri[:], rhs=ew, start=True, stop=True), r=[trik, "ew"], w=["psP0"])
            kb.op("act", lambda e: e.activation(out=Wt, in_=psP[0][:], func=AF.Exp, scale=-1.0), r=["psP0"], w=["Wt", "psP0"])
            kb.op("act", lambda e: e.activation(out=Winv, in_=psP[0][:], func=AF.Exp), r=["psP0"], w=["Winv", "psP0"])
            kb.op("dve", lambda e: e.tensor_tensor(out=t1, in0=psP[0][:], in1=ew, op=ALU.subtract), r=["psP0", "ew"], w=["t1", "psP0"])
            kb.op("act", lambda e: e.activation(out=Wprev, in_=t1, func=AF.Exp, scale=-1.0), r=["t1"], w=["Wprev"])
            nb = 8 if sample else 1
            for p in range(4):
                kb.op("pe", lambda e: e.matmul(psP[1][:, nb * p:nb * p + nb], lhsT=F["ew"][:, 128 * p:128 * p + 128],
                                               rhs=(T["rowmF"][:, 0:8] if sample else T["onesF"][:, 0:1]), start=True, stop=True),
                      r=["ew", "onesF", "rowmF"], w=["psP1"], inc=(p == 3))
            kb.op("act", lambda e: e.activation(out=(WendS[:].rearrange("p a b -> p (a b)") if sample else Wend[:, 0:4]),
                                                in_=psP[1][:, 0:4 * nb], func=AF.Exp, scale=-1.0), r=["psP1"], w=["Wend", "psP1"])
            if cfg.get("bcut", 99) < 4:
                return
            kb.op("dve", lambda e: e.tensor_tensor(out=kk, in0=k_, in1=PB["k_k"][:], op=ALU.mult), r=["pmA", "pmB", "p_k_k"], w=["kk"])
            kb.op("dve", lambda e: e.tensor_tensor(out=t1, in0=kk, in1=kk, op=ALU.mult), r=["kk"], w=["t1"])
            kb.op("dve", lambda e: e.tensor_reduce(out=n8[:], in_=v3(t1), axis=AX.X, op=ALU.add), r=["t1"], w=["n8"])
            kb.op("dve", lambda e: e.tensor_scalar_max(out=n8[:], in0=n8[:], scalar1=1e-24), r=["n8"], w=["n8"])
            kb.op("act", lambda e: e.activation(out=n8[:], in_=n8[:], func=AF.Ln), r=["n8"], w=["n8"])
            kb.op("act", lambda e: e.activation(out=n8[:], in_=n8[:], func=AF.Exp, scale=-0.5), r=["n8"], w=["n8"])
            if mid_hook is not None:
                mid_hook()
            kb.op("dve", lambda e: e.tensor_tensor(out=v3(kk), in0=v3(kk), in1=bc8(n8[:]), op=ALU.mult), r=["kk", "n8"], w=["kk"])
            kb.op("dve", lambda e: e.scalar_tensor_tensor(out=kmod, in0=a_, scalar=-1.0, in1=PB["k_a"][:], op0=ALU.add, op1=ALU.mult), r=["a", "p_k_a"], w=["kmod"])
            kb.op("dve", lambda e: e.scalar_tensor_tensor(out=kmod, in0=kmod, scalar=1.0, in1=k_, op0=ALU.add, op1=ALU.mult), r=["kmod", "pmA", "pmB"], w=["kmod"])
            kb.op("dve", lambda e: e.scalar_tensor_tensor(out=Bt["Ab"][:], in0=kk, scalar=-1.0, in1=Wprev, op0=ALU.mult, op1=ALU.mult), r=["kk", "Wprev"], w=["Ab"])
            kb.op("dve", lambda e: e.tensor_tensor(out=Bt["Rb"][:], in0=r_, in1=Wt, op=ALU.mult), r=["pmA", "pmB", "Wt"], w=["Rb"])
            kb.op("dve", lambda e: e.tensor_tensor(out=t1, in0=kk, in1=a_, op=ALU.mult), r=["kk", "a"], w=["t1"])
            kb.op("dve", lambda e: e.tensor_tensor(out=Bt["Bb"][:], in0=t1, in1=Winv, op=ALU.mult), r=["t1", "Winv"], w=["Bb"])
            kb.op("dve", lambda e: e.tensor_tensor(out=Bt["Kb"][:], in0=kmod, in1=Winv, op=ALU.mult), r=["kmod", "Winv"], w=["Kb"])
            kb.op("dve", lambda e: e.tensor_copy(out=Bt["Vb"][:], in_=v_), r=["pmA", "pmB"], w=["Vb"])
            kb.op("pool", lambda e: e.tensor_tensor(out=t2, in0=r_, in1=kmod, op=ALU.mult), r=["pmA", "pmB", "kmod"], w=["t2"])
            kb.op("pool", lambda e: e.tensor_tensor(out=t2, in0=t2, in1=PB["r_k"][:], op=ALU.mult), r=["t2", "p_r_k"], w=["t2"])
            kb.op("dve", lambda e: e.tensor_reduce(out=bn8[:], in_=v3(t2), axis=AX.X, op=ALU.add), r=["t2"], w=["bn8"])
            sigmoid_inplace(zsb, "zsb", z_, "pmB")
            kb.op("pool", lambda e: e.tensor_tensor(out=zsb, in0=zsb, in1=z_, op=ALU.mult), r=["zsb", "pmA", "pmB"], w=["zsb"])
            if cfg.get("bcut", 99) < 5:
                return
            for p in range(4):
                for a, nm in enumerate(("Ab", "Rb")):
                    kb.op("pe", lambda e: e.transpose(out=psT[:, 2 * p + a, :], in_=Bt[nm][:, 128 * p:128 * p + 128], identity=identb[:]),
                          r=[nm, "identb"], w=["psT"], inc=(p == 3 and a == 1))
            kb.op("dve", lambda e: e.tensor_copy(out=ARz[0:64, 0].rearrange("p a b c -> p (a b) c"), in_=psT[0:64, :, :]), r=["psT"], w=["ARz"])
            kb.op("dve", lambda e: e.tensor_copy(out=ARz[64:128, 1].rearrange("p a b c -> p (a b) c"), in_=psT[64:128, :, :]), r=["psT"], w=["ARz"])
            for p in range(4):
                for a, nm in enumerate(("Bb", "Kb")):
                    kb.op("pe", lambda e: e.transpose(out=psT[:, 2 * p + a, :], in_=Bt[nm][:, 128 * p:128 * p + 128], identity=identb[:]),
                          r=[nm, "identb"], w=["psT"], inc=(p == 3 and a == 1))
            kb.op("dve", lambda e: e.tensor_copy(out=BKT[:].rearrange("p a b c -> p (a b) c"), in_=psT[:]), r=["psT"], w=["BKT"])
            if cfg.get("bcut", 99) < 6:
                return
            mS2 = T["mS2blk"] if sample else T["mS2"]
            mN = T["mNblk"] if sample else T["mN"]
            mS2k, mNk = ("mS2blk", "mNblk") if sample else ("mS2", "mN")
            first_o = [True]
            if not sample:
                lockstep_pairs(mS2, mS2k, mN, mNk)
            for p in (range(4) if sample else ()):
                for h2 in range(2):
                    W = psW[h2]
                    kb.op("pe", lambda e: e.matmul(W[:, 0:256], lhsT=BKT[:, p, 0, :], rhs=ARz[:, h2, p].rearrange("p a c -> p (a c)"), start=True, stop=True),
                          r=["BKT", "ARz"], w=[f"psW{h2}"], inc=False)
                    kb.op("pe", lambda e: e.matmul(W[:, 256:512], lhsT=BKT[:, p, 1, :], rhs=ARz[:, h2, p].rearrange("p a c -> p (a c)"), start=True, stop=True),
                          r=["BKT", "ARz"], w=[f"psW{h2}"])
                    kb.op("pe", lambda e: e.matmul(psC[1][:, 256 + 128 * h2:256 + 128 * h2 + 128], lhsT=ARz[:, h2, p, 0, :], rhs=BKT[:, p, 0, :], start=True, stop=True),
                          r=["BKT", "ARz"], w=["psC1"])
                    kb.op("dve", lambda e: e.tensor_tensor(out=sc[:, h2, :].rearrange("p (a b) -> p a b", b=256), in0=W[:].rearrange("p (a b) -> p a b", b=256),
                                                            in1=mS2[:].unsqueeze(1).broadcast_to([128, 2, 256]), op=ALU.mult),
                          r=[f"psW{h2}", mS2k], w=[f"sc{h2}"])
                cur = Pn[0]
                kb.op("dve", lambda e: e.tensor_tensor(out=cur[:, :, 0, :], in0=psC[1][:, 256:512].rearrange("p (a b) -> p a b", b=128),
                                                        in1=mN[:].unsqueeze(1).broadcast_to([128, 2, 128]), op=ALU.mult), r=["psC1", mNk], w=["Pn0"])
                kb.op("dve", lambda e: e.tensor_copy(out=cur[:, :, 1, :], in_=sc[:, :, 0:128]), r=["sc0", "sc1"], w=["Pn0"])
                kb.op("dve", lambda e: e.tensor_tensor(out=YT[:], in0=sc[:, :, 0:128], in1=T["identf"][:].unsqueeze(1).broadcast_to([128, 2, 128]), op=ALU.add),
                      r=["sc0", "sc1", "identf"], w=["YT"])
                nlev = 4 if sample else 7
                for lv in range(1, nlev):
                    prv, prvk = Pn[(lv - 1) % 2], f"Pn{(lv - 1) % 2}"
                    cur, curk = Pn[lv % 2], f"Pn{lv % 2}"
                    for h2 in range(2):
                        kb.op("pe", lambda e: e.matmul(psC[0][:, 256 * h2:256 * h2 + 128], lhsT=prv[:, h2, 1, :], rhs=prv[:, h2, 0, :], start=True, stop=True),
                              r=[prvk], w=["psC0"], inc=False)
                        kb.op("pe", lambda e: e.matmul(psC[0][:, 256 * h2 + 128:256 * h2 + 256], lhsT=prv[:, h2, 0, :], rhs=prv[:, h2, 1, :], start=True, stop=True),
                              r=[prvk], w=["psC0"], inc=(h2 == 1))
                    kb.op("dve", lambda e: e.tensor_copy(out=cur[:].rearrange("p a b c -> p (a b c)"), in_=psC[0][:]), r=["psC0"], w=[curk])
                    for h2 in range(2):
                        kb.op("pe", lambda e: e.matmul(psC[1][:, 128 * h2:128 * h2 + 128], lhsT=cur[:, h2, 0, :], rhs=YT[:, h2, :], start=True, stop=True),
                              r=[curk, "YT"], w=["psC1"], inc=(h2 == 1))
                    kb.op("dve", lambda e: e.tensor_tensor(out=YT[:], in0=YT[:], in1=psC[1][:, 0:256].rearrange("p (a b) -> p a b", b=128), op=ALU.add),
                          r=["YT", "psC1"], w=["YT", "psC1"])
                if sample:
                    for h2 in range(2):
                        for a in range(2):
                            kb.op("pe", lambda e: e.matmul(psW[a][:], lhsT=ARz[:, h2, p, a, :], rhs=S0b[:, p].rearrange("p b v -> p (b v)"), start=True, stop=True),
                                  r=["ARz", "S0b"], w=[f"psW{a}"])
                            kb.op("dve", lambda e: e.tensor_tensor(out=tmpZ[:], in0=psW[a][:].rearrange("p (b v) -> p b v", v=64),
                                                                    in1=T["rowmF"][:].unsqueeze(2).broadcast_to([128, 8, 64]), op=ALU.mult),
                                  r=[f"psW{a}", "rowmF"], w=["tmpZ", f"psW{a}"])
                            kb.op("dve", lambda e: e.tensor_reduce(out=s0t[a][:, h2, :], in_=tmpZ[:].rearrange("p b v -> p v b"), axis=AX.X, op=ALU.add),
                                  r=["tmpZ"], w=[f"s0t{a}"])
                    for h2 in range(2):
                        hd = 2 * p + h2
                        kb.op("pe", lambda e: e.matmul(psX[:, 64 * h2:64 * h2 + 64], lhsT=sc[:, h2, 256:384], rhs=Bt["Vb"][:, 64 * hd:64 * hd + 64], start=(h2 == 0), stop=(h2 == 1), skip_group_check=True),
                              r=[f"sc{h2}", "Vb"], w=["psP0"], inc=(h2 == 1))
                    kb.op("dve", lambda e: e.tensor_tensor(out=rhs0[:].rearrange("p a b -> p (a b)"), in0=psX[:, 0:128], in1=s0t[0][:].rearrange("p a b -> p (a b)"), op=ALU.add),
                          r=["psP0", "s0t0"], w=["rhs0", "psP0"])
                for h2 in range(2):
                    hd = 2 * p + h2
                    if not sample:
                        kb.op("pe", lambda e: e.matmul(psX[:, 64 * h2:64 * h2 + 64], lhsT=ARz[:, h2, p, 0, :], rhs=STb[:, p, :], start=(h2 == 0), stop=False, skip_group_check=True),
                              r=["ARz", "STb"], w=["psP0"], inc=False)
                        kb.op("pe", lambda e: e.matmul(psX[:, 64 * h2:64 * h2 + 64], lhsT=sc[:, h2, 256:384], rhs=Bt["Vb"][:, 64 * hd:64 * hd + 64], start=False, stop=(h2 == 1), skip_group_check=True),
                              r=[f"sc{h2}", "Vb"], w=["psP0"], inc=(h2 == 1))
                if not sample:
                    kb.op("dve", lambda e: e.tensor_copy(out=rhs0[:].rearrange("p a b -> p (a b)"), in_=psX[:, 0:128]), r=["psP0"], w=["rhs0"])
                for h2 in range(2):
                    kb.op("pe", lambda e: e.matmul(psX[:, 128 + 64 * h2:128 + 64 * h2 + 64], lhsT=YT[:, h2, :], rhs=rhs0[:, h2, :], start=True, stop=True),
                          r=["YT", "rhs0"], w=["psP0"], inc=(h2 == 1))
                kb.op("dve", lambda e: e.tensor_copy(out=Ub[:].rearrange("p a b -> p (a b)"), in_=psX[:, 128:256]), r=["psP0"], w=["Ub"])
                for h2 in range(2):
                    hd = 2 * p + h2
                    oc = psO[:, 64 * hd:64 * hd + 64]
                    if not sample:
                        kb.op("pe", lambda e: e.matmul(oc, lhsT=ARz[:, h2, p, 1, :], rhs=STb[:, p, :], start=first_o[0], stop=False, skip_group_check=True),
                              r=["ARz", "STb"], w=["psO"], inc=False)
                        first_o[0] = False
                    kb.op("pe", lambda e: e.matmul(oc, lhsT=sc[:, h2, 128:256], rhs=Ub[:, h2, :], start=first_o[0], stop=False, skip_group_check=True),
                          r=[f"sc{h2}", "Ub"], w=["psO"], inc=False)
                    first_o[0] = False
                    kb.op("pe", lambda e: e.matmul(oc, lhsT=sc[:, h2, 384:512], rhs=Bt["Vb"][:, 64 * hd:64 * hd + 64], start=False, stop=(hd == 7), skip_group_check=True),
                          r=[f"sc{h2}", "Vb"], w=["psO"], inc=(h2 == 1))
                if sample:
                    kb.op("dve", lambda e: e.tensor_copy(out=s0R[:, 2 * p:2 * p + 2, :], in_=s0t[1][:]), r=["s0t1"], w=["s0R"])
                    for h2 in range(2):
                        hd = 2 * p + h2
                        rows = slice(64 * h2, 64 * h2 + 64)
                        kb.op("dve", lambda e: e.tensor_tensor(out=Uexp[:], in0=Ub[:, h2, :].unsqueeze(1).broadcast_to([128, 8, 64]),
                                                                in1=T["rowmF"][:].unsqueeze(2).broadcast_to([128, 8, 64]), op=ALU.mult), r=["Ub", "rowmF"], w=["Uexp"])
                        kb.op("dve", lambda e: e.tensor_tensor(out=Vexp[:], in0=Bt["Vb"][:, 64 * hd:64 * hd + 64].unsqueeze(1).broadcast_to([128, 8, 64]),
                                                                in1=T["rowmF"][:].unsqueeze(2).broadcast_to([128, 8, 64]), op=ALU.mult), r=["Vb", "rowmF"], w=["Vexp"])
                        kb.op("pe", lambda e: e.matmul(psW[h2][:], lhsT=Bt["Bb"][:, 128 * p:128 * p + 128], rhs=Uexp[:].rearrange("p b v -> p (b v)"), start=True, stop=False),
                              r=["Bb", "Uexp"], w=[f"psW{h2}"], inc=False)
                        kb.op("pe", lambda e: e.matmul(psW[h2][:], lhsT=Bt["Kb"][:, 128 * p:128 * p + 128], rhs=Vexp[:].rearrange("p b v -> p (b v)"), start=False, stop=True),
                              r=["Kb", "Vexp"], w=[f"psW{h2}"])
                        kb.op("dve", lambda e: e.tensor_tensor(out=S0f[rows, p], in0=S0f[rows, p], in1=psW[h2][rows, :].rearrange("p (b v) -> p b v", v=64), op=ALU.add),
                              r=["S0f", f"psW{h2}"], w=["S0f", f"psW{h2}"])
                        kb.op("dve", lambda e: e.tensor_tensor(out=S0f[rows, p], in0=S0f[rows, p], in1=WendS[rows, p, :].unsqueeze(2).broadcast_to([64, 8, 64]), op=ALU.mult),
                              r=["S0f", "Wend"], w=["S0f"])
                    continue
                kb.op("pe", lambda e: e.matmul(psX[:, 256:384], lhsT=Bt["Bb"][:, 128 * p:128 * p + 128], rhs=Ub[:].rearrange("p a b -> p (a b)"), start=True, stop=False, skip_group_check=True),
                      r=["Bb", "Ub"], w=["psP0"], inc=False)
                kb.op("pe", lambda e: e.matmul(psX[:, 256:384], lhsT=Bt["Kb"][:, 128 * p:128 * p + 128], rhs=Bt["Vb"][:, 128 * p:128 * p + 128], start=False, stop=True, skip_group_check=True),
                      r=["Kb", "Vb"], w=["psP0"])
                for h2 in range(2):
                    rows = slice(64 * h2, 64 * h2 + 64)
                    kb.op("dve", lambda e: e.tensor_tensor(out=ST[rows, p, :], in0=ST[rows, p, :], in1=psX[rows, 256 + 64 * h2:256 + 64 * h2 + 64], op=ALU.add),
                          r=["ST", "psP0"], w=["ST", "psP0"])
                    kb.op("dve", lambda e: e.tensor_scalar_mul(out=ST[rows, p, :], in0=ST[rows, p, :], scalar1=Wend[rows, p:p + 1]),
                          r=["ST", "Wend"], w=["ST"])
                kb.op("dve", lambda e: e.tensor_copy(out=STb[:, p, :], in_=ST[:, p, :]), r=["ST"], w=["STb"])
            if cfg.get("bcut", 99) < 8:
                return
            if sample:
                kb.op("dve", lambda e: e.tensor_tensor(out=ob, in0=psO[:], in1=s0R[:].rearrange("p a b -> p (a b)"), op=ALU.add), r=["psO", "s0R"], w=["ob"])
            else:
                kb.op("dve", lambda e: e.tensor_copy(out=ob, in_=psO[:]), r=["psO"], w=["ob"])
            kb.op("dve", lambda e: e.tensor_reduce(out=m8[:], in_=v3(ob), axis=AX.X, op=ALU.add), r=["ob"], w=["m8"])
            kb.op("dve", lambda e: e.tensor_scalar_mul(out=m8[:], in0=m8[:], scalar1=1.0 / 64), r=["m8"], w=["m8"])
            kb.op("dve", lambda e: e.tensor_tensor(out=v3(cen), in0=v3(ob), in1=bc8(m8[:]), op=ALU.subtract), r=["ob", "m8"], w=["cen"])
            kb.op("dve", lambda e: e.tensor_tensor(out=t1, in0=cen, in1=cen, op=ALU.mult), r=["cen"], w=["t1"])
            kb.op("dve", lambda e: e.tensor_reduce(out=m8[:], in_=v3(t1), axis=AX.X, op=ALU.add), r=["t1"], w=["m8"])
            kb.op("dve", lambda e: e.tensor_scalar(out=m8[:], in0=m8[:], scalar1=1.0 / 64, scalar2=LNX_EPS, op0=ALU.mult, op1=ALU.add), r=["m8"], w=["m8"])
            kb.op("act", lambda e: e.activation(out=m8[:], in_=m8[:], func=AF.Ln), r=["m8"], w=["m8"])
            kb.op("act", lambda e: e.activation(out=m8[:], in_=m8[:], func=AF.Exp, scale=-0.5), r=["m8"], w=["m8"])
            kb.op("dve", lambda e: e.tensor_tensor(out=v3(cen), in0=v3(cen), in1=bc8(m8[:]), op=ALU.mult), r=["cen", "m8"], w=["cen"])
            kb.op("dve", lambda e: e.tensor_tensor(out=cen, in0=cen, in1=PB["lnx_g"][:], op=ALU.mult), r=["cen", "p_lnx_g"], w=["cen"])
            kb.op("dve", lambda e: e.tensor_tensor(out=cen, in0=cen, in1=PB["lnx_b"][:], op=ALU.add), r=["cen", "p_lnx_b"], w=["cen"])
            kb.op("pool", lambda e: e.tensor_tensor(out=v3(t2), in0=v3(v_), in1=bc8(bn8[:]), op=ALU.mult), r=["pmA", "pmB", "bn8"], w=["t2"])
            kb.op("dve", lambda e: e.tensor_tensor(out=cen, in0=cen, in1=t2, op=ALU.add), r=["cen", "t2"], w=["cen"])
            kb.op("dve", lambda e: e.tensor_tensor(out=Bt["ub"][:], in0=cen, in1=zsb, op=ALU.mult), r=["cen", "zsb"], w=["ub"])
            for p in range(4):
                kb.op("pe", lambda e: e.transpose(out=psT[:, p, :], in_=Bt["ub"][:, 128 * p:128 * p + 128], identity=identb[:]),
                      r=["ub", "identb"], w=["psT"], inc=(p == 3))
            kb.op("dve", lambda e: e.tensor_copy(out=ubT[:], in_=psT[:, 0:4, :]), r=["psT"], w=["ubT"])
            kb.dma("sp", D["xab"][ti // 6][512:1024, :].rearrange("(p c) t -> c p t", c=128)[:, :, 128 * (ti % 6):128 * (ti % 6) + 128], ubT[:], r=["ubT"], w=[f"xab{ti // 6}"])
            if (ti % 6 == 5 or ti == 32) and "gather" in cfg:
                cfg["gather"](ti // 6)

        if NTP > 0:
            kb.dma("sp", xt[0][:], D["xc"][0:128, :], w=["xt0"])
        def front(j):
            Pc, Pk = P[j % 2], f"P{j % 2}"
            if j + 1 < NTP:
                kb.dma("sp", xt[(j + 1) % 2][:], D["xc"][128 * (j + 1):128 * (j + 2), :], w=[f"xt{(j + 1) % 2}"])
            tile_b(j, xt[j % 2], f"xt{j % 2}", Pc, Pk, False)
            if j == 0:
                kb.op("dve", lambda e: e.memset(Pprev[0:1, :], 0.0), w=["Pprev"])
            else:
                kb.dma("sp", Pprev[0:1, :], P[(j - 1) % 2][127:128, :], r=[f"P{(j - 1) % 2}"], w=["Pprev"])
            kb.dma("sp", Pprev[1:128, :], Pc[0:127, :], r=[Pk], w=["Pprev"])
            if j == NTP - 1:
                kb.dma("sp", D[f"sho{hh}"], Pc[127:128, :], r=[Pk])

        if NTP > 0:
            front(0)
        for i in range(NTP):
            mix_and_scan(i, P[i % 2], f"P{i % 2}", False, mid_hook=((lambda i=i: front(i + 1)) if i + 1 < NTP else None))
        if NTP > 0 and cfg.get("bcut", 99) >= 9:
            psTf = psW[0]
            for p in range(4):
                kb.op("pe", lambda e: e.matmul(psTf[0:64, 128 * p:128 * p + 128], lhsT=ST[:, p, :], rhs=T["identf"][:], start=True, stop=True),
                      r=["ST", "identf"], w=["psW0"], inc=(p == 3))
            stT = F["ob"][0:64, :]
            kb.op("dve", lambda e: e.tensor_copy(out=stT, in_=psTf[0:64, :]), r=["psW0"], w=["ob"])
            kb.dma("sp", D[f"wkvo{hh}"].rearrange("(p h) v k -> v p h k", h=2), stT.rearrange("v (p h k) -> v p h k", p=4, k=64), r=["ob"])
        _drain(kb)
        if cfg.get("sample", True):
            for nm, shp in (("triFblk", [128, 128]), ("mS2blk", [128, 256]), ("mNblk", [128, 128]), ("rowmF", [128, 8])):
                T[nm] = sb("c_" + nm, shp, F32)
                kb.dma("sp", T[nm][:], D[nm], w=[nm])
            WendS = sb("WendS", [128, 4, 8], F32)
            tmpZ = sb("tmpZ", [128, 8, 64], F32)
            s0t = [sb("s0t0", [128, 2, 64], F32), sb("s0t1", [128, 2, 64], F32)]
            s0R = sb("s0R", [128, 8, 64], F32)
            Uexp = sb("Uexp", [128, 8, 64], BF16)
            Vexp = sb("Vexp", [128, 8, 64], BF16)
            S0f = sb("S0f", [128, 4, 8, 64], F32)
            S0b = sb("S0b", [128, 4, 8, 64], BF16)
            with nc.sbuf_tensor(f"b{hh}_S0v", [64, 2, 4, 128], F32) as S0v:
                for bp in range(4):
                    for bb in range(2):
                        kb.dma("sp", S0v[:, bb].rearrange("v p (h k) -> v (p h) k", k=64), D[f"s0{hh}"][2 * bp + bb].rearrange("h v k -> v h k"), w=["S0v"])
                    for p in range(4):
                        W, Wk = psW[p % 2], f"psW{p % 2}"
                        for bb in range(2):
                            kb.op("pe", lambda e: e.matmul(W[:, 64 * bb:64 * bb + 64], lhsT=S0v[:, bb, p, :], rhs=T["identf"][0:64, 0:64], start=True, stop=True),
                                  r=["S0v", "identf"], w=[Wk], inc=(bb == 1))
                        kb.op("dve", lambda e: e.tensor_copy(out=S0f[:, p, 2 * bp:2 * bp + 2, :].rearrange("p b v -> p (b v)"), in_=W[:, 0:128]),
                              r=[Wk], w=["S0f"])
                kb.op("dve", lambda e: e.tensor_copy(out=S0b[:], in_=S0f[:]), r=["S0f"], w=["S0b"])
                _drain(kb)
            x, xk = xt[0], "xt0"
            Pc, Pk = P[0], "P0"
            kb.dma("sp", x[:], D["xc"][4096:4224, :], w=[xk])
            tile_b(32, x, xk, Pc, Pk, True)
            kb.dma("sp", Pprev[1:128, :], Pc[0:127, :], r=[Pk], w=["Pprev"])
            for b in range(8):
                kb.dma("sp", Pprev[16 * b:16 * b + 1, :], D[f"shs{hh}"][b:b + 1, :], w=["Pprev"])
                kb.dma("sp", D[f"shso{hh}"][b:b + 1, :], Pc[16 * b + 15:16 * b + 16, :], r=[Pk])
            mix_and_scan(32, Pc, Pk, True)
            for p in range(4):
                for half in range(2):
                    W = psW[half]
                    for bb in range(4):
                        b = 4 * half + bb
                        kb.op("pe", lambda e: e.matmul(W[0:64, 128 * bb:128 * bb + 128], lhsT=S0f[:, p, b, :], rhs=T["identf"][:], start=True, stop=True),
                              r=["S0f", "identf"], w=[f"psW{half}"], inc=(bb == 3))
                    kb.op("dve", lambda e: e.tensor_copy(out=pm[0:64, 512 * half:512 * half + 512], in_=W[0:64, :]),
                          r=[f"psW{half}"], w=["pmA", "pmB"])
                for b in range(8):
                    kb.dma("sp", D[f"wkvso{hh}"][b, 2 * p:2 * p + 2].rearrange("h v k -> v h k"), pm[0:64, 128 * b:128 * b + 128].rearrange("v (h k) -> v h k", k=64), r=["pmA", "pmB"])
            _drain(kb)


def pass_c(nc, kb, D, cfg):
    tiles = cfg["ctiles"]
    with ExitStack() as es:
        def sb(name, shape, dt):
            return es.enter_context(nc.sbuf_tensor("c_" + name, shape, dt))

        def ps(name, shape, dt):
            return es.enter_context(nc.psum_tensor("c_" + name, shape, dt))

        T = {}
        T["identb"] = sb("identb", [128, 128], BF16)
        kb.dma("sp", T["identb"][:], D["identb"], w=["identb"])
        wGb = sb("wGb", [128, 8, 2048], BF16)
        _load_w_bf16(nc, kb, D["wG"], wGb, 2048, "wG", "c")
        WO = {}
        for nm in ("woa", "wob", "wout"):
            WO[nm] = sb(nm, [128, 8, 1024], BF16)
            _load_w_bf16(nc, kb, D[nm], WO[nm], 1024, nm, "c")
        T["gnb"] = sb("gnb", [128, 1024], F32)
        kb.dma("sp", T["gnb"][:], D["norm_g"].partition_broadcast(128), w=["gnb"])
        fnb = sb("fnb", [128, 1024], F32)
        kb.dma("sp", fnb[:], D["fnorm_g"].partition_broadcast(128), w=["fnb"])
        GS = 4
        xt = [sb(f"xt{q}", [128, 1024], F32) for q in range(GS)]
        T["ss"] = sb("ss", [128, 2], F32)
        T["junk"] = sb("junk", [128, 1024], BF16)
        T["hb"] = sb("hb", [128, 1024], BF16)
        T["hT"] = sb("hT", [128, 8, 128], BF16)
        T["psT"] = ps("psT", [128, 8, 128], BF16)
        hT4 = sb("hT4", [128, 8, GS * 128], BF16)
        gT = sb("gT", [128, 16, GS * 128], F32)
        uaT = sb("uaT", [128, 8, GS * 128], BF16)
        ubT = sb("ubT", [128, 8, GS * 128], BF16)
        tmp = sb("tmp", [128, GS * 128], F32)
        mT = sb("mT", [128, 8, GS * 128], BF16)
        xo = sb("xo", [128, 1024], F32)
        yo = sb("yo", [128, 1024], F32)
        psG = [ps("psG0", [128, 512], F32), ps("psG1", [128, 512], F32)]
        psA = ps("psA", [128, 512], F32)
        psB = ps("psB", [128, 512], F32)
        psY = [ps("psY0", [128, 512], F32), ps("psY1", [128, 512], F32)]
        groups = [tiles[a:a + GS] for a in range(0, len(tiles), GS)]
        for gi, grp in enumerate(groups):
            n = len(grp)
            W = 128 * n
            for q, (r0, o0) in enumerate(grp):
                kb.dma("sp", xt[q][:], D["xc"][r0:r0 + 128, :], w=[f"xt{q}"])
                tix = r0 // 128
                ck, c0 = tix // 6, 128 * (tix % 6)
                for rk in range(2):
                    kb.dma("sp", uaT[:, 4 * rk:4 * rk + 4, 128 * q:128 * q + 128],
                           D["gath"][ck][1024 * rk:1024 * rk + 512, :].rearrange("(k c) t -> c k t", c=128)[:, :, c0:c0 + 128], r=[f"gath{ck}"], w=["uaT"])
                    kb.dma("sp", ubT[:, 4 * rk:4 * rk + 4, 128 * q:128 * q + 128],
                           D["gath"][ck][1024 * rk + 512:1024 * rk + 1024, :].rearrange("(k c) t -> c k t", c=128)[:, :, c0:c0 + 128], r=[f"gath{ck}"], w=["ubT"])
            for q in range(n):
                _norm_hT(nc, kb, T, xt[q], f"xt{q}", dst=hT4[:, :, 128 * q:128 * q + 128], dstk="hT4")
            for dc in range(16):
                G, Gk = psG[dc % 2], f"psG{dc % 2}"
                for c in range(8):
                    kb.op("pe", lambda e: e.matmul(G[:, 0:W], lhsT=wGb[:, c, 128 * dc:128 * dc + 128], rhs=hT4[:, c, 0:W], start=(c == 0), stop=(c == 7)),
                          r=["hT4", f"wG{c}"], w=[Gk], inc=(c == 7))
                gq = gT[:, dc, 0:W]
                kb.op("act", lambda e: e.activation(out=gq, in_=G[:, 0:W], func=AF.Exp, scale=-1.0), r=[Gk], w=[f"gT{dc}"])
                kb.op("act", lambda e: e.activation(out=gq, in_=gq, func=AF.Ln, bias=1.0), r=[f"gT{dc}"], w=[f"gT{dc}"])
                kb.op("act", lambda e: e.activation(out=gq, in_=gq, func=AF.Exp, scale=-1.0), r=[f"gT{dc}"], w=[f"gT{dc}"])
            for j in range(8):
                for (PS, PSk, wnm, uT, uk) in ((psA, "psA", "woa", uaT, "uaT"), (psB, "psB", "wob", ubT, "ubT")):
                    for k in range(8):
                        kb.op("pe", lambda e: e.matmul(PS[:, 0:W], lhsT=WO[wnm][:, k, 128 * j:128 * j + 128], rhs=uT[:, k, 0:W], start=(k == 0), stop=(k == 7)),
                              r=[uk, f"{wnm}{k}"], w=[PSk], inc=(k == 7))
                ga, gb = gT[:, j, 0:W], gT[:, 8 + j, 0:W]
                kb.op("dve", lambda e: e.tensor_tensor(out=tmp[:, 0:W], in0=ga, in1=psA[:, 0:W], op=ALU.mult), r=[f"gT{j}", "psA"], w=["tmp"])
                kb.op("dve", lambda e: e.tensor_tensor(out=ga, in0=gb, in1=psB[:, 0:W], op=ALU.mult), r=[f"gT{8 + j}", "psB"], w=[f"gT{j}"])
                kb.op("dve", lambda e: e.tensor_tensor(out=mT[:, j, 0:W], in0=tmp[:, 0:W], in1=ga, op=ALU.add), r=["tmp", f"gT{j}"], w=[f"mT{j}"])
            for q, (r0, o0) in enumerate(grp):
                x, xk = xt[q], f"xt{q}"
                for eg in range(2):
                    Y, Yk = psY[eg], f"psY{eg}"
                    for j in range(8):
                        kb.op("pe", lambda e: e.matmul(Y[:], lhsT=mT[:, j, 128 * q:128 * q + 128], rhs=WO["wout"][:, j, 512 * eg:512 * eg + 512], start=(j == 0), stop=(j == 7)),
                              r=[f"mT{j}", f"wout{j}"], w=[Yk], inc=(j == 7))
                    kb.op("dve", lambda e: e.tensor_tensor(out=xo[:, 512 * eg:512 * eg + 512], in0=x[:, 512 * eg:512 * eg + 512], in1=Y[:], op=ALU.add),
                          r=[xk, Yk], w=["xo"])
                ss = T["ss"]
                kb.op("dve", lambda e: e.memset(ss[:, 1:2], 0.0), w=["ss2"])
                kb.op("act", lambda e: e.activation(out=T["junk"][:], in_=xo[:], func=AF.Square, accum_out=ss[:, 1:2]), r=["xo"], w=["junk", "ss2"])
                kb.op("dve", lambda e: e.tensor_scalar(out=ss[:, 1:2], in0=ss[:, 1:2], scalar1=1.0 / 1024, scalar2=EPS, op0=ALU.mult, op1=ALU.add), r=["ss2"], w=["ss2"])
                kb.op("act", lambda e: e.activation(out=ss[:, 1:2], in_=ss[:, 1:2], func=AF.Ln), r=["ss2"], w=["ss2"])
                kb.op("act", lambda e: e.activation(out=ss[:, 1:2], in_=ss[:, 1:2], func=AF.Exp, scale=-0.5), r=["ss2"], w=["ss2"])
                kb.op("dve", lambda e: e.scalar_tensor_tensor(out=yo[:], in0=xo[:], scalar=ss[:, 1:2], in1=fnb[:], op0=ALU.mult, op1=ALU.mult),
                      r=["xo", "ss2", "fnb"], w=["yo"])
                kb.dma("sp", D["yo"][o0:o0 + 128, :], yo[:], r=["yo"])
        _drain(kb)


def _drain(kb):
    for en in ("pe", "act", "dve"):
        for other in ("pe", "act", "dve"):
            kb._wait(en, kb.E[other]["sem"], kb.E[other]["cnt"])
    for q in kb.Q:
        for s in kb.Q[q]["pool"]:
            for en in ("pe", "act", "dve", "sp", "pool"):
                kb._wait(en, s["sem"], 16 * s["cnt"])
    for en in ("sp", "pool"):
        for other in ("pe", "act", "dve"):
            kb._wait(en, kb.E[other]["sem"], kb.E[other]["cnt"])


def build(cfg):
    nc = bass.Bass("TRN2", target_bir_lowering=False)
    D = {}
    passes = cfg.get("passes", "ABGC")

    def din(name, shape, dt=F32):
        D[name] = nc.dram_tensor(name, shape, dt, kind="ExternalInput").ap()

    def dout(name, shape, dt=F32):
        D[name] = nc.dram_tensor(name, shape, dt, kind="ExternalOutput").ap()

    din("xc", [4224, 1024])
    din("norm_g", [1024])
    din("fnorm_g", [1024])
    for nm, shp, dt in CONST_SPECS:
        din(nm, shp, dt)
    hh = 0
    din(f"wA{hh}", [1024, 2048])
    din(f"wB{hh}", [1024, 2176])
    din(f"muB{hh}", [2176])
    for nm in ("w0", "a0", "k_k", "k_a", "r_k", "lnx_g", "lnx_b"):
        din(f"{nm}{hh}", [512])
    din(f"w2a2z{hh}", [128, 2, 512])
    dout(f"ko{hh}", [8, 4096, 64])
    dout(f"vo{hh}", [8, 4096, 64])
    dout(f"kso{hh}", [8, 8, 16, 64])
    dout(f"vso{hh}", [8, 8, 16, 64])
    dout(f"sho{hh}", [1, 2176])
    dout(f"wkvo{hh}", [8, 64, 64])
    din(f"s0{hh}", [8, 8, 64, 64])
    din(f"kc{hh}", [8, 8, 1024, 64])
    din(f"vc{hh}", [8, 8, 1024, 64])
    din(f"shs{hh}", [8, 2176])
    dout(f"shso{hh}", [8, 2176])
    dout(f"wkvso{hh}", [8, 8, 64, 64])
    din("wG", [1024, 2048])
    for nm in ("woa", "wob", "wout"):
        din(nm, [1024, 1024])
    dout("yo", [128 * len(cfg["ctiles"]), 1024])
    D["xab"] = [nc.dram_tensor(f"xab{k}", [1024, 768], BF16).ap() for k in range(6)]
    D["gath"] = [nc.dram_tensor(f"gath{k}", [2048, 768], BF16).ap() for k in range(6)]
    kb = KB(nc)

    def gather(k):
        kb._deps("pool", [f"xab{k}"], [f"gath{k}"])
        ccs = nc.alloc_semaphore(name=f"ccsem{k}")
        nc.gpsimd.collective_compute("AllGather", op=ALU.bypass, replica_groups=[[0, 1], [2, 3], [4, 5], [6, 7]],
                                     ins=[D["xab"][k].opt()], outs=[D["gath"][k].opt()]).then_inc(ccs)
        kb._record((ccs, 1), [f"xab{k}"], [f"gath{k}"])

    if "G" in passes:
        cfg = dict(cfg, gather=gather)
    if "A" in passes:
        pass_a(nc, kb, D, cfg, hh)
    if "B" in passes:
        pass_b(nc, kb, D, cfg, hh)
    if "C" in passes:
        pass_c(nc, kb, D, cfg)
    kb.finish()
    return nc


CONST_SPECS = [("identb", [128, 128], BF16), ("triI", [128, 128], BF16), ("onesb", [128, 128], BF16), ("mskS", [128, 128], F32),
               ("mskSblk", [128, 128], F32), ("identf", [128, 128], F32), ("triF", [128, 128], F32), ("mS2", [128, 256], F32),
               ("mN", [128, 128], F32), ("onesF", [128, 8], F32), ("triFblk", [128, 128], F32), ("mS2blk", [128, 256], F32),
               ("mNblk", [128, 128], F32), ("rowmF", [128, 8], F32)]


def host_consts():
    s = np.arange(128)[:, None]
    t = np.arange(128)[None, :]
    bf = ml_dtypes.bfloat16
    c = {}
    c["identb"] = np.eye(128).astype(bf)
    c["triI"] = (s >= t).astype(bf)
    c["onesb"] = np.ones((128, 128), bf)
    c["mskS"] = (s < t).astype(np.float32)
    c["mskSblk"] = ((s < t) & (s // 16 == t // 16)).astype(np.float32)
    c["identf"] = np.eye(128, dtype=np.float32)
    c["triF"] = (s <= t).astype(np.float32)
    c["mS2"] = np.concatenate([(s < t), (s <= t)], axis=1).astype(np.float32)
    c["mN"] = (t < s).astype(np.float32)
    c["onesF"] = np.ones((128, 8), np.float32)
    blk = (s // 16 == t // 16)
    c["triFblk"] = ((s <= t) & blk).astype(np.float32)
    c["mS2blk"] = np.concatenate([(s < t) & blk, (s <= t) & blk], axis=1).astype(np.float32)
    c["mNblk"] = ((t < s) & blk).astype(np.float32)
    c["rowmF"] = (np.arange(128)[:, None] // 16 == np.arange(8)[None, :]).astype(np.float32)
    return c


def core_inputs(inputs, c):
    p, hh = c // 2, c % 2
    w_in = inputs["w_in"][0]
    m = {}
    m["xc"] = np.ascontiguousarray(np.concatenate([inputs["x_prompt"][p], inputs["x_sample"][8 * p:8 * p + 8].reshape(128, 1024)], axis=0))
    wb = w_in[:, 6144:]
    own = slice(512 * hh, 512 * hh + 512)
    bsel = lambda a: np.concatenate([a[..., 0:1024][..., own], a[..., 1024:2048][..., own], a[..., 2048:3072][..., own],
                                     a[..., 3200:4224][..., own], a[..., 3072:3136], a[..., 3136:3200]], axis=-1)
    m["wA0"] = np.ascontiguousarray(np.concatenate([w_in[:, 1024 * g:1024 * g + 1024][:, own] for g in range(4)], axis=1))
    m["wB0"] = np.ascontiguousarray(bsel(wb))
    m["muB0"] = np.ascontiguousarray(bsel(inputs["mu_shift"][0]))
    for nm in ("w0", "a0", "k_k", "k_a", "lnx_g", "lnx_b"):
        m[f"{nm}0"] = np.ascontiguousarray(inputs[nm][0][own])
    m["r_k0"] = np.ascontiguousarray(inputs["r_k"][0].reshape(1024)[own])
    z = np.zeros((128, 2, 512), np.float32)
    z[0:64, 0] = inputs["w2"][0][:, own]
    z[64:128, 1] = inputs["a2"][0][:, own]
    m["w2a2z0"] = z
    m["kc0"] = np.ascontiguousarray(inputs["cache_sb_k"][0, 8 * p:8 * p + 8, 8 * hh:8 * hh + 8])
    m["vc0"] = np.ascontiguousarray(inputs["cache_sb_v"][0, 8 * p:8 * p + 8, 8 * hh:8 * hh + 8])
    m["s00"] = np.ascontiguousarray(inputs["state_wkv"][0, 8 * p:8 * p + 8, 8 * hh:8 * hh + 8])
    m["shs0"] = np.ascontiguousarray(bsel(inputs["state_shift"][0, 8 * p:8 * p + 8, 0]))
    m["wG"] = np.ascontiguousarray(w_in[:, 4096:6144])
    m["woa"] = np.ascontiguousarray(inputs["w_o_a"][0])
    m["wob"] = np.ascontiguousarray(inputs["w_o_b"][0])
    m["wout"] = np.ascontiguousarray(inputs["w_out"][0])
    m["norm_g"] = np.ascontiguousarray(inputs["norm_g"][0])
    m["fnorm_g"] = np.ascontiguousarray(inputs["final_norm_g"])
    m.update(host_consts())
    return m


ALL_TILES = [(128 * n, 128 * n) for n in range(33)]

_CACHE = {}


def _unsel(vec, hh, out):
    own = slice(512 * hh, 512 * hh + 512)
    out[..., 0:1024][..., own] = vec[..., 0:512]
    out[..., 1024:2048][..., own] = vec[..., 512:1024]
    out[..., 2048:3072][..., own] = vec[..., 1024:1536]
    out[..., 3200:4224][..., own] = vec[..., 1536:2048]
    out[..., 3072:3136] = vec[..., 2048:2112]
    out[..., 3136:3200] = vec[..., 2112:2176]


def kernel(**inputs):
    inputs = {k: np.asarray(v) for k, v in inputs.items()}
    cfg = dict(NTP=32, ctiles=ALL_TILES)
    if "nc" not in _CACHE:
        _CACHE["nc"] = build(cfg)
    nc = _CACHE["nc"]
    maps = [core_inputs(inputs, c) for c in range(8)]
    res = run_bass_kernel_spmd(nc, maps, core_ids=list(range(8))).results
    f32 = np.float32
    y_prompt = np.zeros((4, 4096, 1024), f32)
    y_sample = np.zeros((32, 16, 1024), f32)
    k_prompt = np.zeros((1, 4, 16, 4096, 64), f32)
    v_prompt = np.zeros((1, 4, 16, 4096, 64), f32)
    k_sample = np.zeros((1, 32, 16, 16, 64), f32)
    v_sample = np.zeros((1, 32, 16, 16, 64), f32)
    shift_prompt = np.zeros((1, 4, 1, 4224), f32)
    wkv_prompt = np.zeros((1, 4, 16, 64, 64), f32)
    shift_sample = np.zeros((1, 32, 1, 4224), f32)
    wkv_sample = np.zeros((1, 32, 16, 64, 64), f32)
    for c in range(8):
        p, hh = c // 2, c % 2
        r = res[c]
        hs = slice(8 * hh, 8 * hh + 8)
        if hh == 0:
            y_prompt[p] = r["yo"][0:4096]
            y_sample[8 * p:8 * p + 8] = r["yo"][4096:4224].reshape(8, 16, 1024)
        k_prompt[0, p, hs] = r["ko0"]
        v_prompt[0, p, hs] = r["vo0"]
        k_sample[0, 8 * p:8 * p + 8, hs] = r["kso0"]
        v_sample[0, 8 * p:8 * p + 8, hs] = r["vso0"]
        _unsel(r["sho0"][0], hh, shift_prompt[0, p, 0])
        wkv_prompt[0, p, hs] = r["wkvo0"]
        _unsel(r["shso0"], hh, shift_sample[0, 8 * p:8 * p + 8, 0])
        wkv_sample[0, 8 * p:8 * p + 8, hs] = r["wkvso0"]
    return (y_prompt, y_sample, k_prompt, v_prompt, shift_prompt, wkv_prompt,
            k_sample, v_sample, shift_sample, wkv_sample)
```
